# Optimizing a Trainium2 kernel written in Bass

```python
import math
import numpy as np
import jax
import jax.numpy as jnp
from jax import lax

D_MODEL = 1024
BATCH = 32
SEQ = 256
DEPTH = 4
DEC_BATCH = 2
DEC_SEQ = 4096
PAST_LEN = 512

GRID_W = 64
DA_HEADS = 4
DA_QK = 64
DA_V = 2 * DA_QK
ML_HEADS = 4
ML_DIM = 128
ML_CHUNK = 64
ML_FORGET_BIAS = 3.0
NA_HEADS = 8
NA_DIM = 64
NA_WIN_ROWS = 8
NA_WIN_COLS = 16
N_BRANCH = 3
D_FF = 4 * D_MODEL
Q_BLOCK = 128
ROPE_BASE = 10000.0
EPS = 1e-6
DA_W = DA_HEADS * DA_V
ML_W = ML_HEADS * ML_DIM
NA_W = NA_HEADS * NA_DIM
SPLIT_SIZES = (
    DA_HEADS * 2 * DA_QK,
    DA_HEADS * 2 * DA_QK,
    DA_W,
    ML_W,
    ML_W,
    ML_W,
    ML_W,
    2 * 2 * ML_HEADS,
    NA_W,
    NA_W,
    NA_W,
    N_BRANCH * D_MODEL,
)
N_PROJ = sum(SPLIT_SIZES)
ML_GATE_OFF = sum(SPLIT_SIZES[:7])

kernel_name = 'hybrid_diffattn_mlstm_natten_prefix_step'


def rmsnorm(x, g):
    xf = x.astype(jnp.float32)
    y = xf * lax.rsqrt(jnp.mean(xf * xf, axis=-1, keepdims=True) + EPS)
    return (y * g.astype(jnp.float32)).astype(x.dtype)


def rope_1d(x, pos):
    d = x.shape[-1]
    freqs = ROPE_BASE ** (-jnp.arange(0, d, 2, dtype=jnp.float32) / d)
    ang = pos[:, None] * freqs[None, :]
    cos = jnp.cos(ang)[None, :, None, :]
    sin = jnp.sin(ang)[None, :, None, :]
    xf = x.astype(jnp.float32)
    x1, x2 = xf[..., : d // 2], xf[..., d // 2:]
    return jnp.concatenate([x1 * cos - x2 * sin, x1 * sin + x2 * cos], axis=-1).astype(x.dtype)


def rope_2d(x, rows, cols):
    half = x.shape[-1] // 2
    return jnp.concatenate([rope_1d(x[..., :half], rows), rope_1d(x[..., half:], cols)], axis=-1)


def over_query_blocks(fn, *qs):
    b, t = qs[0].shape[:2]
    nb = t // Q_BLOCK
    blocks = tuple(jnp.moveaxis(q.reshape((b, nb, Q_BLOCK) + q.shape[2:]), 1, 0) for q in qs)
    out = lax.map(lambda blk: fn(*blk), blocks)
    out = jnp.moveaxis(out, 0, 1)
    return out.reshape((b, t) + out.shape[3:])


def softmax_attention(q, k, v):
    scale = q.shape[-1] ** -0.5

    def block(qb):
        s = jnp.einsum('bqhd,bkhd->bhqk', qb, k).astype(jnp.float32) * scale
        p = jax.nn.softmax(s, axis=-1)
        return jnp.einsum('bhqk,bkhe->bqhe', p.astype(v.dtype), v)

    return over_query_blocks(block, q)


def diff_attention(q1, q2, k1, k2, v, lam):
    scale = q1.shape[-1] ** -0.5

    def block(q1b, q2b):
        p1 = jax.nn.softmax(jnp.einsum('bqhd,bkhd->bhqk', q1b, k1).astype(jnp.float32) * scale, axis=-1)
        p2 = jax.nn.softmax(jnp.einsum('bqhd,bkhd->bhqk', q2b, k2).astype(jnp.float32) * scale, axis=-1)
        p = p1 - lam * p2
        return jnp.einsum('bhqk,bkhe->bqhe', p.astype(v.dtype), v)

    return over_query_blocks(block, q1, q2)


def neighbourhood_attention(q, k, v, k_ctx, v_ctx, rpb):
    b, t, h, d = q.shape
    rows = t // GRID_W
    wr = min(NA_WIN_ROWS, rows)
    scale = d ** -0.5
    qg = q.reshape(b, rows, GRID_W, h, d)
    kg = k.reshape(b, rows, GRID_W, h, d)
    vg = v.reshape(b, rows, GRID_W, h, d)
    r = jnp.arange(rows)
    row_idx = jnp.clip(r - wr // 2, 0, rows - wr)[:, None] + jnp.arange(wr)[None, :]
    k_rows = kg[:, row_idx]
    v_rows = vg[:, row_idx]
    s_loc = jnp.einsum('brchd,brijhd->brhcij', qg, k_rows).astype(jnp.float32) * scale
    c = jnp.arange(GRID_W)
    col_start = jnp.clip(c - NA_WIN_COLS // 2, 0, GRID_W - NA_WIN_COLS)
    col_ok = (c[None, :] >= col_start[:, None]) & (c[None, :] < col_start[:, None] + NA_WIN_COLS)
    dr = row_idx - r[:, None] + (NA_WIN_ROWS - 1)
    dc = jnp.clip(c[None, :] - c[:, None] + (NA_WIN_COLS - 1), 0, 2 * NA_WIN_COLS - 2)
    bias = rpb[:, dr[:, None, :, None], dc[None, :, None, :]]
    bias = jnp.moveaxis(bias, 0, 1).astype(jnp.float32)
    s_loc = jnp.where(col_ok[:, None, :], s_loc + bias, -jnp.inf)
    s_ctx = jnp.einsum('brchd,bphd->brhcp', qg, k_ctx).astype(jnp.float32) * scale
    n_loc = wr * GRID_W
    s = jnp.concatenate([s_loc.reshape(b, rows, h, GRID_W, n_loc), s_ctx], axis=-1)
    p = jax.nn.softmax(s, axis=-1)
    p_loc = p[..., :n_loc].reshape(b, rows, h, GRID_W, wr, GRID_W).astype(v.dtype)
    p_ctx = p[..., n_loc:].astype(v.dtype)
    out = (jnp.einsum('brhcij,brijhd->brchd', p_loc, v_rows)
           + jnp.einsum('brhcp,bphd->brchd', p_ctx, v_ctx))
    return out.reshape(b, t, h, d)


def mlstm_chunked(q, k, v, ig, lf, c0, n0, m0):
    b, t, h, d = q.shape
    f32 = jnp.float32
    L = ML_CHUNK
    nc = t // L
    k = k.astype(f32) * (d ** -0.5)

    def to_chunks(a):
        a = a.astype(f32).reshape((b, nc, L) + a.shape[2:])
        return jnp.moveaxis(jnp.moveaxis(a, 1, 0), 2, 3)

    causal = jnp.tril(jnp.ones((L, L), dtype=bool))

    def step(carry, inp):
        cm, nm, mm = carry
        qc, kc, vc, ic, fc = inp
        bcum = jnp.cumsum(fc, axis=-1)
        logd = bcum[..., :, None] - bcum[..., None, :] + ic[..., None, :]
        logd = jnp.where(causal, logd, -jnp.inf)
        m_t = jnp.maximum(bcum + mm[..., None], jnp.max(logd, axis=-1))
        inter = jnp.exp(bcum + mm[..., None] - m_t)
        sc = jnp.einsum('bhtd,bhsd->bhts', qc, kc) * jnp.exp(logd - m_t[..., None])
        num = inter[..., None] * jnp.einsum('bhtd,bhde->bhte', qc, cm) + jnp.einsum('bhts,bhse->bhte', sc, vc)
        den = inter * jnp.einsum('bhtd,bhd->bht', qc, nm) + jnp.sum(sc, axis=-1)
        hc = num / jnp.maximum(jnp.abs(den), jnp.exp(-m_t))[..., None]
        m_new = m_t[..., -1]
        w = jnp.exp(bcum[..., -1:] - bcum + ic - m_new[..., None])
        decay = jnp.exp(bcum[..., -1] + mm - m_new)
        c_new = decay[..., None, None] * cm + jnp.einsum('bhs,bhsd,bhse->bhde', w, kc, vc)
        n_new = decay[..., None] * nm + jnp.einsum('bhs,bhsd->bhd', w, kc)
        return (c_new, n_new, m_new), hc

    init = (c0.astype(f32), n0.astype(f32), m0.astype(f32))
    final, hs = lax.scan(step, init, tuple(to_chunks(a) for a in (q, k, v, ig, lf)))
    hs = jnp.moveaxis(jnp.moveaxis(hs, 3, 2), 0, 1).reshape(b, t, h, d)
    return hs, final


def mlstm_bidir(q, k, v, gates, c0, n0, m0):
    hs, cs, ns, ms = [], [], [], []
    for dr in range(2):
        ig = gates[:, :, dr, 0]
        lf = jax.nn.log_sigmoid(gates[:, :, dr, 1].astype(jnp.float32))
        seq = (q, k, v, ig, lf)
        if dr == 1:
            seq = tuple(jnp.flip(a, axis=1) for a in seq)
        hd, (cf, nf, mf) = mlstm_chunked(*seq, c0[:, dr], n0[:, dr], m0[:, dr])
        if dr == 1:
            hd = jnp.flip(hd, axis=1)
        hs.append(hd)
        cs.append(cf)
        ns.append(nf)
        ms.append(mf)
    return hs[0] + hs[1], jnp.stack(cs, axis=1), jnp.stack(ns, axis=1), jnp.stack(ms, axis=1)


def layer(x, mod, lp, lam_init, ctx):
    b, t, _ = x.shape
    f32 = jnp.float32
    is_context = ctx is None
    sh1, sc1, g1, sh2, sc2, g2 = jnp.split(mod, 6, axis=-1)
    h = rmsnorm(x, lp['norm1']) * (1 + sc1) + sh1
    proj = h @ lp['w_in'] + lp['b_in']
    (da_q, da_k, da_v, ml_q, ml_k, ml_v, ml_o, ml_g,
     na_q, na_k, na_v, merge) = jnp.split(proj, np.cumsum(SPLIT_SIZES)[:-1].tolist(), axis=-1)

    dq = da_q.reshape(b, t, 2 * DA_HEADS, DA_QK)
    dk = da_k.reshape(b, t, 2 * DA_HEADS, DA_QK)
    dv = da_v.reshape(b, t, DA_HEADS, DA_V)
    if is_context:
        k_all = dk.reshape(b, t, DA_HEADS, 2 * DA_QK)
        v_all = dv
    else:
        pos = jnp.arange(t)
        rows = (pos // GRID_W).astype(f32)
        cols = (pos % GRID_W).astype(f32)
        dq = rope_2d(dq, rows, cols)
        dk = rope_2d(dk, rows, cols)
        k_all = jnp.concatenate([dk.reshape(b, t, DA_HEADS, 2 * DA_QK), ctx[0]], axis=1)
        v_all = jnp.concatenate([dv, ctx[1]], axis=1)
    q4 = dq.reshape(b, t, DA_HEADS, 2, DA_QK)
    k4 = k_all.reshape(b, -1, DA_HEADS, 2, DA_QK)
    lv = lp['da_lam'].astype(f32)
    lam = jnp.exp(jnp.sum(lv[0] * lv[1])) - jnp.exp(jnp.sum(lv[2] * lv[3])) + lam_init
    o_da = diff_attention(q4[..., 0, :], q4[..., 1, :], k4[..., 0, :], k4[..., 1, :], v_all, lam)
    o_da = (rmsnorm(o_da, lp['da_subln']) * (1.0 - lam_init)).reshape(b, t, DA_W)

    mq = ml_q.reshape(b, t, ML_HEADS, ML_DIM)
    mk = ml_k.reshape(b, t, ML_HEADS, ML_DIM)
    mv = ml_v.reshape(b, t, ML_HEADS, ML_DIM)
    gates = ml_g.reshape(b, t, 2, 2, ML_HEADS)
    if is_context:
        c0 = jnp.zeros((b, 2, ML_HEADS, ML_DIM, ML_DIM), f32)
        n0 = jnp.zeros((b, 2, ML_HEADS, ML_DIM), f32)
        m0 = jnp.zeros((b, 2, ML_HEADS), f32)
    else:
        c0, n0, m0 = ctx[4], ctx[5], ctx[6]
    h_ml, c_f, n_f, m_f = mlstm_bidir(mq, mk, mv, gates, c0, n0, m0)
    o_ml = (rmsnorm(h_ml, lp['ml_norm'].reshape(ML_HEADS, ML_DIM)).reshape(b, t, ML_W)
            * jax.nn.sigmoid(ml_o.astype(f32))).astype(x.dtype)

    nq = na_q.reshape(b, t, NA_HEADS, NA_DIM)
    nk = na_k.reshape(b, t, NA_HEADS, NA_DIM)
    nv = na_v.reshape(b, t, NA_HEADS, NA_DIM)
    if is_context:
        o_na = softmax_attention(nq, nk, nv)
    else:
        o_na = neighbourhood_attention(nq, nk, nv, ctx[2], ctx[3], lp['na_rpb'])
    o_na = o_na.reshape(b, t, NA_W)

    gate = jax.nn.sigmoid(merge.astype(f32)).astype(x.dtype).reshape(b, t, N_BRANCH, D_MODEL)
    merged = (gate[:, :, 0] * (o_da @ lp['w_up_da'])
              + gate[:, :, 1] * (o_ml @ lp['w_up_ml'])
              + gate[:, :, 2] * (o_na @ lp['w_up_na']))
    x = x + g1 * (merged @ lp['w_out'])

    h2 = rmsnorm(x, lp['norm2']) * (1 + sc2) + sh2
    u = jnp.square(jax.nn.relu(h2 @ lp['w_ff1'] + lp['b_ff1']))
    x = x + g2 * (u @ lp['w_ff2'] + lp['b_ff2'])
    new_ctx = (k_all, v_all, nk, nv, c_f, n_f, m_f) if is_context else None
    return x, new_ctx


def setup_inputs(seed: int = 0) -> dict:
    key = jax.random.key(seed)
    ks = jax.random.split(key, 40)
    f32 = jnp.float32

    def nrm(k, shape, s):
        return s * jax.random.normal(k, shape, f32)

    f_cols = ML_GATE_OFF + np.array([dr * 2 * ML_HEADS + ML_HEADS + hh for dr in range(2) for hh in range(ML_HEADS)])
    b_in = nrm(ks[15], (DEPTH, N_PROJ), 0.02).at[:, f_cols].add(ML_FORGET_BIAS)
    return {
        'x_prompt': nrm(ks[0], (BATCH, SEQ, D_MODEL), 1.0),
        'x_sample': nrm(ks[1], (DEC_BATCH, DEC_SEQ, D_MODEL), 1.0),
        'cache_da_k': nrm(ks[2], (DEC_BATCH, DEPTH, PAST_LEN, DA_HEADS, 2 * DA_QK), 1.0),
        'cache_da_v': nrm(ks[3], (DEC_BATCH, DEPTH, PAST_LEN, DA_HEADS, DA_V), 1.0),
        'cache_na_k': nrm(ks[4], (DEC_BATCH, DEPTH, PAST_LEN, NA_HEADS, NA_DIM), 1.0),
        'cache_na_v': nrm(ks[5], (DEC_BATCH, DEPTH, PAST_LEN, NA_HEADS, NA_DIM), 1.0),
        'state_ml_C': nrm(ks[6], (DEC_BATCH, DEPTH, 2, ML_HEADS, ML_DIM, ML_DIM), 0.1),
        'state_ml_n': nrm(ks[7], (DEC_BATCH, DEPTH, 2, ML_HEADS, ML_DIM), 0.1),
        'state_ml_m': nrm(ks[8], (DEC_BATCH, DEPTH, 2, ML_HEADS), 1.0),
        'c': nrm(ks[9], (DEC_BATCH, D_MODEL), 1.0),
        'c_ctx': nrm(ks[10], (D_MODEL,), 1.0),
        'w_mod': nrm(ks[11], (DEPTH, D_MODEL, 6 * D_MODEL), D_MODEL ** -0.5),
        'b_mod': nrm(ks[12], (DEPTH, 6 * D_MODEL), 0.02),
        'norm1': 1.0 + nrm(ks[13], (DEPTH, D_MODEL), 0.02),
        'w_in': nrm(ks[14], (DEPTH, D_MODEL, N_PROJ), D_MODEL ** -0.5),
        'b_in': b_in,
        'da_lam': nrm(ks[16], (DEPTH, 4, DA_QK), 0.1),
        'da_subln': 1.0 + nrm(ks[17], (DEPTH, DA_V), 0.02),
        'ml_norm': 1.0 + nrm(ks[18], (DEPTH, ML_W), 0.02),
        'na_rpb': nrm(ks[19], (DEPTH, NA_HEADS, 2 * NA_WIN_ROWS - 1, 2 * NA_WIN_COLS - 1), 0.02),
        'w_up_da': nrm(ks[20], (DEPTH, DA_W, D_MODEL), DA_W ** -0.5),
        'w_up_ml': nrm(ks[21], (DEPTH, ML_W, D_MODEL), ML_W ** -0.5),
        'w_up_na': nrm(ks[22], (DEPTH, NA_W, D_MODEL), NA_W ** -0.5),
        'w_out': nrm(ks[23], (DEPTH, D_MODEL, D_MODEL), D_MODEL ** -0.5),
        'norm2': 1.0 + nrm(ks[24], (DEPTH, D_MODEL), 0.02),
        'w_ff1': nrm(ks[25], (DEPTH, D_MODEL, D_FF), D_MODEL ** -0.5),
        'b_ff1': nrm(ks[26], (DEPTH, D_FF), 0.02),
        'w_ff2': nrm(ks[27], (DEPTH, D_FF, D_MODEL), D_FF ** -0.5),
        'b_ff2': nrm(ks[28], (DEPTH, D_MODEL), 0.02),
        'norm_f': 1.0 + nrm(ks[29], (D_MODEL,), 0.02),
    }


def reference(x_prompt, x_sample, cache_da_k, cache_da_v, cache_na_k, cache_na_v,
              state_ml_C, state_ml_n, state_ml_m, c, c_ctx, w_mod, b_mod, norm1, w_in, b_in,
              da_lam, da_subln, ml_norm, na_rpb, w_up_da, w_up_ml, w_up_na, w_out, norm2,
              w_ff1, b_ff1, w_ff2, b_ff2, norm_f):
    collected = [[] for _ in range(7)]
    xp, xs = x_prompt, x_sample
    for l in range(DEPTH):
        lp = {
            'norm1': norm1[l], 'w_in': w_in[l], 'b_in': b_in[l], 'da_lam': da_lam[l],
            'da_subln': da_subln[l], 'ml_norm': ml_norm[l], 'na_rpb': na_rpb[l],
            'w_up_da': w_up_da[l], 'w_up_ml': w_up_ml[l], 'w_up_na': w_up_na[l],
            'w_out': w_out[l], 'norm2': norm2[l], 'w_ff1': w_ff1[l], 'b_ff1': b_ff1[l],
            'w_ff2': w_ff2[l], 'b_ff2': b_ff2[l],
        }
        lam_init = 0.8 - 0.6 * math.exp(-0.3 * l)
        mod_ctx = (jax.nn.silu(c_ctx) @ w_mod[l] + b_mod[l])[None, None, :]
        mod_lat = (jax.nn.silu(c) @ w_mod[l] + b_mod[l])[:, None, :]
        xp, ctx_l = layer(xp, mod_ctx, lp, lam_init, None)
        cache_l = (cache_da_k[:, l], cache_da_v[:, l], cache_na_k[:, l], cache_na_v[:, l],
                   state_ml_C[:, l], state_ml_n[:, l], state_ml_m[:, l])
        xs, _ = layer(xs, mod_lat, lp, lam_init, cache_l)
        for lst, arr in zip(collected, ctx_l):
            lst.append(arr)
    y_prompt = rmsnorm(xp, norm_f)
    y_sample = rmsnorm(xs, norm_f)
    new_da_k = jnp.stack(collected[0], axis=1)
    new_da_v = jnp.stack(collected[1], axis=1)
    new_na_k = jnp.stack(collected[2], axis=1)
    new_na_v = jnp.stack(collected[3], axis=1)
    new_ml_C = jnp.stack(collected[4], axis=1)
    new_ml_n = jnp.stack(collected[5], axis=1)
    new_ml_m = jnp.stack(collected[6], axis=1)
    return (y_prompt, y_sample, new_da_k, new_da_v, new_na_k, new_na_v, new_ml_C, new_ml_n, new_ml_m)
```

```python
import contextlib
import math
import numpy as np
import concourse.bass as bass
import concourse.mybir as mybir
from concourse.bass_utils import run_bass_kernel_spmd

F32 = mybir.dt.float32
BF16 = mybir.dt.bfloat16
AF = mybir.ActivationFunctionType
ALU = mybir.AluOpType

DEPTH = 4
D = 1024
NPROJ = 8208
EPS = 1e-6
NEG = -30000.0
EPOCH = 20000
N_DMA_SEMS = 24


class Buf:
    __slots__ = ("name", "w", "r")

    def __init__(self, name):
        self.name = name
        self.w = None
        self.r = []


class Sched:
    ENG = ("pe", "dve", "act", "pool", "sp")

    def __init__(self, nc, stack):
        self.nc = nc
        self.stack = stack
        self.eobj = {"pe": nc.tensor, "dve": nc.vector, "act": nc.scalar, "pool": nc.gpsimd, "sp": nc.sync}
        self.sems = []
        self.cur = {}
        for e in self.ENG:
            self.cur[e] = [self._new_sem("s_" + e), 0]
        self.waited = {e: {} for e in self.ENG}
        self.dma_sems = [self._new_sem("d%d" % i) for i in range(N_DMA_SEMS)]
        self.dma_val = [0] * N_DMA_SEMS
        self.dma_rr = 0
        self.n_inst = 0
        self.cc_sem = self._new_sem("cc")
        self.cc_val = 0

    def _new_sem(self, name):
        h = self.stack.enter_context(self.nc.semaphore(name + "_%d" % len(self.sems)))
        self.sems.append(h)
        return len(self.sems) - 1

    def _need(self, eng, ev, out):
        if ev is None:
            return
        si, val, src = ev
        if src == eng and eng == "pe":
            return
        if self.waited[eng].get(si, 0) >= val:
            return
        if out.get(si, 0) < val:
            out[si] = val

    def _emit_waits(self, eng, reads, writes, same_engine_war=False):
        need = {}
        for b in reads:
            self._need(eng, b.w, need)
        for b in writes:
            self._need(eng, b.w, need)
            for ev in b.r:
                if ev[2] == eng and not same_engine_war:
                    continue
                self._need(eng, ev, need)
        for si, val in need.items():
            self.waited[eng][si] = val
            self.eobj[eng].wait_ge(self.sems[si], val)

    def _mark(self, ev, reads, writes):
        for b in writes:
            b.w = ev
            b.r = []
        for b in reads:
            if b.w is not ev:
                b.r.append(ev)
            if len(b.r) > 48:
                last = {}
                for x in b.r:
                    if last.get(x[0], (0, 0, 0))[1] <= x[1]:
                        last[x[0]] = x
                b.r = list(last.values())

    def op(self, eng, fn, reads=(), writes=()):
        reads = [b for b in reads if b is not None]
        writes = [b for b in writes if b is not None]
        self._emit_waits(eng, reads, writes)
        c = self.cur[eng]
        if c[1] >= EPOCH:
            c[0] = self._new_sem("s_" + eng)
            c[1] = 0
        c[1] += 1
        si, val = c[0], c[1]
        fn(self.eobj[eng]).then_inc(self.sems[si], 1)
        self._mark((si, val, eng), reads, writes)
        self.n_inst += 1

    def dma(self, q, fn, reads=(), writes=()):
        reads = [b for b in reads if b is not None]
        writes = [b for b in writes if b is not None]
        self._emit_waits(q, reads, writes, same_engine_war=True)
        k = self.dma_rr
        self.dma_rr = (self.dma_rr + 1) % N_DMA_SEMS
        si = self.dma_sems[k]
        h = self.sems[si]
        prev = self.dma_val[k]
        if prev > 0 and self.waited[q].get(si, 0) < prev:
            self.waited[q][si] = prev
            self.eobj[q].wait_ge(h, prev)
        self.dma_val[k] = prev + 16
        fn(self.eobj[q]).then_inc(h, 16)
        ev = (si, prev + 16, "dma")
        self._mark(ev, reads, writes)
        self.n_inst += 1
        return ev

    def cc(self, fn, reads=(), writes=()):
        q = "pool"
        reads = [b for b in reads if b is not None]
        writes = [b for b in writes if b is not None]
        self._emit_waits(q, reads, writes, same_engine_war=True)
        si = self.cc_sem
        if self.cc_val > 0:
            self.wait_event(q, (si, self.cc_val, "cc"))
        self.cc_val += 1
        fn(self.eobj[q]).then_inc(self.sems[si], 1)
        ev = (si, self.cc_val, "cc")
        self._mark(ev, reads, writes)
        self.n_inst += 1
        return ev

    def wait_event(self, eng, ev):
        si, val, _ = ev
        if self.waited[eng].get(si, 0) >= val:
            return
        self.waited[eng][si] = val
        self.eobj[eng].wait_ge(self.sems[si], val)

    def all_events(self):
        evs = [(self.cur[p][0], self.cur[p][1], p) for p in self.ENG if self.cur[p][1] > 0]
        for k in range(N_DMA_SEMS):
            if self.dma_val[k] > 0:
                evs.append((self.dma_sems[k], self.dma_val[k], "dma"))
        if self.cc_val > 0:
            evs.append((self.cc_sem, self.cc_val, "cc"))
        return evs

    def barrier(self):
        evs = self.all_events()
        for e in self.ENG:
            for ev in evs:
                if ev[2] == e:
                    continue
                self.wait_event(e, ev)

    def finish(self):
        for ev in self.all_events():
            self.wait_event("sp", ev)


class TL:
    __slots__ = ("t", "b")

    def __init__(self, t, b):
        self.t = t
        self.b = b

    def __getitem__(self, k):
        return self.t[k]


def PID(dh):
    return (dh // 4) * 32 + (dh % 4)


def build_program(depth=DEPTH, dbg=False):
    nc = bass.Bass("TRN2", target_bir_lowering=False)
    LD = depth

    def din(name, shape, dt=F32):
        if name in TINY:
            shape = [1, 1]
        return nc.dram_tensor(name, list(shape), dt, kind="ExternalInput").ap()

    def dout(name, shape):
        return TL(nc.dram_tensor(name, list(shape), F32, kind="ExternalOutput").ap(), Buf(name))

    def dscr(name, shape, dt):
        return TL(nc.dram_tensor(name, list(shape), dt).ap(), Buf(name))

    xin = [din("xp", [1024, D]), din("xs", [1024, D])]
    cvec = din("cvec", [2, D])
    w_mod = din("w_mod", [LD, D, 6 * D]); b_mod = din("b_mod", [LD, 6 * D])
    norm1 = din("norm1", [LD, D]); w_in = din("w_in", [LD, D, NPROJ]); b_in = din("b_in", [LD, NPROJ])
    da_lam = din("da_lam", [LD, 256]); da_subln = din("da_subln", [LD, 128]); ml_norm = din("ml_norm", [LD, 512])
    rpbT = din("rpbT", [LD, 64, 8 * 15 * 64])
    w_up = [din("w_up_da", [LD, 512, D]), din("w_up_ml", [LD, 512, D]), din("w_up_na", [LD, 512, D])]
    w_out = din("w_out", [LD, D, D]); norm2 = din("norm2", [LD, D])
    w_ff1 = din("w_ff1", [LD, D, 4 * D]); b_ff1 = din("b_ff1", [LD, 4 * D])
    w_ff2 = din("w_ff2", [LD, 4 * D, D]); b_ff2 = din("b_ff2", [LD, D]); norm_f = din("norm_f", [D])
    cda_k = din("cda_k", [LD, 512, 512]); cda_v = din("cda_v", [LD, 512, 512])
    cna_k = din("cna_k", [LD, 512, 512]); cna_v = din("cna_v", [LD, 512, 512])
    st_C = din("st_C", [LD, 8, 128, 128]); st_n = din("st_n", [LD, 8, 128]); st_m = din("st_m", [LD, 8])
    ident_d = din("ident", [128, 128]); rope_cos = din("rope_cos", [128, 1024]); rope_sin = din("rope_sin", [128, 1024])
    selv_d = din("selv", [128, 16]); wm_d = din("wm", [128, 16 * 512]); colmask_d = din("colmask", [128, 64])
    selm_d = din("selm", [64, 8 * 128]); mmask_d = din("mmask", [128, 256])

    y_out = [dout("y_p", [1024, D]), dout("y_s", [1024, D])]
    o_da_k = dout("o_da_k", [4, LD, 256, 512]); o_da_v = dout("o_da_v", [4, LD, 256, 512])
    o_na_k = dout("o_na_k", [4, LD, 256, 512]); o_na_v = dout("o_na_v", [4, LD, 256, 512])
    o_ml_C = dout("o_ml_C", [4, LD, 8, 128, 128]); o_ml_n = dout("o_ml_n", [4, LD, 8, 128]); o_ml_m = dout("o_ml_m", [4, LD, 8])

    sc_q_da = [dscr("sc_q_da%d" % g, [512, 1024], BF16) for g in range(2)]
    sc_k_da = dscr("sc_k_da0", [512, 1024], BF16)
    sc_v_da = dscr("sc_v_da0", [1024, 512], BF16)
    sc_q_na = [dscr("sc_q_na%d" % g, [512, 1024], BF16) for g in range(2)]
    sc_k_na = [dscr("sc_k_na%d" % g, [512, 1024], BF16) for g in range(2)]
    sc_v_na = [dscr("sc_v_na%d" % g, [1024, 512], BF16) for g in range(2)]
    sc_ml_q = [dscr("sc_ml_q%d" % g, [512, 1024], BF16) for g in range(2)]
    sc_ml_k = [dscr("sc_ml_k%d" % g, [512, 1024], BF16) for g in range(2)]
    sc_ml_ktm = [dscr("sc_ml_ktm%d" % g, [1024, 512], BF16) for g in range(2)]
    sc_ml_v = [dscr("sc_ml_v%d" % g, [1024, 512], BF16) for g in range(2)]
    sc_ml_o = [dscr("sc_ml_o%d" % g, [512, 1024], BF16) for g in range(2)]
    sc_ml_g = [dscr("sc_ml_g%d" % g, [16, 1024], F32) for g in range(2)]
    sc_gate = [dscr("sc_gate%d" % g, [3072, 1024], BF16) for g in range(2)]
    sc_o = [dscr("sc_o%d" % g, [1536, 1024], BF16) for g in range(2)]
    sc_hml = [dscr("sc_hml%d" % g, [2, 1024, 512], F32) for g in range(2)]
    ag_kt_in = dscr("ag_kt_in", [512, 1024], BF16); ag_kt_out = dscr("ag_kt_out", [2048, 1024], BF16)
    agV_i = dscr("agV_i", [512, 1024], BF16); agV_o = dscr("agV_o", [2048, 1024], BF16)
    ag_h_in = dscr("ag_h_in", [512, 1024], BF16); ag_h_out = dscr("ag_h_out", [2048, 1024], BF16)
    ag_ml_in = dscr("ag_ml_in", [1024, 132], F32); ag_ml_out = dscr("ag_ml_out", [4096, 132], F32)
    RG = [[0, 1, 2, 3], [4, 5, 6, 7]]
    agVi_view = agV_i.t.rearrange("r (h c) -> (r h) c", h=2)
    agVo_view = agV_o.t.rearrange("r (h c) -> (r h) c", h=2)

    with contextlib.ExitStack() as st:
        S = Sched(nc, st)
        cnt = [0]

        def sb(shape, dt, stk, name=None):
            cnt[0] += 1
            nm = (name or "t") + "_%d" % cnt[0]
            return TL(stk.enter_context(nc.sbuf_tensor(nm, list(shape), dt)), Buf(nm))

        PS = [TL(st.enter_context(nc.psum_tensor("ps%d" % i, [128, 512], F32)), Buf("ps%d" % i)) for i in range(8)]

        def bl(x):
            return [t.b if isinstance(t, TL) else t for t in x]

        def op(eng, fn, W=(), R=()):
            if eng == "pool" and not POOL_COMPUTE:
                eng = "dve"
            S.op(eng, fn, reads=bl(R), writes=bl(W))

        def dma(q, out_ap, in_ap, W=(), R=(), slow=False):
            if slow:
                S.dma(q, lambda e: e.dma_start(out=out_ap, in_=in_ap, allow_slow_non_contiguous=True), reads=bl(R), writes=bl(W))
            else:
                S.dma(q, lambda e: e.dma_start(out=out_ap, in_=in_ap), reads=bl(R), writes=bl(W))

        def mm(out_ap, lhsT, rhs, start, stop, W, R):
            op("pe", lambda e: e.matmul(out_ap, lhsT, rhs, start=start, stop=stop), W=W, R=R)

        def act(out_ap, in_ap, func, W, R, bias=None, scale=None, accum=None):
            kw = {}
            if bias is not None:
                kw["bias"] = bias
            if scale is not None:
                kw["scale"] = scale
            if accum is not None:
                kw["accum_out"] = accum
            op("act", lambda e: e.activation(out_ap, in_ap, func, **kw), W=W, R=R)

        def tt(eng, out_ap, a, b, alu, W, R):
            op(eng, lambda e: e.tensor_tensor(out_ap, a, b, alu), W=W, R=R)

        def ts(eng, out_ap, a, s1, s2, op0, op1, W, R):
            if s2 is None:
                op(eng, lambda e: e.tensor_scalar(out_ap, a, s1, None, op0), W=W, R=R)
            else:
                op(eng, lambda e: e.tensor_scalar(out_ap, a, s1, s2, op0, op1), W=W, R=R)

        def stt(eng, out_ap, a, s, b, op0, op1, W, R):
            eng = "dve"
            op(eng, lambda e: e.scalar_tensor_tensor(out_ap, a, s, b, op0, op1), W=W, R=R)

        def cp(eng, out_ap, in_ap, W, R):
            if eng == "act":
                act(out_ap, in_ap, AF.Identity, W, R)
            else:
                op(eng, lambda e: e.tensor_copy(out_ap, in_ap), W=W, R=R)

        def mset(eng, ap, val, W):
            op(eng, lambda e: e.memset(ap, val), W=W)

        def rstd_from(ps_ap, out_ap, n, W, R):
            act(out_ap, ps_ap, AF.Ln, W, R, bias=epsb[:, 0:1], scale=1.0 / n)
            act(out_ap, out_ap, AF.Exp, W, W, scale=-0.5)

        xT = [sb([128, 8, 1024], F32, st, "xT%d" % g) for g in range(2)]
        ident = sb([128, 128], F32, st, "ident")
        identb = sb([128, 128], BF16, st, "identb")
        onesb = sb([128, 128], BF16, st, "onesb")
        epsb = sb([128, 1], F32, st, "epsb")
        selv = sb([128, 16], F32, st, "selv")
        selm = sb([64, 8, 128], F32, st, "selm")
        mmask = sb([128, 2, 128], F32, st, "mmask")
        modv = sb([128, 7, 8, 2], F32, st, "modv")
        bin_fm = sb([128, 65], F32, st, "bin_fm")
        bin_sw = sb([128, 8], F32, st, "bin_sw")
        bg16 = sb([16, 1], F32, st, "bg16")
        bff1 = sb([128, 32], F32, st, "bff1")
        bff2 = sb([128, 8], F32, st, "bff2")
        nrm = sb([128, 3, 8], F32, st, "nrm")
        subw = sb([128, 1], F32, st, "subw")
        lamt = sb([128, 4], F32, st, "lamt")
        mlnw = sb([128, 512], F32, st, "mlnw")

        dma("sp", ident[:], ident_d, W=[ident])
        dma("sp", selv[:], selv_d, W=[selv])
        dma("sp", selm[:], selm_d.rearrange("p (a b) -> p a b", a=8), W=[selm])
        dma("sp", mmask[:], mmask_d.rearrange("p (a b) -> p a b", a=2), W=[mmask])
        cp("dve", identb[:], ident[:], [identb], [ident])
        mset("dve", onesb[:], 1.0, [onesb])
        mset("dve", epsb[:], EPS, [epsb])

        def fmvec(dst_ap, W, src2d, n, stk):
            stg = sb([64, 128], F32, stk, "fmstg")
            dma("sp", stg[0:n, :], src2d, W=[stg])
            op("pe", lambda e: e.transpose(PS[7][:, 0:n], stg[0:n, :], ident[0:n, 0:n]), W=[PS[7]], R=[stg, ident])
            cp("dve", dst_ap, PS[7][:, 0:n], W, [PS[7]])

        with contextlib.ExitStack() as s2:
            xl = [sb([128, 1024], F32, s2, "xl") for _ in range(2)]
            for g in range(2):
                for t8 in range(8):
                    x_ = xl[t8 % 2]
                    dma("sp", x_[:], xin[g][t8 * 128:(t8 + 1) * 128, :], W=[x_])
                    for half in range(2):
                        p = PS[half + 2 * (t8 % 2)]
                        for c in range(4):
                            cc_ = half * 4 + c
                            op("pe", lambda e, p=p, c=c, cc_=cc_, x_=x_: e.transpose(p[:, c * 128:(c + 1) * 128], x_[:, cc_ * 128:(cc_ + 1) * 128], ident[:]),
                               W=[p], R=[x_, ident])
                        cp("dve" if half == 0 else "act", xT[g][:, half * 4:half * 4 + 4, t8 * 128:(t8 + 1) * 128],
                           p[:].rearrange("p (c t) -> p c t", c=4), [xT[g]], [p])
            fmvec(nrm[:, 2, :], [nrm], norm_f.rearrange("(c p) -> c p", p=128), 8, s2)
            S.barrier()

        def layer_vectors(l):
            with contextlib.ExitStack() as s2:
                cv = sb([2, 1024], F32, s2, "cv")
                cT = sb([128, 8, 2], F32, s2, "cT")
                dma("sp", cv[:], cvec, W=[cv])
                act(cv[:], cv[:], AF.Silu, [cv], [cv])
                for kc in range(8):
                    op("pe", lambda e, kc=kc: e.transpose(PS[6][:, kc * 2:kc * 2 + 2], cv[0:2, kc * 128:(kc + 1) * 128], ident[0:2, 0:2]),
                       W=[PS[6]], R=[cv, ident])
                cp("dve", cT[:], PS[6][:, 0:16].rearrange("p (k v) -> p k v", v=2), [cT], [PS[6]])
                if KSUB <= 1:
                    S.barrier(); return
                bm = sb([128, 48], F32, s2, "bm")
                fmvec(bm[:], [bm], b_mod[l].rearrange("(c p) -> c p", p=128), 48, s2)
                if KSUB <= 2:
                    S.barrier(); return
                mraw = sb([128, 48, 2], F32, s2, "mraw")
                wts = [sb([128, 8, 512], F32, s2, "wmod") for _ in range(2)]
                wv = w_mod[l].rearrange("(kc p) n -> p kc n", p=128)
                for blk in range(12):
                    wt = wts[blk % 2]
                    dma("sp", wt[:], wv[:, :, blk * 512:(blk + 1) * 512], W=[wt])
                    p = PS[4 + blk % 2]
                    for j in range(4):
                        for kc in range(8):
                            mm(p[:, j * 2:j * 2 + 2], wt[:, kc, j * 128:(j + 1) * 128], cT[:, kc, :], kc == 0, kc == 7, [p], [wt, cT])
                    for j in range(4):
                        jj = blk * 4 + j
                        ts("dve", mraw[:, jj, :], p[:, j * 2:j * 2 + 2], bm[:, jj:jj + 1], None, ALU.add, None, [mraw], [p, bm])
                if KSUB <= 3:
                    S.barrier(); return
                fmvec(nrm[:, 0, :], [nrm], norm1[l].rearrange("(c p) -> c p", p=128), 8, s2)
                fmvec(nrm[:, 1, :], [nrm], norm2[l].rearrange("(c p) -> c p", p=128), 8, s2)
                fmvec(bff2[:], [bff2], b_ff2[l].rearrange("(c p) -> c p", p=128), 8, s2)
                fmvec(bff1[:], [bff1], b_ff1[l].rearrange("(c p) -> c p", p=128), 32, s2)
                fmvec(bin_fm[:, 0:28], [bin_fm], b_in[l, 0:3584].rearrange("(c p) -> c p", p=128), 28, s2)
                fmvec(bin_fm[:, 28:64], [bin_fm], b_in[l, 3600:8208].rearrange("(c p) -> c p", p=128), 36, s2)
                dma("sp", bg16[:], b_in[l, 3584:3600].rearrange("(p o) -> p o", o=1), W=[bg16])
                if KSUB <= 4:
                    S.barrier(); return
                stg = sb([8, 4, 2, 16], F32, s2, "bsw")
                src = b_in[l, 0:1024].rearrange("(c b t s) -> c b t s", c=8, b=4, t=2)
                dma("sp", stg[:, :, 0, :], src[:, :, 1, :], W=[stg])
                dma("sp", stg[:, :, 1, :], src[:, :, 0, :], W=[stg])
                op("pe", lambda e: e.transpose(PS[7][:, 0:8], stg[:].rearrange("c b t s -> c (b t s)"), ident[0:8, 0:8]), W=[PS[7]], R=[stg, ident])
                cp("dve", bin_sw[:], PS[7][:, 0:8], [bin_sw], [PS[7]])
                if KSUB <= 5:
                    S.barrier(); return
                for k, (sh, sc, gg, nidx) in enumerate(((0, 8, 16, 0), (24, 32, 40, 1))):
                    for v in range(2):
                        stt("dve", modv[:, 3 * k + 0, :, v], mraw[:, sc:sc + 8, v], 1.0, nrm[:, nidx, :], ALU.add, ALU.mult, [modv], [mraw, nrm])
                        cp("dve", modv[:, 3 * k + 1, :, v], mraw[:, sh:sh + 8, v], [modv], [mraw])
                        cp("dve", modv[:, 3 * k + 2, :, v], mraw[:, gg:gg + 8, v], [modv], [mraw])
                for v in range(2):
                    tt("dve", modv[:, 6, :, v], modv[:, 5, :, v], bff2[:], ALU.mult, [modv], [modv, bff2])
                if KSUB <= 6:
                    S.barrier(); return
                lv = sb([128, 4, 64], F32, s2, "lv")
                dma("sp", lv[:].rearrange("p a b -> p (a b)"), da_lam[l:l + 1, :].partition_broadcast(128), W=[lv])
                pr = sb([128, 2, 64], F32, s2, "pr")
                sm = sb([128, 2], F32, s2, "sm")
                tt("dve", pr[:, 0, :], lv[:, 0, :], lv[:, 1, :], ALU.mult, [pr], [lv])
                tt("dve", pr[:, 1, :], lv[:, 2, :], lv[:, 3, :], ALU.mult, [pr], [lv])
                op("dve", lambda e: e.reduce_sum(sm[:], pr[:], mybir.AxisListType.X), W=[sm], R=[pr])
                act(sm[:], sm[:], AF.Exp, [sm], [sm])
                lam_init = 0.8 - 0.6 * math.exp(-0.3 * l)
                tt("dve", lamt[:, 0:1], sm[:, 0:1], sm[:, 1:2], ALU.subtract, [lamt], [sm])
                ts("dve", lamt[:, 0:1], lamt[:, 0:1], lam_init, None, ALU.add, None, [lamt], [lamt])
                ts("dve", lamt[:, 1:2], lamt[:, 0:1], -1.0, None, ALU.mult, None, [lamt], [lamt])
                if KSUB <= 7:
                    S.barrier(); return
                dma("sp", subw[:], da_subln[l].rearrange("(p o) -> p o", o=1), W=[subw])
                ts("dve", subw[:], subw[:], 1.0 - lam_init, None, ALU.mult, None, [subw], [subw])
                dma("sp", mlnw[:], ml_norm[l:l + 1, :].partition_broadcast(128), W=[mlnw])
                S.barrier()

        def norm_to_hT(g, hT, kset):
            with contextlib.ExitStack() as s2:
                sq = sb([128, 8, 512], BF16, s2, "sq")
                rs = sb([128, 512], F32, s2, "rs")
                tmp = [sb([128, 512], F32, s2, "ntmp") for _ in range(2)]
                for tb in range(2):
                    tsl = slice(tb * 512, (tb + 1) * 512)
                    act(sq[:], xT[g][:, :, tsl], AF.Square, [sq], [xT[g]])
                    if KSUB == 81:
                        continue
                    for c in range(8):
                        mm(PS[7][:], onesb[:], sq[:, c, :], c == 0, c == 7, [PS[7]], [onesb, sq])
                    if KSUB == 82:
                        continue
                    rstd_from(PS[7][:], rs[:], 1024.0, [rs], [PS[7], epsb])
                    if KSUB == 83:
                        continue
                    for c in range(8):
                        t_ = tmp[c % 2]
                        stt("dve", t_[:], xT[g][:, c, tsl], modv[:, 3 * kset, c, g:g + 1], rs[:], ALU.mult, ALU.mult, [t_], [xT[g], modv, rs])
                        if KSUB == 84:
                            continue
                        ts("pool" if c % 2 else "dve", hT[:, c, tsl], t_[:], modv[:, 3 * kset + 1, c, g:g + 1], None, ALU.add, None, [hT], [t_, modv])
                S.barrier()

        def bin_chunk(col):
            return col // 128 if col < 3584 else 28 + (col - 3600) // 128

        def proj_fm(l, hT, wts, col0, ncols, handler):
            wv = w_in[l].rearrange("(kc p) n -> p kc n", p=128)
            wt = wts[proj_fm.k % len(wts)]
            proj_fm.k += 1
            S.dma("pool", lambda e: e.dma_start(out=wt[:, :, 0:ncols], in_=wv[:, :, col0:col0 + ncols]), writes=[wt.b])
            nmc = (ncols + 127) // 128
            for mc in range(nmc):
                m = min(128, ncols - mc * 128)
                for tb in range(2):
                    p = PS[proj_fm.pk % 4]
                    proj_fm.pk += 1
                    for kc in range(8):
                        mm(p[0:m, :], wt[:, kc, mc * 128:mc * 128 + m], hT[:, kc, tb * 512:(tb + 1) * 512], kc == 0, kc == 7, [p], [wt, hT])
                    handler(mc, tb, p)
        proj_fm.k = 0
        proj_fm.pk = 0

        def proj_tm(l, hT, wts, col0, handler):
            wv = w_in[l].rearrange("(kc p) n -> p kc n", p=128)
            wt = wts[proj_fm.k % len(wts)]
            proj_fm.k += 1
            S.dma("pool", lambda e: e.dma_start(out=wt[:, :, 0:512], in_=wv[:, :, col0:col0 + 512]), writes=[wt.b])
            S.dma("pool", lambda e: e.dma_start(out=brow[:], in_=b_in[l:l + 1, col0:col0 + 512]), writes=[brow.b])
            for t8 in range(8):
                p = PS[proj_fm.pk % 4]
                proj_fm.pk += 1
                for kc in range(8):
                    mm(p[:], hT[:, kc, t8 * 128:(t8 + 1) * 128], wt[:, kc, 0:512], kc == 0, False, [p], [wt, hT])
                mm(p[:], onesb[0:1, :], brow[0:1, :], False, True, [p], [onesb, brow])
                handler(t8, p)

        brow = sb([1, 512], BF16, st, "brow")

        def inproj(l, g, hT):
            with contextlib.ExitStack() as s2:
                wts = [sb([128, 8, 512], BF16, s2, "wt") for _ in range(2)]
                wsw = sb([128, 8, 512], BF16, s2, "wsw")
                ofm = [sb([128, 1024], BF16, s2, "ofm") for _ in range(2)]
                otm = [sb([128, 512], BF16, s2, "otm") for _ in range(2)]
                otf = [sb([128, 512], F32, s2, "otf") for _ in range(2)]
                og = sb([16, 1024], F32, s2, "og")
                kk = [0, 0]
                if g == 1:
                    rc = sb([128, 1024], F32, s2, "rc"); rsn = sb([128, 1024], F32, s2, "rsn")
                    dma("sp", rc[:], rope_cos, W=[rc]); dma("sp", rsn[:], rope_sin, W=[rsn])
                    t1 = sb([128, 512], F32, s2, "rt1"); t2 = sb([128, 512], F32, s2, "rt2")

                def fm_plain(col0, func, dsts):
                    def h(mc, tb, p):
                        o = ofm[(kk[0] // 2) % 2]
                        kk[0] += 1
                        bc_ = bin_chunk(col0) + mc
                        ts("dve", o[:, tb * 512:(tb + 1) * 512], p[:], bin_fm[:, bc_:bc_ + 1], None, ALU.add, None, [o], [p, bin_fm])
                        if func != AF.Identity:
                            act(o[:, tb * 512:(tb + 1) * 512], o[:, tb * 512:(tb + 1) * 512], func, [o], [o])
                        if tb == 1:
                            for d in dsts:
                                d(mc, o)
                    proj_fm(l, hT, wts, col0, 512, h)

                def to_rows(scr, row0):
                    return lambda mc, o: dma("pool", scr[row0 + mc * 128:row0 + (mc + 1) * 128, :], o[:], W=[scr], R=[o])

                def fm_rope(col0, dsts):
                    wv = w_in[l].rearrange("(kc p) n -> p kc n", p=128)
                    wt = wts[proj_fm.k % 2]
                    proj_fm.k += 1
                    S.dma("pool", lambda e: e.dma_start(out=wt[:], in_=wv[:, :, col0:col0 + 512]), writes=[wt.b])
                    wtv = wt[:].rearrange("p k (b t s) -> p k b t s", t=2, s=16)
                    wsv = wsw[:].rearrange("p k (b t s) -> p k b t s", t=2, s=16)
                    for kc in range(8):
                        cp("pool", wsv[:, kc, :, 0, :], wtv[:, kc, :, 1, :], [wsw], [wt])
                        cp("pool", wsv[:, kc, :, 1, :], wtv[:, kc, :, 0, :], [wsw], [wt])
                    for mc in range(4):
                        ch = col0 // 128 + mc
                        for tb in range(2):
                            tsl = slice(tb * 512, (tb + 1) * 512)
                            pa = PS[proj_fm.pk % 4]; pb = PS[(proj_fm.pk + 1) % 4]
                            proj_fm.pk += 2
                            for kc in range(8):
                                mm(pa[:], wt[:, kc, mc * 128:(mc + 1) * 128], hT[:, kc, tsl], kc == 0, kc == 7, [pa], [wt, hT])
                            for kc in range(8):
                                mm(pb[:], wsw[:, kc, mc * 128:(mc + 1) * 128], hT[:, kc, tsl], kc == 0, kc == 7, [pb], [wsw, hT])
                            o = ofm[(kk[0] // 2) % 2]
                            kk[0] += 1
                            stt("dve", t1[:], pa[:], bin_fm[:, ch:ch + 1], rc[:, tsl], ALU.add, ALU.mult, [t1], [pa, bin_fm, rc])
                            stt("dve", t2[:], pb[:], bin_sw[:, ch:ch + 1], rsn[:, tsl], ALU.add, ALU.mult, [t2], [pb, bin_sw, rsn])
                            tt("pool", o[:, tsl], t1[:], t2[:], ALU.add, [o], [t1, t2])
                            if tb == 1:
                                for d in dsts:
                                    d(mc, o)

                def tm_seg(col0, f32_dst, bf_dsts):
                    def h(t8, p):
                        if f32_dst is not None:
                            o = otf[kk[1] % 2]
                            cp("act", o[:], p[:], [o], [p])
                            f32_dst(t8, o)
                        if bf_dsts:
                            ob = otm[kk[1] % 2]
                            if f32_dst is not None:
                                cp("dve", ob[:], o[:], [ob], [o])
                            else:
                                cp("dve", ob[:], p[:], [ob], [p])
                            for d in bf_dsts:
                                d(t8, ob)
                        kk[1] += 1
                    proj_tm(l, hT, wts, col0, h)

                def tm_rows(scr, row0=0):
                    return lambda t8, o: dma("pool", scr[row0 + t8 * 128:row0 + (t8 + 1) * 128, :], o[:], W=[scr], R=[o])

                def out_rows(dst):
                    return lambda t8, o: dma("pool", dst[t8 // 2, l, (t8 % 2) * 128:(t8 % 2) * 128 + 128, :], o[:], W=[dst], R=[o])

                def halo_rows(scr):
                    def f(t8, o):
                        if t8 < 2:
                            dma("pool", scr[t8 * 128:(t8 + 1) * 128, 512:1024], o[:], W=[scr], R=[o])
                        if t8 >= 6:
                            dma("pool", scr[256 + (t8 - 6) * 128:256 + (t8 - 5) * 128, 512:1024], o[:], W=[scr], R=[o])
                    return f

                def kt_cols(scr, c0, c1, d0):
                    return lambda mc, o: dma("pool", scr[mc * 128:(mc + 1) * 128, d0:d0 + (c1 - c0)], o[:, c0:c1], W=[scr], R=[o])

                if g == 0:
                    fm_plain(0, AF.Identity, [to_rows(sc_q_da[0], 0)])
                    fm_plain(512, AF.Identity, [to_rows(sc_k_da, 0)])
                    if KSUB == 19:
                        S.barrier(); return
                    tm_seg(512, out_rows(o_da_k), [])
                    if KSUB == 20:
                        S.barrier(); return
                    tm_seg(1024, out_rows(o_da_v), [tm_rows(sc_v_da)])
                    if KSUB == 21:
                        S.barrier(); return
                else:
                    fm_rope(0, [to_rows(sc_q_da[1], 0)])
                    if KSUB <= 9:
                        S.barrier(); return
                    fm_rope(512, [kt_cols(ag_kt_in, 0, 1024, 0)])
                    tm_seg(1024, None, [lambda t8, o: dma("pool", agVi_view[t8 * 128:(t8 + 1) * 128, :], o[:], W=[agV_i], R=[o])])
                if KSUB <= 10:
                    S.barrier(); return
                fm_plain(3600, AF.Identity, [to_rows(sc_q_na[g], 0)])
                if g == 0:
                    fm_plain(4112, AF.Identity, [to_rows(sc_k_na[0], 0)])
                    tm_seg(4112, out_rows(o_na_k), [])
                    tm_seg(4624, out_rows(o_na_v), [tm_rows(sc_v_na[0])])
                else:
                    fm_plain(4112, AF.Identity, [to_rows(sc_k_na[1], 0), kt_cols(ag_h_in, 0, 256, 0), kt_cols(ag_h_in, 768, 1024, 256)])
                    tm_seg(4624, None, [tm_rows(sc_v_na[1]), halo_rows(ag_h_in)])
                    if KSUB <= 11:
                        S.barrier(); return
                    S.cc(lambda e: e.collective_compute("AllGather", ALU.bypass, replica_groups=RG, ins=[ag_h_in.t.opt()], outs=[ag_h_out.t.opt()]),
                         reads=[ag_h_in.b], writes=[ag_h_out.b])
                    S.cc(lambda e: e.collective_compute("AllGather", ALU.bypass, replica_groups=RG, ins=[ag_kt_in.t.opt()], outs=[ag_kt_out.t.opt()]),
                         reads=[ag_kt_in.b], writes=[ag_kt_out.b])
                    S.cc(lambda e: e.collective_compute("AllGather", ALU.bypass, replica_groups=RG, ins=[agV_i.t.opt()], outs=[agV_o.t.opt()]),
                         reads=[agV_i.b], writes=[agV_o.b])
                if KSUB <= 12:
                    S.barrier(); return
                fm_plain(1536, AF.Identity, [to_rows(sc_ml_q[g], 0)])
                fm_plain(2048, AF.Identity, [to_rows(sc_ml_k[g], 0)])
                fm_plain(3072, AF.Sigmoid, [to_rows(sc_ml_o[g], 0)])
                tm_seg(2048, None, [tm_rows(sc_ml_ktm[g])])
                tm_seg(2560, None, [tm_rows(sc_ml_v[g])])
                if KSUB <= 13:
                    S.barrier(); return

                def hg(mc, tb, p):
                    ts("dve", og[:, tb * 512:(tb + 1) * 512], p[0:16, :], bg16[:, 0:1], None, ALU.add, None, [og], [p, bg16])
                    if tb == 1:
                        dma("pool", sc_ml_g[g][:, :], og[:], W=[sc_ml_g[g]], R=[og])
                proj_fm(l, hT, wts, 3584, 16, hg)
                for i in range(6):
                    fm_plain(5136 + 512 * i, AF.Sigmoid, [to_rows(sc_gate[g], 512 * i)])
                S.barrier()

        def attn_block(kts, vts, qT, nq, dv, p_lo, o_ps, den_ps, e_tiles, ecnt, bias_tiles=None):
            n = len(kts)
            for i in range(n):
                sp_ = PS[ecnt[0] % 2]
                e_ = e_tiles[ecnt[0] % 2]
                ecnt[0] += 1
                bt = bias_tiles[i] if bias_tiles is not None else None
                mm(sp_[:, 0:nq], kts[i][0], qT[0], True, bt is None, [sp_], kts[i][1] + qT[1])
                if bt is not None:
                    mm(sp_[:, 0:nq], identb[:], bt[0], False, True, [sp_], [identb] + bt[1])
                act(e_[:, 0:nq], sp_[:, 0:nq], AF.Exp, [e_], [sp_], scale=0.125)
                mm(o_ps[p_lo:p_lo + dv, 0:nq], vts[i][0], e_[:, 0:nq], i == 0, i == n - 1, [o_ps], vts[i][1] + [e_])
                mm(den_ps[p_lo:p_lo + dv, 0:nq], onesb[:, 0:dv], e_[:, 0:nq], i == 0, i == n - 1, [den_ps], [onesb, e_])

        def da_finish(nq, o0, d0, o1, d1, work, out_ap, W):
            r0, r1, t0, t1, sq, rs = work
            op("dve", lambda e: e.reciprocal(r0[:, 0:nq], d0[:, 0:nq]), W=[r0], R=[d0])
            tt("dve", t0[:, 0:nq], o0[:, 0:nq], r0[:, 0:nq], ALU.mult, [t0], [o0, r0])
            op("dve", lambda e: e.reciprocal(r1[:, 0:nq], d1[:, 0:nq]), W=[r1], R=[d1])
            tt("dve", t1[:, 0:nq], o1[:, 0:nq], r1[:, 0:nq], ALU.mult, [t1], [o1, r1])
            stt("dve", t0[:, 0:nq], t1[:, 0:nq], lamt[:, 1:2], t0[:, 0:nq], ALU.mult, ALU.add, [t0], [t1, lamt, t0])
            act(sq[:, 0:nq], t0[:, 0:nq], AF.Square, [sq], [t0])
            mm(PS[6][:, 0:nq], onesb[:], sq[:, 0:nq], True, True, [PS[6]], [onesb, sq])
            rstd_from(PS[6][:, 0:nq], rs[:, 0:nq], 128.0, [rs], [PS[6], epsb])
            stt("dve", out_ap, t0[:, 0:nq], subw[:, 0:1], rs[:, 0:nq], ALU.mult, ALU.mult, W, [t0, subw, rs])

        def da_work(s2):
            return (sb([128, 512], F32, s2, "r0"), sb([128, 512], F32, s2, "r1"), sb([128, 512], F32, s2, "t0"),
                    sb([128, 512], F32, s2, "t1"), sb([128, 512], BF16, s2, "sq"), sb([128, 512], F32, s2, "rs"))

        def da_prompt(l):
            with contextlib.ExitStack() as s2:
                qt = [sb([128, 1024], BF16, s2, "qt") for _ in range(2)]
                kt = [sb([128, 1024], BF16, s2, "kt") for _ in range(2)]
                vt = [sb([128, 8, 128], BF16, s2, "vt") for _ in range(2)]
                et = [sb([128, 512], BF16, s2, "et") for _ in range(2)]
                ot = [sb([128, 1024], BF16, s2, "ot") for _ in range(2)]
                work = da_work(s2)
                ecnt = [0]
                for h in range(4):
                    q_, k_, v_, o_ = qt[h % 2], kt[h % 2], vt[h % 2], ot[h % 2]
                    dma("sp", q_[:], sc_q_da[0][h * 128:(h + 1) * 128, :], W=[q_], R=[sc_q_da[0]])
                    dma("sp", k_[:], sc_k_da[h * 128:(h + 1) * 128, :], W=[k_], R=[sc_k_da])
                    dma("sp", v_[:], sc_v_da[:, h * 128:(h + 1) * 128].rearrange("(t p) d -> p t d", p=128), W=[v_], R=[sc_v_da])
                    for s in range(4):
                        for j in range(2):
                            kts = [(k_[j * 64:(j + 1) * 64, s * 256 + i * 128:s * 256 + (i + 1) * 128], [k_]) for i in range(2)]
                            vts = [(v_[:, s * 2 + i, :], [v_]) for i in range(2)]
                            attn_block(kts, vts, (q_[j * 64:(j + 1) * 64, s * 256:(s + 1) * 256], [q_]), 256, 128, 0,
                                       PS[2 + 2 * j], PS[3 + 2 * j], et, ecnt)
                        da_finish(256, PS[2], PS[3], PS[4], PS[5], work, o_[:, s * 256:(s + 1) * 256], [o_])
                    dma("pool", sc_o[0][h * 128:(h + 1) * 128, :], o_[:], W=[sc_o[0]], R=[o_])
                S.barrier()

        def load_ctx_kT(l, src, dst, s2):
            stg = [sb([128, 512], F32, s2, "cstg") for _ in range(2)]
            for t4 in range(4):
                g_ = stg[t4 % 2]
                dma("sp", g_[:], src[l, t4 * 128:(t4 + 1) * 128, :], W=[g_])
                p = PS[t4 % 2]
                for c in range(4):
                    op("pe", lambda e, p=p, c=c, g_=g_: e.transpose(p[:, c * 128:(c + 1) * 128], g_[:, c * 128:(c + 1) * 128], ident[:]), W=[p], R=[g_, ident])
                cp("dve", dst[:, :, t4 * 128:(t4 + 1) * 128], p[:].rearrange("p (c t) -> p c t", c=4), [dst], [p])

        def da_sample(l):
            with contextlib.ExitStack() as s2:
                ckt = sb([128, 4, 512], BF16, s2, "ckt")
                cv = sb([128, 4, 512], BF16, s2, "cvt")
                load_ctx_kT(l, cda_k, ckt, s2)
                S.dma("pool", lambda e: e.dma_start(out=cv[:], in_=cda_v[l].rearrange("(t p) d -> p t d", p=128)), writes=[cv.b])
                qt = [sb([128, 1024], BF16, s2, "qt") for _ in range(2)]
                kt = [sb([128, 4096], BF16, s2, "kt") for _ in range(2)]
                vt = [sb([128, 32, 128], BF16, s2, "vt") for _ in range(2)]
                et = [sb([128, 512], BF16, s2, "et") for _ in range(2)]
                ot = [sb([128, 1024], BF16, s2, "ot") for _ in range(2)]
                work = da_work(s2)
                ecnt = [0]
                kv = ag_kt_out.t.rearrange("(r f) t -> f r t", r=4)
                vv = agVo_view.rearrange("(r t) d -> r t d", r=4)
                for h in range(4):
                    q_, k_, v_, o_ = qt[h % 2], kt[h % 2], vt[h % 2], ot[h % 2]
                    dma("sp", q_[:], sc_q_da[1][h * 128:(h + 1) * 128, :], W=[q_], R=[sc_q_da[1]])
                    dma("sp", k_[:].rearrange("p (r t) -> p r t", r=4), kv[h * 128:(h + 1) * 128, :, 0:1024], W=[k_], R=[ag_kt_out])
                    for r in range(4):
                        dma("sp", v_[:, r * 8:(r + 1) * 8, :], vv[r, 0:1024, h * 128:(h + 1) * 128].rearrange("(t p) d -> p t d", p=128), W=[v_], R=[agV_o])
                    for qb in range(2):
                        for j in range(2):
                            kts = [(k_[j * 64:(j + 1) * 64, i * 128:(i + 1) * 128], [k_]) for i in range(32)]
                            kts += [(ckt[j * 64:(j + 1) * 64, h, i * 128:(i + 1) * 128], [ckt]) for i in range(4)]
                            vts = [(v_[:, i, :], [v_]) for i in range(32)] + [(cv[:, i, h * 128:(h + 1) * 128], [cv]) for i in range(4)]
                            attn_block(kts, vts, (q_[j * 64:(j + 1) * 64, qb * 512:(qb + 1) * 512], [q_]), 512, 128, 0,
                                       PS[2 + 2 * j], PS[3 + 2 * j], et, ecnt)
                        da_finish(512, PS[2], PS[3], PS[4], PS[5], work, o_[:, qb * 512:(qb + 1) * 512], [o_])
                    dma("pool", sc_o[1][h * 128:(h + 1) * 128, :], o_[:], W=[sc_o[1]], R=[o_])
                S.barrier()

        def na_finish(nq, o_ps, d_ps, rr, out_ap, W):
            op("dve", lambda e: e.reciprocal(rr[:, 0:nq], d_ps[:, 0:nq]), W=[rr], R=[d_ps])
            tt("dve", out_ap, o_ps[:, 0:nq], rr[:, 0:nq], ALU.mult, W, [o_ps, rr])

        def na_prompt(l):
            with contextlib.ExitStack() as s2:
                qt = [sb([128, 1024], BF16, s2, "qt") for _ in range(2)]
                kt = [sb([128, 1024], BF16, s2, "kt") for _ in range(2)]
                vt = sb([128, 8, 512], BF16, s2, "vt")
                et = [sb([128, 512], BF16, s2, "et") for _ in range(2)]
                ot = [sb([128, 1024], BF16, s2, "ot") for _ in range(2)]
                rr = sb([128, 512], F32, s2, "rr")
                ecnt = [0]
                dma("sp", vt[:], sc_v_na[0][:, :].rearrange("(t p) d -> p t d", p=128), W=[vt], R=[sc_v_na[0]])
                for c in range(4):
                    q_, k_, o_ = qt[c % 2], kt[c % 2], ot[c % 2]
                    dma("sp", q_[:], sc_q_na[0][c * 128:(c + 1) * 128, :], W=[q_], R=[sc_q_na[0]])
                    dma("sp", k_[:], sc_k_na[0][c * 128:(c + 1) * 128, :], W=[k_], R=[sc_k_na[0]])
                    for s in range(4):
                        for hh in range(2):
                            hd = 2 * c + hh
                            kts = [(k_[hh * 64:(hh + 1) * 64, s * 256 + i * 128:s * 256 + (i + 1) * 128], [k_]) for i in range(2)]
                            vts = [(vt[:, s * 2 + i, hd * 64:(hd + 1) * 64], [vt]) for i in range(2)]
                            attn_block(kts, vts, (q_[hh * 64:(hh + 1) * 64, s * 256:(s + 1) * 256], [q_]), 256, 64, hh * 64,
                                       PS[2], PS[3], et, ecnt)
                        na_finish(256, PS[2], PS[3], rr, o_[:, s * 256:(s + 1) * 256], [o_])
                    dma("pool", sc_o[0][1024 + c * 128:1024 + (c + 1) * 128, :], o_[:], W=[sc_o[0]], R=[o_])
                S.barrier()

        def sel_combine(dst_ap, W, src4, off, stk_tmp):
            ts("dve", dst_ap, src4[0], selv[:, off:off + 1], None, ALU.mult, None, W, [stk_tmp, selv])
            for r in range(1, 4):
                stt("dve", dst_ap, src4[r], selv[:, off + r:off + r + 1], dst_ap, ALU.mult, ALU.add, W, [stk_tmp, selv] + list(W))

        def na_sample(l):
            with contextlib.ExitStack() as s2:
                ckt = sb([128, 4, 512], BF16, s2, "ckt")
                cv = sb([128, 4, 512], BF16, s2, "cvt")
                load_ctx_kT(l, cna_k, ckt, s2)
                S.dma("pool", lambda e: e.dma_start(out=cv[:], in_=cna_v[l].rearrange("(t p) d -> p t d", p=128)), writes=[cv.b])
                tbl = sb([128, 8, 15, 64], BF16, s2, "tbl")
                wmt = sb([128, 16, 512], BF16, s2, "wmt")
                S.dma("pool", lambda e: e.dma_start(out=wmt[:], in_=wm_d.rearrange("p (a b) -> p a b", a=16)), writes=[wmt.b])
                with contextlib.ExitStack() as s3:
                    rp = sb([128, 8 * 15, 64], F32, s3, "rp")
                    cm = sb([128, 64], F32, s3, "cm")
                    dma("sp", rp[0:64], rpbT[l].rearrange("p (a b) -> p a b", b=64), W=[rp])
                    dma("sp", rp[64:128], rpbT[l].rearrange("p (a b) -> p a b", b=64), W=[rp])
                    dma("sp", cm[:], colmask_d, W=[cm])
                    for h in range(8):
                        stt("dve", tbl[:, h, :, :], rp[:, h * 15:(h + 1) * 15, :], 8.0, cm[:].rearrange("p (o q) -> p o q", o=1).to_broadcast([128, 15, 64]), ALU.mult, ALU.add, [tbl], [rp, cm])
                    S.barrier()
                vx = sb([128, 12, 512], BF16, s2, "vx")
                dma("sp", vx[:, 2:10, :], sc_v_na[1][:, :].rearrange("(t p) d -> p t d", p=128), W=[vx], R=[sc_v_na[1]])
                with contextlib.ExitStack() as s3:
                    h4 = sb([128, 4, 2, 512], BF16, s3, "h4")
                    vv = ag_h_out.t.rearrange("(r i) c -> r i c", r=4)
                    for part, (row0, off, t0) in enumerate(((256, 0, 0), (0, 4, 10))):
                        for r in range(4):
                            dma("sp", h4[:, r, :, :], vv[r, row0:row0 + 256, 512:1024].rearrange("(t p) d -> p t d", p=128), W=[h4], R=[ag_h_out])
                        sel_combine(vx[:, t0:t0 + 2, :], [vx], [h4[:, r, :, :] for r in range(4)], off, h4)
                    S.barrier()
                qt = [sb([128, 1024], BF16, s2, "qt") for _ in range(2)]
                kx = [sb([128, 1536], BF16, s2, "kx") for _ in range(2)]
                k4 = sb([128, 4, 512], BF16, s2, "k4")
                et = [sb([128, 512], BF16, s2, "et") for _ in range(2)]
                ot = [sb([128, 1024], BF16, s2, "ot") for _ in range(2)]
                bts = [sb([128, 512], BF16, s2, "bt") for _ in range(8)]
                rr = sb([128, 512], F32, s2, "rr")
                ecnt = [0]
                bcnt = [0]
                kv = ag_h_out.t.rearrange("(r f) t -> f r t", r=4)
                for c in range(4):
                    q_, k_, o_ = qt[c % 2], kx[c % 2], ot[c % 2]
                    dma("sp", q_[:], sc_q_na[1][c * 128:(c + 1) * 128, :], W=[q_], R=[sc_q_na[1]])
                    dma("sp", k_[:, 256:1280], sc_k_na[1][c * 128:(c + 1) * 128, :], W=[k_], R=[sc_k_na[1]])
                    dma("sp", k4[:], kv[c * 128:(c + 1) * 128, :, 0:512], W=[k4], R=[ag_h_out])
                    sel_combine(k_[:, 0:256], [k_], [k4[:, r, 256:512] for r in range(4)], 0, k4)
                    sel_combine(k_[:, 1280:1536], [k_], [k4[:, r, 0:256] for r in range(4)], 4, k4)
                    for qb in range(2):
                        for hh in range(2):
                            hd = 2 * c + hh
                            kts, vts, bl_ = [], [], []
                            for kt8 in range(8):
                                ktile = 4 * qb + kt8
                                bt = bts[bcnt[0] % 8]
                                bcnt[0] += 1
                                cp("pool", bt[:], wmt[:, qb * 8 + kt8, :], [bt], [wmt])
                                for a in range(2):
                                    jp = 2 * ktile + a
                                    ilo = max(jp - 11, 8 * qb); ihi = min(jp + 3, 8 * qb + 7)
                                    if ilo > ihi:
                                        continue
                                    clo, chi = ilo - 8 * qb, ihi - 8 * qb
                                    tlo = ilo - jp + 11
                                    n_ = chi - clo + 1
                                    tt("pool", bt[a * 64:(a + 1) * 64, clo * 64:(chi + 1) * 64].rearrange("p (n q) -> p n q", q=64),
                                       bt[a * 64:(a + 1) * 64, clo * 64:(chi + 1) * 64].rearrange("p (n q) -> p n q", q=64),
                                       tbl[a * 64:(a + 1) * 64, hd, tlo:tlo + n_, :], ALU.add, [bt], [bt, tbl])
                                kts.append((k_[hh * 64:(hh + 1) * 64, ktile * 128:(ktile + 1) * 128], [k_]))
                                vts.append((vx[:, ktile, hd * 64:(hd + 1) * 64], [vx]))
                                bl_.append((bt[:], [bt]))
                            for i in range(4):
                                kts.append((ckt[hh * 64:(hh + 1) * 64, c, i * 128:(i + 1) * 128], [ckt]))
                                vts.append((cv[:, i, hd * 64:(hd + 1) * 64], [cv]))
                                bl_.append(None)
                            attn_block(kts, vts, (q_[hh * 64:(hh + 1) * 64, qb * 512:(qb + 1) * 512], [q_]), 512, 64, hh * 64,
                                       PS[2], PS[3], et, ecnt, bias_tiles=bl_)
                        na_finish(512, PS[2], PS[3], rr, o_[:, qb * 512:(qb + 1) * 512], [o_])
                    dma("pool", sc_o[1][1024 + c * 128:1024 + (c + 1) * 128, :], o_[:], W=[sc_o[1]], R=[o_])
                S.barrier()

        CS = 128.0 ** -0.5

        def ml_gates(g, tok0, T, s2):
            IG = sb([64, T], F32, s2, "IG"); FP = sb([64, T], F32, s2, "FP"); B = sb([64, T], F32, s2, "B"); A = sb([64, T], F32, s2, "A")
            mset("pool", IG[:], 0.0, [IG]); mset("pool", FP[:], 0.0, [FP])
            for d in range(2):
                dma("sp", IG[d * 32:d * 32 + 4, :], sc_ml_g[g][d * 8:d * 8 + 4, tok0:tok0 + T], W=[IG], R=[sc_ml_g[g]])
                dma("sp", FP[d * 32:d * 32 + 4, :], sc_ml_g[g][d * 8 + 4:d * 8 + 8, tok0:tok0 + T], W=[FP], R=[sc_ml_g[g]])
            act(FP[:], FP[:], AF.Exp, [FP], [FP], scale=-1.0)
            ts("dve", FP[:], FP[:], 1.0, None, ALU.add, None, [FP], [FP])
            act(FP[:], FP[:], AF.Ln, [FP], [FP])
            ts("dve", FP[:], FP[:], -1.0, None, ALU.mult, None, [FP], [FP])
            op("dve", lambda e: e.tensor_tensor_scan(B[0:32, :], FP[0:32, :], FP[0:32, :], 0.0, ALU.add, ALU.bypass), W=[B], R=[FP])
            op("dve", lambda e: e.tensor_tensor_scan(B[32:64, T - 1::-1], FP[32:64, T - 1::-1], FP[32:64, T - 1::-1], 0.0, ALU.add, ALU.bypass), W=[B], R=[FP])
            tt("dve", A[:], IG[:], B[:], ALU.subtract, [A], [IG, B])
            return B, A

        def cummax(G, A, init_ap, T, R):
            op("dve", lambda e: e.tensor_tensor_scan(G[0:32, :], A[0:32, :], A[0:32, :], init_ap[0:32, :], ALU.max, ALU.bypass), W=[G], R=[A] + R)
            op("dve", lambda e: e.tensor_tensor_scan(G[32:64, T - 1::-1], A[32:64, T - 1::-1], A[32:64, T - 1::-1], init_ap[32:64, :], ALU.max, ALU.bypass), W=[G], R=[A] + R)

        def ends(dst, src, T, nch, W, R):
            v = src.rearrange("p (c t) -> p c t", t=128)
            cp("dve", dst[0:32, :], v[0:32, :, 127], W, R)
            cp("dve", dst[32:64, :], v[32:64, :, 0], W, R)

        def ml_scan(l, g, tok0, T, m_in, c_in, seq, s2o):
            nch = T // 128
            with contextlib.ExitStack() as s2:
                B, A = ml_gates(g, tok0, T, s2)
                G = sb([64, T], F32, s2, "G")
                cummax(G, A, m_in, T, [m_in])
                VEC = sb([64, 4, T], F32, s2, "VEC")
                NG = sb([64, T], F32, s2, "NG")
                Ge = sb([64, nch], F32, s2, "Ge"); Gp = sb([64, nch], F32, s2, "Gp"); DEC = sb([64, nch], F32, s2, "DEC")
                Mt = sb([64, T], F32, s2, "Mt")
                cp("dve", VEC[:, 0, :], A[:], [VEC], [A])
                ts("dve", NG[:], G[:], -1.0, None, ALU.mult, None, [NG], [G])
                tt("dve", Mt[:], B[:], G[:], ALU.add, [Mt], [B, G])
                act(VEC[:, 3, :], Mt[:], AF.Exp, [VEC], [Mt], scale=-1.0)
                ends(Ge[:], G[:], T, nch, [Ge], [G])
                cp("dve", Gp[0:32, 0:1], m_in[0:32, :], [Gp], [m_in])
                cp("dve", Gp[32:64, nch - 1:nch], m_in[32:64, :], [Gp], [m_in])
                if nch > 1:
                    cp("dve", Gp[0:32, 1:nch], Ge[0:32, 0:nch - 1], [Gp], [Ge])
                    cp("dve", Gp[32:64, 0:nch - 1], Ge[32:64, 1:nch], [Gp], [Ge])
                tt("dve", DEC[:], Gp[:], Ge[:], ALU.subtract, [DEC], [Gp, Ge])
                act(DEC[:], DEC[:], AF.Exp, [DEC], [DEC])
                Gv = G[:].rearrange("p (c t) -> p c t", t=128)
                Av = A[:].rearrange("p (c t) -> p c t", t=128)
                tt("dve", VEC[:, 2, :].rearrange("p (c t) -> p c t", t=128), Gp[:].rearrange("p (c o) -> p c o", o=1).to_broadcast([64, nch, 128]), Gv, ALU.subtract, [VEC], [Gp, G])
                tt("dve", VEC[:, 1, :].rearrange("p (c t) -> p c t", t=128), Av, Ge[:].rearrange("p (c o) -> p c o", o=1).to_broadcast([64, nch, 128]), ALU.subtract, [VEC], [A, Ge])
                act(VEC[:, 1:3, :], VEC[:, 1:3, :], AF.Exp, [VEC], [VEC])
                TMV = sb([128, nch, 4, 64], F32, s2, "TMV")
                for c in range(nch):
                    p = PS[6 + c % 2]
                    for k in range(4):
                        op("pe", lambda e, p=p, k=k, c=c: e.transpose(p[:, k * 64:(k + 1) * 64], VEC[:, k, c * 128:(c + 1) * 128], ident[0:64, 0:64]), W=[p], R=[VEC, ident])
                    cp("act", TMV[:, c, :, :], p[:, 0:256].rearrange("p (k d) -> p k d", k=4), [TMV], [p])
                DECB = sb([128, 8, nch], F32, s2, "DECB")
                for dh in range(8):
                    mm(PS[6][:, dh * nch:(dh + 1) * nch], selm[:, dh, :], DEC[:], True, True, [PS[6]], [selm, DEC])
                cp("dve", DECB[:], PS[6][:, 0:8 * nch].rearrange("p (a b) -> p a b", a=8), [DECB], [PS[6]])
                if seq is not None:
                    mf = sb([64, 1], F32, s2, "mf")
                    cp("dve", mf[0:32, :], Mt[0:32, T - 1:T], [mf], [Mt])
                    cp("dve", mf[32:64, :], Mt[32:64, 0:1], [mf], [Mt])
                    for d in range(2):
                        dma("pool", o_ml_m[seq, l, d * 4:(d + 1) * 4].rearrange("(p o) -> p o", o=1), mf[d * 32:d * 32 + 4, :], W=[o_ml_m], R=[mf])
                qts = [sb([128, T], BF16, s2, "mq") for _ in range(2)]
                kts = [sb([128, T], BF16, s2, "mk") for _ in range(2)]
                ktm = [sb([128, nch, 128], BF16, s2, "mkt") for _ in range(2)]
                vau = [sb([128, nch, 132], BF16, s2, "mv") for _ in range(2)]
                CSTs = [sb([128, 129], F32, s2, "CST") for _ in range(2)]; CSTbs = [sb([128, 129], BF16, s2, "CSTb") for _ in range(2)]
                dars = [sb([128, 128], F32, s2, "dar") for _ in range(2)]; dds = [sb([128, 128], F32, s2, "dd") for _ in range(2)]
                pTs = [sb([128, 128], BF16, s2, "pT") for _ in range(2)]
                has = [sb([128, 129], F32, s2, "ha") for _ in range(2)]; hns = [sb([128, 129], F32, s2, "hn") for _ in range(2)]
                dns = [sb([128, 2], F32, s2, "dn") for _ in range(2)]
                houts = [[sb([128, 128], F32, s2, "hout") for _ in range(2)] for _ in range(2)]
                kps = [sb([128, 128], BF16, s2, "kp") for _ in range(2)]
                PSd = [[TL(PS[k].t, Buf("psml%d_%d" % (d, k))) for k in range(5)] for d in range(2)]
                hc = [0, 0]
                for h in range(4):
                    q_, k_, km, v_ = qts[h % 2], kts[h % 2], ktm[h % 2], vau[h % 2]
                    dma("sp", q_[:], sc_ml_q[g][h * 128:(h + 1) * 128, tok0:tok0 + T], W=[q_], R=[sc_ml_q[g]])
                    dma("sp", k_[:], sc_ml_k[g][h * 128:(h + 1) * 128, tok0:tok0 + T], W=[k_], R=[sc_ml_k[g]])
                    dma("sp", km[:], sc_ml_ktm[g][tok0:tok0 + T, h * 128:(h + 1) * 128].rearrange("(c p) d -> p c d", p=128), W=[km], R=[sc_ml_ktm[g]])
                    mset("pool", v_[:, :, 128:129], 1.0, [v_])
                    dma("sp", v_[:, :, 0:128], sc_ml_v[g][tok0:tok0 + T, h * 128:(h + 1) * 128].rearrange("(c p) d -> p c d", p=128), W=[v_], R=[sc_ml_v[g]])
                    for d in range(2):
                        dh = d * 4 + h
                        if c_in is None:
                            mset("pool", CSTs[d][:], 0.0, [CSTs[d]])
                        else:
                            cp("pool", CSTs[d][:], c_in[:, dh, :], [CSTs[d]], [c_in])
                        cp("dve", CSTbs[d][:], CSTs[d][:], [CSTbs[d]], [CSTs[d]])
                    for step in range(nch):
                        for d in range(2):
                            dh = d * 4 + h
                            pid = PID(dh)
                            c = step if d == 0 else nch - 1 - step
                            csl = slice(c * 128, (c + 1) * 128)
                            o0 = d * 256
                            P0, P1, P2, P3, P4 = PSd[d]
                            CST, CSTb, dar, dd, pT, ha, hn, dn, kp = CSTs[d], CSTbs[d], dars[d], dds[d], pTs[d], has[d], hns[d], dns[d], kps[d]
                            mm(P0[:, o0:o0 + 128], k_[:, csl], q_[:, csl], True, True, [P0], [k_, q_])
                            mm(P1[:, o0:o0 + 128], selm[:, dh, :], NG[:, csl], True, True, [P1], [selm, NG])
                            stt("dve", dar[:], P1[:, o0:o0 + 128], TMV[:, c, 0, pid:pid + 1], mmask[:, d, :], ALU.add, ALU.min, [dar], [P1, TMV, mmask])
                            act(dd[:], dar[:], AF.Exp, [dd], [dar])
                            stt("dve", pT[:], P0[:, o0:o0 + 128], CS, dd[:], ALU.mult, ALU.mult, [pT], [P0, dd])
                            mm(P2[:, o0:o0 + 129], pT[:], v_[:, c, 0:129], True, True, [P2], [pT, v_])
                            mm(P3[:, o0:o0 + 129], q_[:, csl], CSTb[:], True, True, [P3], [q_, CSTb])
                            cp("act", ha[:], P2[:, o0:o0 + 129], [ha], [P2])
                            stt("dve", hn[:], P3[:, o0:o0 + 129], TMV[:, c, 2, pid:pid + 1], ha[:], ALU.mult, ALU.add, [hn], [P3, TMV, ha])
                            stt("pool", dn[:, 0:1], hn[:, 128:129], -1.0, hn[:, 128:129], ALU.mult, ALU.max, [dn], [hn])
                            ts("pool", dn[:, 0:1], dn[:, 0:1], TMV[:, c, 3, pid:pid + 1], None, ALU.max, None, [dn], [dn, TMV])
                            op("dve", lambda e, dn=dn: e.reciprocal(dn[:, 1:2], dn[:, 0:1]), W=[dn], R=[dn])
                            ho = houts[d][hc[d] % 2]
                            hc[d] += 1
                            ts("pool", ho[:], hn[:, 0:128], dn[:, 1:2], None, ALU.mult, None, [ho], [hn, dn])
                            dma("pool", sc_hml[g][d, tok0 + c * 128:tok0 + (c + 1) * 128, h * 128:(h + 1) * 128], ho[:], W=[sc_hml[g]], R=[ho])
                            ts("dve", kp[:], km[:, c, :], TMV[:, c, 1, pid:pid + 1], CS, ALU.mult, ALU.mult, [kp], [km, TMV])
                            mm(P4[:, o0:o0 + 129], kp[:], v_[:, c, 0:129], True, True, [P4], [kp, v_])
                            stt("dve", CST[:], CST[:], DECB[:, dh, c:c + 1], P4[:, o0:o0 + 129], ALU.mult, ALU.add, [CST], [CST, DECB, P4])
                            cp("act", CSTb[:], CST[:], [CSTb], [CST])
                    if seq is not None:
                        for d in range(2):
                            dh = d * 4 + h
                            dma("pool", o_ml_C[seq, l, dh, :, :], CSTs[d][:, 0:128], W=[o_ml_C], R=[CSTs[d]])
                            dma("pool", o_ml_n[seq, l, dh, :].rearrange("(p o) -> p o", o=1), CSTs[d][:, 128:129], W=[o_ml_n], R=[CSTs[d]])
                S.barrier()

        def ml_finish(g):
            with contextlib.ExitStack() as s2:
                h0 = [sb([128, 512], F32, s2, "h0") for _ in range(2)]
                h1 = [sb([128, 512], F32, s2, "h1") for _ in range(2)]
                sqj = sb([128, 128], F32, s2, "sqj")
                ss = sb([128, 4], F32, s2, "ss")
                mo = [sb([128, 4, 128], BF16, s2, "mo") for _ in range(2)]
                ob = [sb([128, 4, 128], BF16, s2, "ob") for _ in range(2)]
                for t8 in range(8):
                    a, b_, m_, o_ = h0[t8 % 2], h1[t8 % 2], mo[t8 % 2], ob[t8 % 2]
                    tsl = slice(t8 * 128, (t8 + 1) * 128)
                    dma("sp", a[:], sc_hml[g][0, tsl, :], W=[a], R=[sc_hml[g]])
                    dma("sp", b_[:], sc_hml[g][1, tsl, :], W=[b_], R=[sc_hml[g]])
                    dma("sp", m_[:], sc_ml_o[g][:, tsl].rearrange("(h p) t -> p h t", p=128), W=[m_], R=[sc_ml_o[g]])
                    tt("dve", a[:], a[:], b_[:], ALU.add, [a], [a, b_])
                    mset("dve", ss[:], 0.0, [ss])
                    for h in range(4):
                        act(sqj[:], a[:, h * 128:(h + 1) * 128], AF.Square, [sqj, ss], [a], accum=ss[:, h:h + 1])
                    rstd_from(ss[:], ss[:], 128.0, [ss], [ss, epsb])
                    for h in range(4):
                        ts("dve", a[:, h * 128:(h + 1) * 128], a[:, h * 128:(h + 1) * 128], ss[:, h:h + 1], None, ALU.mult, None, [a], [a, ss])
                    tt("dve", a[:], a[:], mlnw[:], ALU.mult, [a], [a, mlnw])
                    p = PS[t8 % 2]
                    for h in range(4):
                        op("pe", lambda e, p=p, h=h, a=a: e.transpose(p[:, h * 128:(h + 1) * 128], a[:, h * 128:(h + 1) * 128], ident[:]), W=[p], R=[a, ident])
                    tt("dve", o_[:], p[:].rearrange("p (h t) -> p h t", h=4), m_[:], ALU.mult, [o_], [p, m_])
                    dma("pool", sc_o[g][512:1024, tsl].rearrange("(h p) t -> p h t", p=128), o_[:], W=[sc_o[g]], R=[o_])
                S.barrier()

        zero_m = sb([64, 1], F32, st, "zero_m")
        mset("dve", zero_m[:], 0.0, [zero_m])

        def ml_prompt(l):
            for s in range(4):
                ml_scan(l, 0, s * 256, 256, zero_m, None, s, None)
            ml_finish(0)

        def ml_sample(l):
            T = 1024
            with contextlib.ExitStack() as s1:
                m_in = sb([64, 1], F32, s1, "m_in")
                c_in = sb([128, 8, 129], F32, s1, "c_in")
                with contextlib.ExitStack() as s2:
                    B, A = ml_gates(1, 0, T, s2)
                    G0 = sb([64, T], F32, s2, "G0")
                    neg = sb([64, 1], F32, s2, "neg")
                    mset("dve", neg[:], -1e30, [neg])
                    cummax(G0, A, neg, T, [neg])
                    GF = sb([64, 2], F32, s2, "GF")
                    cp("dve", GF[0:32, 0:1], G0[0:32, T - 1:T], [GF], [G0]); cp("dve", GF[32:64, 0:1], G0[32:64, 0:1], [GF], [G0])
                    cp("dve", GF[0:32, 1:2], B[0:32, T - 1:T], [GF], [B]); cp("dve", GF[32:64, 1:2], B[32:64, 0:1], [GF], [B])
                    WA = sb([64, T], F32, s2, "WA")
                    ts("dve", WA[:], A[:], GF[:, 0:1], None, ALU.subtract, None, [WA], [A, GF])
                    act(WA[:], WA[:], AF.Exp, [WA], [WA])
                    WT = sb([128, 8, 64], F32, s2, "WT")
                    for c in range(8):
                        p = PS[6 + c % 2]
                        op("pe", lambda e, p=p, c=c: e.transpose(p[:, 0:64], WA[:, c * 128:(c + 1) * 128], ident[0:64, 0:64]), W=[p], R=[WA, ident])
                        cp("act", WT[:, c, :], p[:, 0:64], [WT], [p])
                    ktm = [sb([128, 8, 128], BF16, s2, "mkt") for _ in range(2)]
                    vau = [sb([128, 8, 132], BF16, s2, "mv") for _ in range(2)]
                    kp = [sb([128, 128], BF16, s2, "kp") for _ in range(2)]
                    so = [sb([128, 132], F32, s2, "so") for _ in range(2)]
                    kc_ = [0]
                    for h in range(4):
                        km, v_ = ktm[h % 2], vau[h % 2]
                        dma("sp", km[:], sc_ml_ktm[1][:, h * 128:(h + 1) * 128].rearrange("(c p) d -> p c d", p=128), W=[km], R=[sc_ml_ktm[1]])
                        mset("pool", v_[:, :, 128:129], 1.0, [v_])
                        dma("sp", v_[:, :, 0:128], sc_ml_v[1][:, h * 128:(h + 1) * 128].rearrange("(c p) d -> p c d", p=128), W=[v_], R=[sc_ml_v[1]])
                        for d in range(2):
                            dh = d * 4 + h
                            pid = PID(dh)
                            p = PS[dh % 2]
                            for c in range(8):
                                k2 = kp[kc_[0] % 2]
                                kc_[0] += 1
                                ts("dve", k2[:], km[:, c, :], WT[:, c, pid:pid + 1], CS, ALU.mult, ALU.mult, [k2], [km, WT])
                                mm(p[:, 0:129], k2[:], v_[:, c, 0:129], c == 0, c == 7, [p], [k2, v_])
                            mm(PS[2 + dh % 2][:, 0:2], selm[:, dh, :], GF[:], True, True, [PS[2 + dh % 2]], [selm, GF])
                            o_ = so[dh % 2]
                            mset("pool", o_[:, 131:132], 0.0, [o_])
                            cp("act", o_[:, 0:129], p[:, 0:129], [o_], [p])
                            cp("dve", o_[:, 129:131], PS[2 + dh % 2][:, 0:2], [o_], [PS[2 + dh % 2]])
                            dma("pool", ag_ml_in[dh * 128:(dh + 1) * 128, :], o_[:], W=[ag_ml_in], R=[o_])
                    S.cc(lambda e: e.collective_compute("AllGather", ALU.bypass, replica_groups=RG, ins=[ag_ml_in.t.opt()], outs=[ag_ml_out.t.opt()]),
                         reads=[ag_ml_in.b], writes=[ag_ml_out.b])
                    S.barrier()
                with contextlib.ExitStack() as s2:
                    sm = sb([128, 4, 8, 132], F32, s2, "sm")
                    dma("sp", sm[:], ag_ml_out.t.rearrange("(r d p) c -> p r d c", r=4, d=8), W=[sm], R=[ag_ml_out])
                    mrep = sb([128, 8], F32, s2, "mrep")
                    dma("sp", mrep[:], st_m[l:l + 1, :].partition_broadcast(128), W=[mrep])
                    dma("sp", c_in[:, :, 0:128], st_C[l].rearrange("d p e -> p d e"), W=[c_in])
                    dma("sp", c_in[:, :, 128:129], st_n[l].rearrange("d (p o) -> p d o", o=1), W=[c_in], slow=True)
                    fp_ = sb([128, 4], F32, s2, "fp"); gp_ = sb([128, 4], F32, s2, "gp"); off_ = sb([128, 1], F32, s2, "off")
                    mx = sb([128, 4], F32, s2, "mx"); e0 = sb([128, 4], F32, s2, "e0"); e1 = sb([128, 4], F32, s2, "e1")
                    for d in range(2):
                        hs = slice(d * 4, d * 4 + 4)
                        for r in (range(4) if d == 0 else range(3, -1, -1)):
                            fl = selv[:, 8 + 4 * d + r:8 + 4 * d + r + 1]
                            ts("dve", off_[:], fl, 1e30, -1e30, ALU.mult, ALU.add, [off_], [selv])
                            ts("dve", fp_[:], sm[:, r, hs, 130], fl, None, ALU.mult, None, [fp_], [sm, selv])
                            ts("dve", gp_[:], sm[:, r, hs, 129], fl, off_[:, 0:1], ALU.mult, ALU.add, [gp_], [sm, selv, off_])
                            tt("dve", mx[:], mrep[:, hs], gp_[:], ALU.max, [mx], [mrep, gp_])
                            tt("dve", e0[:], mrep[:, hs], mx[:], ALU.subtract, [e0], [mrep, mx])
                            tt("dve", e1[:], gp_[:], mx[:], ALU.subtract, [e1], [gp_, mx])
                            act(e0[:], e0[:], AF.Exp, [e0], [e0])
                            act(e1[:], e1[:], AF.Exp, [e1], [e1])
                            for h in range(4):
                                dh = d * 4 + h
                                ts("dve", c_in[:, dh, :], c_in[:, dh, :], e0[:, h:h + 1], None, ALU.mult, None, [c_in], [c_in, e0])
                                stt("dve", c_in[:, dh, :], sm[:, r, dh, 0:129], e1[:, h:h + 1], c_in[:, dh, :], ALU.mult, ALU.add, [c_in], [sm, e1, c_in])
                            tt("dve", mrep[:, hs], fp_[:], mx[:], ALU.add, [mrep], [fp_, mx])
                    md = sb([64, 8], F32, s2, "md")
                    tt("dve", md[:], mrep[0:64, :], selm[:, :, 0], ALU.mult, [md], [mrep, selm])
                    op("dve", lambda e: e.reduce_sum(m_in[:], md[:], mybir.AxisListType.X), W=[m_in], R=[md])
                    S.barrier()
                ml_scan(l, 1, 0, T, m_in, c_in, None, None)
            ml_finish(1)

        def merge_out(l, g):
            with contextlib.ExitStack() as s2:
                OT = sb([128, 12, 1024], BF16, s2, "OT")
                dma("sp", OT[:], sc_o[g][:, :].rearrange("(k p) t -> p k t", p=128), W=[OT], R=[sc_o[g]])
                mT = sb([128, 8, 1024], BF16, s2, "mT")
                gts = [sb([128, 3, 1024], BF16, s2, "gt") for _ in range(2)]
                wus = [sb([128, 3, 4, 128], BF16, s2, "wu") for _ in range(2)]
                tm_ = [sb([128, 512], F32, s2, "mtmp") for _ in range(3)]
                pk = 0
                for fc in range(8):
                    gt, wu = gts[fc % 2], wus[fc % 2]
                    dma("sp", gt[:], sc_gate[g][:, :].rearrange("(i f) t -> f i t", i=3)[fc * 128:(fc + 1) * 128], W=[gt], R=[sc_gate[g]])
                    for i in range(3):
                        S.dma("pool", lambda e, i=i, wu=wu: e.dma_start(out=wu[:, i, :, :], in_=w_up[i][l].rearrange("(k p) n -> p k n", p=128)[:, :, fc * 128:(fc + 1) * 128]), writes=[wu.b])
                    for tb in range(2):
                        tsl = slice(tb * 512, (tb + 1) * 512)
                        for i in range(3):
                            p = PS[pk % 4]
                            pk += 1
                            for kc in range(4):
                                mm(p[:], wu[:, i, kc, :], OT[:, i * 4 + kc, tsl], kc == 0, kc == 3, [p], [wu, OT])
                            tt("dve", tm_[i][:], p[:], gt[:, i, tsl], ALU.mult, [tm_[i]], [p, gt])
                        tt("pool", tm_[0][:], tm_[0][:], tm_[1][:], ALU.add, [tm_[0]], [tm_[0], tm_[1]])
                        tt("pool", mT[:, fc, tsl], tm_[0][:], tm_[2][:], ALU.add, [mT], [tm_[0], tm_[2]])
                wo = [sb([128, 8, 512], BF16, s2, "wo") for _ in range(2)]
                wv = w_out[l].rearrange("(k p) n -> p k n", p=128)
                for hf in range(2):
                    S.dma("pool", lambda e, hf=hf: e.dma_start(out=wo[hf][:], in_=wv[:, :, hf * 512:(hf + 1) * 512]), writes=[wo[hf].b])
                for fc in range(8):
                    w_ = wo[fc // 4]
                    for tb in range(2):
                        tsl = slice(tb * 512, (tb + 1) * 512)
                        p = PS[pk % 4]
                        pk += 1
                        for kc in range(8):
                            mm(p[:], w_[:, kc, (fc % 4) * 128:(fc % 4 + 1) * 128], mT[:, kc, tsl], kc == 0, kc == 7, [p], [w_, mT])
                        stt("dve", xT[g][:, fc, tsl], p[:], modv[:, 2, fc, g:g + 1], xT[g][:, fc, tsl], ALU.mult, ALU.add, [xT[g]], [p, modv, xT[g]])
                S.barrier()

        def mlp(l, g, hT):
            with contextlib.ExitStack() as s2:
                uT = sb([128, 32, 1024], BF16, s2, "uT")
                w1 = [sb([128, 8, 256], BF16, s2, "w1") for _ in range(2)]
                w2 = [sb([128, 32, 128], BF16, s2, "w2") for _ in range(2)]
                rt = [sb([128, 512], F32, s2, "rt") for _ in range(2)]
                wv1 = w_ff1[l].rearrange("(k p) n -> p k n", p=128)
                wv2 = w_ff2[l].rearrange("(k p) n -> p k n", p=128)
                pk = 0
                k = 0
                for jb in range(16):
                    w_ = w1[jb % 2]
                    S.dma("pool", lambda e, w_=w_, jb=jb: e.dma_start(out=w_[:], in_=wv1[:, :, jb * 256:(jb + 1) * 256]), writes=[w_.b])
                    for mc in range(2):
                        fi = jb * 2 + mc
                        for tb in range(2):
                            tsl = slice(tb * 512, (tb + 1) * 512)
                            p = PS[pk % 4]
                            pk += 1
                            for kc in range(8):
                                mm(p[:], w_[:, kc, mc * 128:(mc + 1) * 128], hT[:, kc, tsl], kc == 0, kc == 7, [p], [w_, hT])
                            r_ = rt[k % 2]
                            k += 1
                            ts("dve", r_[:], p[:], bff1[:, fi:fi + 1], 0.0, ALU.add, ALU.max, [r_], [p, bff1])
                            tt("pool", uT[:, fi, tsl], r_[:], r_[:], ALU.mult, [uT], [r_])
                for fc in range(8):
                    w_ = w2[fc % 2]
                    S.dma("pool", lambda e, w_=w_, fc=fc: e.dma_start(out=w_[:], in_=wv2[:, :, fc * 128:(fc + 1) * 128]), writes=[w_.b])
                    for tb in range(2):
                        tsl = slice(tb * 512, (tb + 1) * 512)
                        p = PS[pk % 4]
                        pk += 1
                        for kc in range(32):
                            mm(p[:], w_[:, kc, :], uT[:, kc, tsl], kc == 0, kc == 31, [p], [w_, uT])
                        r_ = rt[k % 2]
                        k += 1
                        ts("dve", r_[:], p[:], modv[:, 5, fc, g:g + 1], modv[:, 6, fc, g:g + 1], ALU.mult, ALU.add, [r_], [p, modv])
                        tt("pool", xT[g][:, fc, tsl], xT[g][:, fc, tsl], r_[:], ALU.add, [xT[g]], [xT[g], r_])
                S.barrier()

        for l in range(depth):
            if STOP < 1:
                break
            layer_vectors(l)
            if STOP < 2:
                break
            for g in ((1, 0) if KG != 0 else ()):
                with contextlib.ExitStack() as sg:
                    hT = sb([128, 8, 1024], BF16, sg, "hT")
                    norm_to_hT(g, hT, 0)
                    if KSUB <= 8 or 80 < KSUB < 90:
                        S.barrier(); break
                    inproj(l, g, hT)
                    S.barrier()
                    if KSUB < 14 or KG == 1:
                        break
            if KG == 0:
                with contextlib.ExitStack() as sg:
                    hT = sb([128, 8, 1024], BF16, sg, "hT")
                    norm_to_hT(0, hT, 0)
                    inproj(l, 0, hT)
                    S.barrier()
                break
            if STOP >= 3:
                da_prompt(l); na_prompt(l)
            if STOP >= 4:
                ml_prompt(l)
            if STOP >= 5:
                ml_sample(l)
            if STOP >= 6:
                da_sample(l); na_sample(l)
            if STOP < 7:
                break
            for g in (0, 1):
                merge_out(l, g)
                with contextlib.ExitStack() as sg:
                    hT = sb([128, 8, 1024], BF16, sg, "hT")
                    norm_to_hT(g, hT, 1)
                    mlp(l, g, hT)
                    S.barrier()

        with contextlib.ExitStack() as s2:
            sq = sb([128, 8, 512], BF16, s2, "sq")
            rs = sb([128, 512], F32, s2, "rs")
            yt = sb([128, 8, 512], F32, s2, "yt")
            yo = [sb([128, 1024], F32, s2, "yo") for _ in range(2)]
            k = 0
            for g in range(2):
                for tb in range(2):
                    tsl = slice(tb * 512, (tb + 1) * 512)
                    act(sq[:], xT[g][:, :, tsl], AF.Square, [sq], [xT[g]])
                    for c in range(8):
                        mm(PS[7][:], onesb[:], sq[:, c, :], c == 0, c == 7, [PS[7]], [onesb, sq])
                    rstd_from(PS[7][:], rs[:], 1024.0, [rs], [PS[7], epsb])
                    for c in range(8):
                        stt("dve", yt[:, c, :], xT[g][:, c, tsl], nrm[:, 2, c:c + 1], rs[:], ALU.mult, ALU.mult, [yt], [xT[g], nrm, rs])
                    for t4 in range(4):
                        o_ = yo[k % 2]
                        k += 1
                        for half in range(2):
                            p = PS[half + 2 * (k % 2)]
                            for c in range(4):
                                cc_ = half * 4 + c
                                op("pe", lambda e, p=p, c=c, cc_=cc_, t4=t4: e.transpose(p[:, c * 128:(c + 1) * 128], yt[:, cc_, t4 * 128:(t4 + 1) * 128], ident[:]), W=[p], R=[yt, ident])
                            cp("dve" if half == 0 else "act", o_[:, half * 512:(half + 1) * 512], p[:], [o_], [p])
                        row = tb * 512 + t4 * 128
                        dma("pool", y_out[g][row:row + 128, :], o_[:], W=[y_out[g]], R=[o_])
            S.barrier()
        S.finish()
        print("program built: n_inst=%d sems=%d" % (S.n_inst, len(S.sems)))
    return nc


def _rope_tables(qtr):
    pos = np.arange(qtr * 1024, (qtr + 1) * 1024)
    rows = (pos // 64).astype(np.float32)
    cols = (pos % 64).astype(np.float32)
    freqs = (10000.0 ** (-np.arange(0, 32, 2, dtype=np.float32) / np.float32(32))).astype(np.float32)
    cos = np.zeros((64, 1024), np.float32)
    sin = np.zeros((64, 1024), np.float32)
    for half, p in enumerate((rows, cols)):
        ang = (p[None, :] * freqs[:, None]).astype(np.float32)
        c, s = np.cos(ang).astype(np.float32), np.sin(ang).astype(np.float32)
        b = half * 32
        cos[b:b + 16] = c; cos[b + 16:b + 32] = c
        sin[b:b + 16] = -s; sin[b + 16:b + 32] = s
    return np.concatenate([cos, cos], 0), np.concatenate([sin, sin], 0)


def _consts(core):
    r = core % 4
    selv = np.zeros((16,), np.float32)
    for rr in range(4):
        selv[0 + rr] = 1.0 if rr == r - 1 else 0.0
        selv[4 + rr] = 1.0 if rr == r + 1 else 0.0
        selv[8 + rr] = 1.0 if rr < r else 0.0
        selv[12 + rr] = 1.0 if rr > r else 0.0
    selv = np.tile(selv[None, :], (128, 1))
    wm = np.full((128, 16, 512), NEG * 8, np.float32)
    for qb in range(2):
        for kt8 in range(8):
            ktile = 4 * qb + kt8
            for a in range(2):
                jp = 2 * ktile + a
                j = 16 * r - 4 + jp
                for c in range(8):
                    i = 8 * qb + c
                    R = 16 * r + i
                    ws = min(max(R - 4, 0), 56)
                    if ws <= j < ws + 8:
                        wm[a * 64:(a + 1) * 64, qb * 8 + kt8, c * 64:(c + 1) * 64] = 0.0
    cos, sin = _rope_tables(r)
    return selv, wm.reshape(128, 16 * 512), cos, sin


def _static_consts():
    kc = np.arange(64)[:, None]; qc = np.arange(64)[None, :]
    cs = np.clip(qc - 8, 0, 48)
    ok = (kc >= cs) & (kc < cs + 16)
    colmask = np.where(ok, 0.0, NEG * 8).astype(np.float32)
    colmask = np.concatenate([colmask, colmask], 0)
    selm = np.zeros((64, 8, 128), np.float32)
    for dh in range(8):
        selm[PID(dh), dh, :] = 1.0
    s = np.arange(128)[:, None]; t = np.arange(128)[None, :]
    mm = np.zeros((128, 2, 128), np.float32)
    mm[:, 0, :] = np.where(s <= t, 0.0, NEG)
    mm[:, 1, :] = np.where(s >= t, 0.0, NEG)
    return colmask, selm.reshape(64, 1024), mm.reshape(128, 256), np.eye(128, dtype=np.float32)


_CACHE = {}
RUN_DEPTH = DEPTH
import os
STOP = int(os.environ.get("KSTOP", "99"))
KSUB = int(os.environ.get("KSUB", "99"))
KG = int(os.environ.get("KG", "2"))
POOL_COMPUTE = bool(int(os.environ.get("KPOOL", "1")))
TINY = set()
if STOP <= 1:
    TINY = {"w_in", "w_up_da", "w_up_ml", "w_up_na", "w_out", "w_ff1", "w_ff2", "rpbT", "wm", "cda_k", "cda_v", "cna_k", "cna_v", "st_C"}


def kernel(**inp):
    f = lambda a: np.ascontiguousarray(np.asarray(a, dtype=np.float32))
    inp = {k: f(v) for k, v in inp.items()}
    LD = RUN_DEPTH
    if "nc" not in _CACHE:
        _CACHE["nc"] = build_program(depth=LD)
    nc = _CACHE["nc"]
    colmask, selm, mmask, ident = _static_consts()
    rpb = inp["na_rpb"]
    kc = np.arange(64)[:, None]; qc = np.arange(64)[None, :]
    dc = np.clip(kc - qc + 15, 0, 30)
    g_ = rpb[:, :, ::-1, :][:, :, :, dc]
    rpbT = np.ascontiguousarray(np.transpose(g_, (0, 3, 1, 2, 4))).reshape(DEPTH, 64, 8 * 15 * 64)
    shared = {
        "w_mod": inp["w_mod"][:LD], "b_mod": inp["b_mod"][:LD], "norm1": inp["norm1"][:LD], "w_in": inp["w_in"][:LD], "b_in": inp["b_in"][:LD],
        "da_lam": inp["da_lam"].reshape(DEPTH, 256)[:LD], "da_subln": inp["da_subln"][:LD], "ml_norm": inp["ml_norm"][:LD], "rpbT": rpbT[:LD],
        "w_up_da": inp["w_up_da"][:LD], "w_up_ml": inp["w_up_ml"][:LD], "w_up_na": inp["w_up_na"][:LD], "w_out": inp["w_out"][:LD],
        "norm2": inp["norm2"][:LD], "w_ff1": inp["w_ff1"][:LD], "b_ff1": inp["b_ff1"][:LD], "w_ff2": inp["w_ff2"][:LD], "b_ff2": inp["b_ff2"][:LD],
        "norm_f": inp["norm_f"], "ident": ident, "colmask": colmask, "selm": selm, "mmask": mmask,
    }
    in_maps = []
    for c in range(8):
        b = c // 4
        q = c % 4
        selv, wm, cos, sin = _consts(c)
        m = dict(shared)
        m.update({
            "xp": inp["x_prompt"][4 * c:4 * c + 4].reshape(1024, D),
            "xs": inp["x_sample"][b, q * 1024:(q + 1) * 1024],
            "cvec": np.stack([inp["c_ctx"], inp["c"][b]], 0),
            "cda_k": inp["cache_da_k"][b].reshape(DEPTH, 512, 512)[:LD], "cda_v": inp["cache_da_v"][b].reshape(DEPTH, 512, 512)[:LD],
            "cna_k": inp["cache_na_k"][b].reshape(DEPTH, 512, 512)[:LD], "cna_v": inp["cache_na_v"][b].reshape(DEPTH, 512, 512)[:LD],
            "st_C": inp["state_ml_C"][b].reshape(DEPTH, 8, 128, 128)[:LD], "st_n": inp["state_ml_n"][b].reshape(DEPTH, 8, 128)[:LD],
            "st_m": inp["state_ml_m"][b].reshape(DEPTH, 8)[:LD],
            "rope_cos": cos, "rope_sin": sin, "selv": selv, "wm": wm,
        })
        for k in TINY:
            m[k] = np.zeros((1, 1), np.float32)
        in_maps.append({k: np.ascontiguousarray(v) for k, v in m.items()})
    res = run_bass_kernel_spmd(nc, in_maps, core_ids=list(range(8)))
    R = res.results
    y_p = np.concatenate([R[c]["y_p"].reshape(4, 256, D) for c in range(8)], 0)
    y_s = np.stack([np.concatenate([R[b * 4 + q]["y_s"] for q in range(4)], 0) for b in range(2)], 0)

    def cat(name, shape):
        a = np.concatenate([R[c][name].reshape((4, LD) + shape) for c in range(8)], 0)
        if LD < DEPTH:
            a = np.concatenate([a, np.zeros((a.shape[0], DEPTH - LD) + shape, np.float32)], 1)
        return a
    da_k = cat("o_da_k", (256, 4, 128)); da_v = cat("o_da_v", (256, 4, 128))
    na_k = cat("o_na_k", (256, 8, 64)); na_v = cat("o_na_v", (256, 8, 64))
    ml_C = cat("o_ml_C", (2, 4, 128, 128)); ml_n = cat("o_ml_n", (2, 4, 128)); ml_m = cat("o_ml_m", (2, 4))
    return tuple(np.ascontiguousarray(a, dtype=np.float32) for a in (y_p, y_s, da_k, da_v, na_k, na_v, ml_C, ml_n, ml_m))
```

```python
import contextlib
import math
import numpy as np
import concourse.bass as bass
import concourse.mybir as mybir
from concourse.bass_utils import run_bass_kernel_spmd

F32 = mybir.dt.float32
BF16 = mybir.dt.bfloat16
AF = mybir.ActivationFunctionType
ALU = mybir.AluOpType

DEPTH = 4
D = 1024
NPROJ = 8208
EPS = 1e-6
NEG = -30000.0
EPOCH = 20000
N_DMA_SEMS = 24


class Buf:
    __slots__ = ("name", "w", "r")

    def __init__(self, name):
        self.name = name
        self.w = None
        self.r = []


class Sched:
    ENG = ("pe", "dve", "act", "pool", "sp")

    def __init__(self, nc, stack):
        self.nc = nc
        self.stack = stack
        self.eobj = {"pe": nc.tensor, "dve": nc.vector, "act": nc.scalar, "pool": nc.gpsimd, "sp": nc.sync}
        self.sems = []
        self.cur = {}
        for e in self.ENG:
            self.cur[e] = [self._new_sem("s_" + e), 0]
        self.waited = {e: {} for e in self.ENG}
        self.dma_sems = [self._new_sem("d%d" % i) for i in range(N_DMA_SEMS)]
        self.dma_val = [0] * N_DMA_SEMS
        self.dma_rr = 0
        self.n_inst = 0
        self.cc_sem = self._new_sem("cc")
        self.cc_val = 0

    def _new_sem(self, name):
        h = self.stack.enter_context(self.nc.semaphore(name + "_%d" % len(self.sems)))
        self.sems.append(h)
        return len(self.sems) - 1

    def _need(self, eng, ev, out):
        if ev is None:
            return
        si, val, src = ev
        if src == eng and eng == "pe":
            return
        if self.waited[eng].get(si, 0) >= val:
            return
        if out.get(si, 0) < val:
            out[si] = val

    def _emit_waits(self, eng, reads, writes, same_engine_war=False):
        need = {}
        for b in reads:
            self._need(eng, b.w, need)
        for b in writes:
            self._need(eng, b.w, need)
            for ev in b.r:
                if ev[2] == eng and not same_engine_war:
                    continue
                self._need(eng, ev, need)
        for si, val in need.items():
            self.waited[eng][si] = val
            self.eobj[eng].wait_ge(self.sems[si], val)

    def _mark(self, ev, reads, writes):
        for b in writes:
            b.w = ev
            b.r = []
        for b in reads:
            if b.w is not ev:
                b.r.append(ev)
            if len(b.r) > 48:
                last = {}
                for x in b.r:
                    if last.get(x[0], (0, 0, 0))[1] <= x[1]:
                        last[x[0]] = x
                b.r = list(last.values())

    def op(self, eng, fn, reads=(), writes=()):
        reads = [b for b in reads if b is not None]
        writes = [b for b in writes if b is not None]
        self._emit_waits(eng, reads, writes)
        c = self.cur[eng]
        if c[1] >= EPOCH:
            c[0] = self._new_sem("s_" + eng)
            c[1] = 0
        c[1] += 1
        si, val = c[0], c[1]
        fn(self.eobj[eng]).then_inc(self.sems[si], 1)
        self._mark((si, val, eng), reads, writes)
        self.n_inst += 1

    def dma(self, q, fn, reads=(), writes=()):
        reads = [b for b in reads if b is not None]
        writes = [b for b in writes if b is not None]
        self._emit_waits(q, reads, writes, same_engine_war=True)
        k = self.dma_rr
        self.dma_rr = (self.dma_rr + 1) % N_DMA_SEMS
        si = self.dma_sems[k]
        h = self.sems[si]
        prev = self.dma_val[k]
        if prev > 0 and self.waited[q].get(si, 0) < prev:
            self.waited[q][si] = prev
            self.eobj[q].wait_ge(h, prev)
        self.dma_val[k] = prev + 16
        fn(self.eobj[q]).then_inc(h, 16)
        ev = (si, prev + 16, "dma")
        self._mark(ev, reads, writes)
        self.n_inst += 1
        return ev

    def cc(self, fn, reads=(), writes=()):
        q = "pool"
        reads = [b for b in reads if b is not None]
        writes = [b for b in writes if b is not None]
        self._emit_waits(q, reads, writes, same_engine_war=True)
        si = self.cc_sem
        if self.cc_val > 0:
            self.wait_event(q, (si, self.cc_val, "cc"))
        self.cc_val += 1
        fn(self.eobj[q]).then_inc(self.sems[si], 1)
        ev = (si, self.cc_val, "cc")
        self._mark(ev, reads, writes)
        self.n_inst += 1
        return ev

    def wait_event(self, eng, ev):
        si, val, _ = ev
        if self.waited[eng].get(si, 0) >= val:
            return
        self.waited[eng][si] = val
        self.eobj[eng].wait_ge(self.sems[si], val)

    def all_events(self):
        evs = [(self.cur[p][0], self.cur[p][1], p) for p in self.ENG if self.cur[p][1] > 0]
        for k in range(N_DMA_SEMS):
            if self.dma_val[k] > 0:
                evs.append((self.dma_sems[k], self.dma_val[k], "dma"))
        if self.cc_val > 0:
            evs.append((self.cc_sem, self.cc_val, "cc"))
        return evs

    def barrier(self):
        evs = self.all_events()
        for e in self.ENG:
            for ev in evs:
                if ev[2] == e:
                    continue
                self.wait_event(e, ev)

    def finish(self):
        for ev in self.all_events():
            self.wait_event("sp", ev)


class TL:
    __slots__ = ("t", "b")

    def __init__(self, t, b):
        self.t = t
        self.b = b

    def __getitem__(self, k):
        return self.t[k]


def PID(dh):
    return (dh // 4) * 32 + (dh % 4)


def build_program(depth=DEPTH, dbg=False):
    nc = bass.Bass("TRN2", target_bir_lowering=False)
    LD = depth

    def din(name, shape, dt=F32):
        if name in TINY:
            shape = [1, 1]
        return nc.dram_tensor(name, list(shape), dt, kind="ExternalInput").ap()

    def dout(name, shape):
        return TL(nc.dram_tensor(name, list(shape), F32, kind="ExternalOutput").ap(), Buf(name))

    def dscr(name, shape, dt):
        return TL(nc.dram_tensor(name, list(shape), dt).ap(), Buf(name))

    xin = [din("xp", [1024, D]), din("xs", [1024, D])]
    cvec = din("cvec", [2, D])
    w_mod = din("w_mod", [LD, D, 6 * D]); b_mod = din("b_mod", [LD, 6 * D])
    norm1 = din("norm1", [LD, D]); w_in = din("w_in", [LD, D, NPROJ]); b_in = din("b_in", [LD, NPROJ])
    da_lam = din("da_lam", [LD, 256]); da_subln = din("da_subln", [LD, 128]); ml_norm = din("ml_norm", [LD, 512])
    rpbT = din("rpbT", [LD, 64, 8 * 15 * 64])
    w_up = [din("w_up_da", [LD, 512, D]), din("w_up_ml", [LD, 512, D]), din("w_up_na", [LD, 512, D])]
    w_out = din("w_out", [LD, D, D]); norm2 = din("norm2", [LD, D])
    w_ff1 = din("w_ff1", [LD, D, 4 * D]); b_ff1 = din("b_ff1", [LD, 4 * D])
    w_ff2 = din("w_ff2", [LD, 4 * D, D]); b_ff2 = din("b_ff2", [LD, D]); norm_f = din("norm_f", [D])
    cda_k = din("cda_k", [LD, 512, 512]); cda_v = din("cda_v", [LD, 512, 512])
    cna_k = din("cna_k", [LD, 512, 512]); cna_v = din("cna_v", [LD, 512, 512])
    st_C = din("st_C", [LD, 8, 128, 128]); st_n = din("st_n", [LD, 8, 128]); st_m = din("st_m", [LD, 8])
    ident_d = din("ident", [128, 128]); rope_cos = din("rope_cos", [128, 1024]); rope_sin = din("rope_sin", [128, 1024])
    selv_d = din("selv", [128, 16]); wm_d = din("wm", [128, 16 * 512]); colmask_d = din("colmask", [128, 64])
    selm_d = din("selm", [64, 8 * 128]); mmask_d = din("mmask", [128, 256])

    y_out = [dout("y_p", [1024, D]), dout("y_s", [1024, D])]
    o_da_k = dout("o_da_k", [4, LD, 256, 512]); o_da_v = dout("o_da_v", [4, LD, 256, 512])
    o_na_k = dout("o_na_k", [4, LD, 256, 512]); o_na_v = dout("o_na_v", [4, LD, 256, 512])
    o_ml_C = dout("o_ml_C", [4, LD, 8, 128, 128]); o_ml_n = dout("o_ml_n", [4, LD, 8, 128]); o_ml_m = dout("o_ml_m", [4, LD, 8])

    sc_q_da = [dscr("sc_q_da%d" % g, [512, 1024], BF16) for g in range(2)]
    sc_k_da = dscr("sc_k_da0", [512, 1024], BF16)
    sc_v_da = dscr("sc_v_da0", [1024, 512], BF16)
    sc_q_na = [dscr("sc_q_na%d" % g, [512, 1024], BF16) for g in range(2)]
    sc_k_na = [dscr("sc_k_na%d" % g, [512, 1024], BF16) for g in range(2)]
    sc_v_na = [dscr("sc_v_na%d" % g, [1024, 512], BF16) for g in range(2)]
    sc_ml_q = [dscr("sc_ml_q%d" % g, [512, 1024], BF16) for g in range(2)]
    sc_ml_k = [dscr("sc_ml_k%d" % g, [512, 1024], BF16) for g in range(2)]
    sc_ml_ktm = [dscr("sc_ml_ktm%d" % g, [1024, 512], BF16) for g in range(2)]
    sc_ml_v = [dscr("sc_ml_v%d" % g, [1024, 512], BF16) for g in range(2)]
    sc_ml_o = [dscr("sc_ml_o%d" % g, [512, 1024], BF16) for g in range(2)]
    sc_ml_g = [dscr("sc_ml_g%d" % g, [16, 1024], F32) for g in range(2)]
    sc_gate = [dscr("sc_gate%d" % g, [3072, 1024], BF16) for g in range(2)]
    sc_o = [dscr("sc_o%d" % g, [1536, 1024], BF16) for g in range(2)]
    sc_hml = [dscr("sc_hml%d" % g, [2, 1024, 512], F32) for g in range(2)]
    ag_kt_in = dscr("ag_kt_in", [512, 1024], BF16); ag_kt_out = dscr("ag_kt_out", [2048, 1024], BF16)
    agV_i = dscr("agV_i", [512, 1024], BF16); agV_o = dscr("agV_o", [2048, 1024], BF16)
    ag_h_in = dscr("ag_h_in", [512, 1024], BF16); ag_h_out = dscr("ag_h_out", [2048, 1024], BF16)
    ag_ml_in = dscr("ag_ml_in", [1024, 132], F32); ag_ml_out = dscr("ag_ml_out", [4096, 132], F32)
    RG = [[0, 1, 2, 3], [4, 5, 6, 7]]
    agVi_view = agV_i.t.rearrange("r (h c) -> (r h) c", h=2)
    agVo_view = agV_o.t.rearrange("r (h c) -> (r h) c", h=2)

    with contextlib.ExitStack() as st:
        S = Sched(nc, st)
        cnt = [0]

        def sb(shape, dt, stk, name=None):
            cnt[0] += 1
            nm = (name or "t") + "_%d" % cnt[0]
            return TL(stk.enter_context(nc.sbuf_tensor(nm, list(shape), dt)), Buf(nm))

        PS = [TL(st.enter_context(nc.psum_tensor("ps%d" % i, [128, 512], F32)), Buf("ps%d" % i)) for i in range(8)]

        def bl(x):
            return [t.b if isinstance(t, TL) else t for t in x]

        def op(eng, fn, W=(), R=()):
            if eng == "pool" and not POOL_COMPUTE:
                eng = "dve"
            S.op(eng, fn, reads=bl(R), writes=bl(W))

        def dma(q, out_ap, in_ap, W=(), R=(), slow=False):
            if slow:
                S.dma(q, lambda e: e.dma_start(out=out_ap, in_=in_ap, allow_slow_non_contiguous=True), reads=bl(R), writes=bl(W))
            else:
                S.dma(q, lambda e: e.dma_start(out=out_ap, in_=in_ap), reads=bl(R), writes=bl(W))

        def mm(out_ap, lhsT, rhs, start, stop, W, R):
            op("pe", lambda e: e.matmul(out_ap, lhsT, rhs, start=start, stop=stop), W=W, R=R)

        def act(out_ap, in_ap, func, W, R, bias=None, scale=None, accum=None):
            kw = {}
            if bias is not None:
                kw["bias"] = bias
            if scale is not None:
                kw["scale"] = scale
            if accum is not None:
                kw["accum_out"] = accum
            op("act", lambda e: e.activation(out_ap, in_ap, func, **kw), W=W, R=R)

        def tt(eng, out_ap, a, b, alu, W, R):
            op(eng, lambda e: e.tensor_tensor(out_ap, a, b, alu), W=W, R=R)

        def ts(eng, out_ap, a, s1, s2, op0, op1, W, R):
            if s2 is None:
                op(eng, lambda e: e.tensor_scalar(out_ap, a, s1, None, op0), W=W, R=R)
            else:
                op(eng, lambda e: e.tensor_scalar(out_ap, a, s1, s2, op0, op1), W=W, R=R)

        def stt(eng, out_ap, a, s, b, op0, op1, W, R):
            eng = "dve"
            op(eng, lambda e: e.scalar_tensor_tensor(out_ap, a, s, b, op0, op1), W=W, R=R)

        def cp(eng, out_ap, in_ap, W, R):
            if eng == "act":
                act(out_ap, in_ap, AF.Identity, W, R)
            else:
                op(eng, lambda e: e.tensor_copy(out_ap, in_ap), W=W, R=R)

        def mset(eng, ap, val, W):
            op(eng, lambda e: e.memset(ap, val), W=W)

        def rstd_from(ps_ap, out_ap, n, W, R):
            act(out_ap, ps_ap, AF.Ln, W, R, bias=epsb[:, 0:1], scale=1.0 / n)
            act(out_ap, out_ap, AF.Exp, W, W, scale=-0.5)

        xT = [sb([128, 8, 1024], F32, st, "xT%d" % g) for g in range(2)]
        ident = sb([128, 128], F32, st, "ident")
        identb = sb([128, 128], BF16, st, "identb")
        onesb = sb([128, 128], BF16, st, "onesb")
        epsb = sb([128, 1], F32, st, "epsb")
        selv = sb([128, 16], F32, st, "selv")
        selm = sb([64, 8, 128], F32, st, "selm")
        mmask = sb([128, 2, 128], F32, st, "mmask")
        modv = sb([128, 7, 8, 2], F32, st, "modv")
        bin_fm = sb([128, 65], F32, st, "bin_fm")
        bin_sw = sb([128, 8], F32, st, "bin_sw")
        bg16 = sb([16, 1], F32, st, "bg16")
        bff1 = sb([128, 32], F32, st, "bff1")
        bff2 = sb([128, 8], F32, st, "bff2")
        nrm = sb([128, 3, 8], F32, st, "nrm")
        subw = sb([128, 1], F32, st, "subw")
        lamt = sb([128, 4], F32, st, "lamt")
        mlnw = sb([128, 512], F32, st, "mlnw")

        dma("sp", ident[:], ident_d, W=[ident])
        dma("sp", selv[:], selv_d, W=[selv])
        dma("sp", selm[:], selm_d.rearrange("p (a b) -> p a b", a=8), W=[selm])
        dma("sp", mmask[:], mmask_d.rearrange("p (a b) -> p a b", a=2), W=[mmask])
        cp("dve", identb[:], ident[:], [identb], [ident])
        mset("dve", onesb[:], 1.0, [onesb])
        mset("dve", epsb[:], EPS, [epsb])

        def fmvec(dst_ap, W, src2d, n, stk):
            stg = sb([64, 128], F32, stk, "fmstg")
            dma("sp", stg[0:n, :], src2d, W=[stg])
            op("pe", lambda e: e.transpose(PS[7][:, 0:n], stg[0:n, :], ident[0:n, 0:n]), W=[PS[7]], R=[stg, ident])
            cp("dve", dst_ap, PS[7][:, 0:n], W, [PS[7]])

        with contextlib.ExitStack() as s2:
            xl = [sb([128, 1024], F32, s2, "xl") for _ in range(2)]
            for g in range(2):
                for t8 in range(8):
                    x_ = xl[t8 % 2]
                    dma("sp", x_[:], xin[g][t8 * 128:(t8 + 1) * 128, :], W=[x_])
                    for half in range(2):
                        p = PS[half + 2 * (t8 % 2)]
                        for c in range(4):
                            cc_ = half * 4 + c
                            op("pe", lambda e, p=p, c=c, cc_=cc_, x_=x_: e.transpose(p[:, c * 128:(c + 1) * 128], x_[:, cc_ * 128:(cc_ + 1) * 128], ident[:]),
                               W=[p], R=[x_, ident])
                        cp("dve" if half == 0 else "act", xT[g][:, half * 4:half * 4 + 4, t8 * 128:(t8 + 1) * 128],
                           p[:].rearrange("p (c t) -> p c t", c=4), [xT[g]], [p])
            fmvec(nrm[:, 2, :], [nrm], norm_f.rearrange("(c p) -> c p", p=128), 8, s2)
            S.barrier()

        def layer_vectors(l):
            with contextlib.ExitStack() as s2:
                cv = sb([2, 1024], F32, s2, "cv")
                cT = sb([128, 8, 2], F32, s2, "cT")
                dma("sp", cv[:], cvec, W=[cv])
                act(cv[:], cv[:], AF.Silu, [cv], [cv])
                for kc in range(8):
                    op("pe", lambda e, kc=kc: e.transpose(PS[6][:, kc * 2:kc * 2 + 2], cv[0:2, kc * 128:(kc + 1) * 128], ident[0:2, 0:2]),
                       W=[PS[6]], R=[cv, ident])
                cp("dve", cT[:], PS[6][:, 0:16].rearrange("p (k v) -> p k v", v=2), [cT], [PS[6]])
                if KSUB <= 1:
                    S.barrier(); return
                bm = sb([128, 48], F32, s2, "bm")
                fmvec(bm[:], [bm], b_mod[l].rearrange("(c p) -> c p", p=128), 48, s2)
                if KSUB <= 2:
                    S.barrier(); return
                mraw = sb([128, 48, 2], F32, s2, "mraw")
                wts = [sb([128, 8, 512], F32, s2, "wmod") for _ in range(2)]
                wv = w_mod[l].rearrange("(kc p) n -> p kc n", p=128)
                for blk in range(12):
                    wt = wts[blk % 2]
                    dma("sp", wt[:], wv[:, :, blk * 512:(blk + 1) * 512], W=[wt])
                    p = PS[4 + blk % 2]
                    for j in range(4):
                        for kc in range(8):
                            mm(p[:, j * 2:j * 2 + 2], wt[:, kc, j * 128:(j + 1) * 128], cT[:, kc, :], kc == 0, kc == 7, [p], [wt, cT])
                    for j in range(4):
                        jj = blk * 4 + j
                        ts("dve", mraw[:, jj, :], p[:, j * 2:j * 2 + 2], bm[:, jj:jj + 1], None, ALU.add, None, [mraw], [p, bm])
                if KSUB <= 3:
                    S.barrier(); return
                fmvec(nrm[:, 0, :], [nrm], norm1[l].rearrange("(c p) -> c p", p=128), 8, s2)
                fmvec(nrm[:, 1, :], [nrm], norm2[l].rearrange("(c p) -> c p", p=128), 8, s2)
                fmvec(bff2[:], [bff2], b_ff2[l].rearrange("(c p) -> c p", p=128), 8, s2)
                fmvec(bff1[:], [bff1], b_ff1[l].rearrange("(c p) -> c p", p=128), 32, s2)
                fmvec(bin_fm[:, 0:28], [bin_fm], b_in[l, 0:3584].rearrange("(c p) -> c p", p=128), 28, s2)
                fmvec(bin_fm[:, 28:64], [bin_fm], b_in[l, 3600:8208].rearrange("(c p) -> c p", p=128), 36, s2)
                dma("sp", bg16[:], b_in[l, 3584:3600].rearrange("(p o) -> p o", o=1), W=[bg16])
                if KSUB <= 4:
                    S.barrier(); return
                stg = sb([8, 4, 2, 16], F32, s2, "bsw")
                src = b_in[l, 0:1024].rearrange("(c b t s) -> c b t s", c=8, b=4, t=2)
                dma("sp", stg[:, :, 0, :], src[:, :, 1, :], W=[stg])
                dma("sp", stg[:, :, 1, :], src[:, :, 0, :], W=[stg])
                op("pe", lambda e: e.transpose(PS[7][:, 0:8], stg[:].rearrange("c b t s -> c (b t s)"), ident[0:8, 0:8]), W=[PS[7]], R=[stg, ident])
                cp("dve", bin_sw[:], PS[7][:, 0:8], [bin_sw], [PS[7]])
                if KSUB <= 5:
                    S.barrier(); return
                for k, (sh, sc, gg, nidx) in enumerate(((0, 8, 16, 0), (24, 32, 40, 1))):
                    for v in range(2):
                        stt("dve", modv[:, 3 * k + 0, :, v], mraw[:, sc:sc + 8, v], 1.0, nrm[:, nidx, :], ALU.add, ALU.mult, [modv], [mraw, nrm])
                        cp("dve", modv[:, 3 * k + 1, :, v], mraw[:, sh:sh + 8, v], [modv], [mraw])
                        cp("dve", modv[:, 3 * k + 2, :, v], mraw[:, gg:gg + 8, v], [modv], [mraw])
                for v in range(2):
                    tt("dve", modv[:, 6, :, v], modv[:, 5, :, v], bff2[:], ALU.mult, [modv], [modv, bff2])
                if KSUB <= 6:
                    S.barrier(); return
                lv = sb([128, 4, 64], F32, s2, "lv")
                dma("sp", lv[:].rearrange("p a b -> p (a b)"), da_lam[l:l + 1, :].partition_broadcast(128), W=[lv])
                pr = sb([128, 2, 64], F32, s2, "pr")
                sm = sb([128, 2], F32, s2, "sm")
                tt("dve", pr[:, 0, :], lv[:, 0, :], lv[:, 1, :], ALU.mult, [pr], [lv])
                tt("dve", pr[:, 1, :], lv[:, 2, :], lv[:, 3, :], ALU.mult, [pr], [lv])
                op("dve", lambda e: e.reduce_sum(sm[:], pr[:], mybir.AxisListType.X), W=[sm], R=[pr])
                act(sm[:], sm[:], AF.Exp, [sm], [sm])
                lam_init = 0.8 - 0.6 * math.exp(-0.3 * l)
                tt("dve", lamt[:, 0:1], sm[:, 0:1], sm[:, 1:2], ALU.subtract, [lamt], [sm])
                ts("dve", lamt[:, 0:1], lamt[:, 0:1], lam_init, None, ALU.add, None, [lamt], [lamt])
                ts("dve", lamt[:, 1:2], lamt[:, 0:1], -1.0, None, ALU.mult, None, [lamt], [lamt])
                if KSUB <= 7:
                    S.barrier(); return
                dma("sp", subw[:], da_subln[l].rearrange("(p o) -> p o", o=1), W=[subw])
                ts("dve", subw[:], subw[:], 1.0 - lam_init, None, ALU.mult, None, [subw], [subw])
                dma("sp", mlnw[:], ml_norm[l:l + 1, :].partition_broadcast(128), W=[mlnw])
                S.barrier()

        def norm_to_hT(g, hT, kset):
            with contextlib.ExitStack() as s2:
                sq = sb([128, 8, 512], BF16, s2, "sq")
                rs = sb([128, 512], F32, s2, "rs")
                tmp = [sb([128, 512], F32, s2, "ntmp") for _ in range(2)]
                for tb in range(2):
                    tsl = slice(tb * 512, (tb + 1) * 512)
                    act(sq[:], xT[g][:, :, tsl], AF.Square, [sq], [xT[g]])
                    if KSUB == 81:
                        continue
                    for c in range(8):
                        mm(PS[7][:], onesb[:], sq[:, c, :], c == 0, c == 7, [PS[7]], [onesb, sq])
                    if KSUB == 82:
                        continue
                    rstd_from(PS[7][:], rs[:], 1024.0, [rs], [PS[7], epsb])
                    if KSUB == 83:
                        continue
                    for c in range(8):
                        t_ = tmp[c % 2]
                        stt("dve", t_[:], xT[g][:, c, tsl], modv[:, 3 * kset, c, g:g + 1], rs[:], ALU.mult, ALU.mult, [t_], [xT[g], modv, rs])
                        if KSUB == 84:
                            continue
                        ts("pool" if c % 2 else "dve", hT[:, c, tsl], t_[:], modv[:, 3 * kset + 1, c, g:g + 1], None, ALU.add, None, [hT], [t_, modv])
                S.barrier()

        def bin_chunk(col):
            return col // 128 if col < 3584 else 28 + (col - 3600) // 128

        def proj_fm(l, hT, wts, col0, ncols, handler):
            wv = w_in[l].rearrange("(kc p) n -> p kc n", p=128)
            wt = wts[proj_fm.k % len(wts)]
            proj_fm.k += 1
            S.dma("pool", lambda e: e.dma_start(out=wt[:, :, 0:ncols], in_=wv[:, :, col0:col0 + ncols]), writes=[wt.b])
            nmc = (ncols + 127) // 128
            for mc in range(nmc):
                m = min(128, ncols - mc * 128)
                for tb in range(2):
                    p = PS[proj_fm.pk % 4]
                    proj_fm.pk += 1
                    for kc in range(8):
                        mm(p[0:m, :], wt[:, kc, mc * 128:mc * 128 + m], hT[:, kc, tb * 512:(tb + 1) * 512], kc == 0, kc == 7, [p], [wt, hT])
                    handler(mc, tb, p)
        proj_fm.k = 0
        proj_fm.pk = 0

        def proj_tm(l, hT, wts, col0, handler):
            wv = w_in[l].rearrange("(kc p) n -> p kc n", p=128)
            wt = wts[proj_fm.k % len(wts)]
            proj_fm.k += 1
            S.dma("pool", lambda e: e.dma_start(out=wt[:, :, 0:512], in_=wv[:, :, col0:col0 + 512]), writes=[wt.b])
            S.dma("pool", lambda e: e.dma_start(out=brow[:], in_=b_in[l:l + 1, col0:col0 + 512]), writes=[brow.b])
            for t8 in range(8):
                p = PS[proj_fm.pk % 4]
                proj_fm.pk += 1
                for kc in range(8):
                    mm(p[:], hT[:, kc, t8 * 128:(t8 + 1) * 128], wt[:, kc, 0:512], kc == 0, False, [p], [wt, hT])
                mm(p[:], onesb[0:1, :], brow[0:1, :], False, True, [p], [onesb, brow])
                handler(t8, p)

        brow = sb([1, 512], BF16, st, "brow")

        def inproj(l, g, hT):
            with contextlib.ExitStack() as s2:
                wts = [sb([128, 8, 512], BF16, s2, "wt") for _ in range(2)]
                wsw = sb([128, 8, 512], BF16, s2, "wsw")
                ofm = [sb([128, 1024], BF16, s2, "ofm") for _ in range(2)]
                otm = [sb([128, 512], BF16, s2, "otm") for _ in range(2)]
                otf = [sb([128, 512], F32, s2, "otf") for _ in range(2)]
                og = sb([16, 1024], F32, s2, "og")
                kk = [0, 0]
                if g == 1:
                    rc = sb([128, 1024], F32, s2, "rc"); rsn = sb([128, 1024], F32, s2, "rsn")
                    dma("sp", rc[:], rope_cos, W=[rc]); dma("sp", rsn[:], rope_sin, W=[rsn])
                    t1 = sb([128, 512], F32, s2, "rt1"); t2 = sb([128, 512], F32, s2, "rt2")

                def fm_plain(col0, func, dsts):
                    def h(mc, tb, p):
                        o = ofm[(kk[0] // 2) % 2]
                        kk[0] += 1
                        bc_ = bin_chunk(col0) + mc
                        ts("dve", o[:, tb * 512:(tb + 1) * 512], p[:], bin_fm[:, bc_:bc_ + 1], None, ALU.add, None, [o], [p, bin_fm])
                        if func != AF.Identity:
                            act(o[:, tb * 512:(tb + 1) * 512], o[:, tb * 512:(tb + 1) * 512], func, [o], [o])
                        if tb == 1:
                            for d in dsts:
                                d(mc, o)
                    proj_fm(l, hT, wts, col0, 512, h)

                def to_rows(scr, row0):
                    return lambda mc, o: dma("pool", scr[row0 + mc * 128:row0 + (mc + 1) * 128, :], o[:], W=[scr], R=[o])

                def fm_rope(col0, dsts):
                    wv = w_in[l].rearrange("(kc p) n -> p kc n", p=128)
                    wt = wts[proj_fm.k % 2]
                    proj_fm.k += 1
                    S.dma("pool", lambda e: e.dma_start(out=wt[:], in_=wv[:, :, col0:col0 + 512]), writes=[wt.b])
                    wtv = wt[:].rearrange("p k (b t s) -> p k b t s", t=2, s=16)
                    wsv = wsw[:].rearrange("p k (b t s) -> p k b t s", t=2, s=16)
                    for kc in range(8):
                        cp("pool", wsv[:, kc, :, 0, :], wtv[:, kc, :, 1, :], [wsw], [wt])
                        cp("pool", wsv[:, kc, :, 1, :], wtv[:, kc, :, 0, :], [wsw], [wt])
                    for mc in range(4):
                        ch = col0 // 128 + mc
                        for tb in range(2):
                            tsl = slice(tb * 512, (tb + 1) * 512)
                            pa = PS[proj_fm.pk % 4]; pb = PS[(proj_fm.pk + 1) % 4]
                            proj_fm.pk += 2
                            for kc in range(8):
                                mm(pa[:], wt[:, kc, mc * 128:(mc + 1) * 128], hT[:, kc, tsl], kc == 0, kc == 7, [pa], [wt, hT])
                            for kc in range(8):
                                mm(pb[:], wsw[:, kc, mc * 128:(mc + 1) * 128], hT[:, kc, tsl], kc == 0, kc == 7, [pb], [wsw, hT])
                            o = ofm[(kk[0] // 2) % 2]
                            kk[0] += 1
                            stt("dve", t1[:], pa[:], bin_fm[:, ch:ch + 1], rc[:, tsl], ALU.add, ALU.mult, [t1], [pa, bin_fm, rc])
                            stt("dve", t2[:], pb[:], bin_sw[:, ch:ch + 1], rsn[:, tsl], ALU.add, ALU.mult, [t2], [pb, bin_sw, rsn])
                            tt("pool", o[:, tsl], t1[:], t2[:], ALU.add, [o], [t1, t2])
                            if tb == 1:
                                for d in dsts:
                                    d(mc, o)

                def tm_seg(col0, f32_dst, bf_dsts):
                    def h(t8, p):
                        if f32_dst is not None:
                            o = otf[kk[1] % 2]
                            cp("act", o[:], p[:], [o], [p])
                            f32_dst(t8, o)
                        if bf_dsts:
                            ob = otm[kk[1] % 2]
                            if f32_dst is not None:
                                cp("dve", ob[:], o[:], [ob], [o])
                            else:
                                cp("dve", ob[:], p[:], [ob], [p])
                            for d in bf_dsts:
                                d(t8, ob)
                        kk[1] += 1
                    proj_tm(l, hT, wts, col0, h)

                def tm_rows(scr, row0=0):
                    return lambda t8, o: dma("pool", scr[row0 + t8 * 128:row0 + (t8 + 1) * 128, :], o[:], W=[scr], R=[o])

                def out_rows(dst):
                    return lambda t8, o: dma("pool", dst[t8 // 2, l, (t8 % 2) * 128:(t8 % 2) * 128 + 128, :], o[:], W=[dst], R=[o])

                def halo_rows(scr):
                    def f(t8, o):
                        if t8 < 2:
                            dma("pool", scr[t8 * 128:(t8 + 1) * 128, 512:1024], o[:], W=[scr], R=[o])
                        if t8 >= 6:
                            dma("pool", scr[256 + (t8 - 6) * 128:256 + (t8 - 5) * 128, 512:1024], o[:], W=[scr], R=[o])
                    return f

                def kt_cols(scr, c0, c1, d0):
                    return lambda mc, o: dma("pool", scr[mc * 128:(mc + 1) * 128, d0:d0 + (c1 - c0)], o[:, c0:c1], W=[scr], R=[o])

                if g == 0:
                    fm_plain(0, AF.Identity, [to_rows(sc_q_da[0], 0)])
                    fm_plain(512, AF.Identity, [to_rows(sc_k_da, 0)])
                    if KSUB == 19:
                        S.barrier(); return
                    tm_seg(512, out_rows(o_da_k), [])
                    if KSUB == 20:
                        S.barrier(); return
                    tm_seg(1024, out_rows(o_da_v), [tm_rows(sc_v_da)])
                    if KSUB == 21:
                        S.barrier(); return
                else:
                    fm_rope(0, [to_rows(sc_q_da[1], 0)])
                    if KSUB <= 9:
                        S.barrier(); return
                    fm_rope(512, [kt_cols(ag_kt_in, 0, 1024, 0)])
                    tm_seg(1024, None, [lambda t8, o: dma("pool", agVi_view[t8 * 128:(t8 + 1) * 128, :], o[:], W=[agV_i], R=[o])])
                if KSUB <= 10:
                    S.barrier(); return
                fm_plain(3600, AF.Identity, [to_rows(sc_q_na[g], 0)])
                if g == 0:
                    fm_plain(4112, AF.Identity, [to_rows(sc_k_na[0], 0)])
                    tm_seg(4112, out_rows(o_na_k), [])
                    tm_seg(4624, out_rows(o_na_v), [tm_rows(sc_v_na[0])])
                else:
                    fm_plain(4112, AF.Identity, [to_rows(sc_k_na[1], 0), kt_cols(ag_h_in, 0, 256, 0), kt_cols(ag_h_in, 768, 1024, 256)])
                    tm_seg(4624, None, [tm_rows(sc_v_na[1]), halo_rows(ag_h_in)])
                    if KSUB <= 11:
                        S.barrier(); return
                    S.cc(lambda e: e.collective_compute("AllGather", ALU.bypass, replica_groups=RG, ins=[ag_h_in.t.opt()], outs=[ag_h_out.t.opt()]),
                         reads=[ag_h_in.b], writes=[ag_h_out.b])
                    S.cc(lambda e: e.collective_compute("AllGather", ALU.bypass, replica_groups=RG, ins=[ag_kt_in.t.opt()], outs=[ag_kt_out.t.opt()]),
                         reads=[ag_kt_in.b], writes=[ag_kt_out.b])
                    S.cc(lambda e: e.collective_compute("AllGather", ALU.bypass, replica_groups=RG, ins=[agV_i.t.opt()], outs=[agV_o.t.opt()]),
                         reads=[agV_i.b], writes=[agV_o.b])
                if KSUB <= 12:
                    S.barrier(); return
                fm_plain(1536, AF.Identity, [to_rows(sc_ml_q[g], 0)])
                fm_plain(2048, AF.Identity, [to_rows(sc_ml_k[g], 0)])
                fm_plain(3072, AF.Sigmoid, [to_rows(sc_ml_o[g], 0)])
                tm_seg(2048, None, [tm_rows(sc_ml_ktm[g])])
                tm_seg(2560, None, [tm_rows(sc_ml_v[g])])
                if KSUB <= 13:
                    S.barrier(); return

                def hg(mc, tb, p):
                    ts("dve", og[:, tb * 512:(tb + 1) * 512], p[0:16, :], bg16[:, 0:1], None, ALU.add, None, [og], [p, bg16])
                    if tb == 1:
                        dma("pool", sc_ml_g[g][:, :], og[:], W=[sc_ml_g[g]], R=[og])
                proj_fm(l, hT, wts, 3584, 16, hg)
                for i in range(6):
                    fm_plain(5136 + 512 * i, AF.Sigmoid, [to_rows(sc_gate[g], 512 * i)])
                S.barrier()

        def attn_block(kts, vts, qT, nq, dv, p_lo, o_ps, den_ps, e_tiles, ecnt, bias_tiles=None):
            n = len(kts)
            for i in range(n):
                sp_ = PS[ecnt[0] % 2]
                e_ = e_tiles[ecnt[0] % 2]
                ecnt[0] += 1
                bt = bias_tiles[i] if bias_tiles is not None else None
                mm(sp_[:, 0:nq], kts[i][0], qT[0], True, bt is None, [sp_], kts[i][1] + qT[1])
                if bt is not None:
                    mm(sp_[:, 0:nq], identb[:], bt[0], False, True, [sp_], [identb] + bt[1])
                act(e_[:, 0:nq], sp_[:, 0:nq], AF.Exp, [e_], [sp_], scale=0.125)
                mm(o_ps[p_lo:p_lo + dv, 0:nq], vts[i][0], e_[:, 0:nq], i == 0, i == n - 1, [o_ps], vts[i][1] + [e_])
                mm(den_ps[p_lo:p_lo + dv, 0:nq], onesb[:, 0:dv], e_[:, 0:nq], i == 0, i == n - 1, [den_ps], [onesb, e_])

        def da_finish(nq, o0, d0, o1, d1, work, out_ap, W):
            r0, r1, t0, t1, sq, rs = work
            op("dve", lambda e: e.reciprocal(r0[:, 0:nq], d0[:, 0:nq]), W=[r0], R=[d0])
            tt("dve", t0[:, 0:nq], o0[:, 0:nq], r0[:, 0:nq], ALU.mult, [t0], [o0, r0])
            op("dve", lambda e: e.reciprocal(r1[:, 0:nq], d1[:, 0:nq]), W=[r1], R=[d1])
            tt("dve", t1[:, 0:nq], o1[:, 0:nq], r1[:, 0:nq], ALU.mult, [t1], [o1, r1])
            stt("dve", t0[:, 0:nq], t1[:, 0:nq], lamt[:, 1:2], t0[:, 0:nq], ALU.mult, ALU.add, [t0], [t1, lamt, t0])
            act(sq[:, 0:nq], t0[:, 0:nq], AF.Square, [sq], [t0])
            mm(PS[6][:, 0:nq], onesb[:], sq[:, 0:nq], True, True, [PS[6]], [onesb, sq])
            rstd_from(PS[6][:, 0:nq], rs[:, 0:nq], 128.0, [rs], [PS[6], epsb])
            stt("dve", out_ap, t0[:, 0:nq], subw[:, 0:1], rs[:, 0:nq], ALU.mult, ALU.mult, W, [t0, subw, rs])

        def da_work(s2):
            return (sb([128, 512], F32, s2, "r0"), sb([128, 512], F32, s2, "r1"), sb([128, 512], F32, s2, "t0"),
                    sb([128, 512], F32, s2, "t1"), sb([128, 512], BF16, s2, "sq"), sb([128, 512], F32, s2, "rs"))

        def da_prompt(l):
            with contextlib.ExitStack() as s2:
                qt = [sb([128, 1024], BF16, s2, "qt") for _ in range(2)]
                kt = [sb([128, 1024], BF16, s2, "kt") for _ in range(2)]
                vt = [sb([128, 8, 128], BF16, s2, "vt") for _ in range(2)]
                et = [sb([128, 512], BF16, s2, "et") for _ in range(2)]
                ot = [sb([128, 1024], BF16, s2, "ot") for _ in range(2)]
                work = da_work(s2)
                ecnt = [0]
                for h in range(4):
                    q_, k_, v_, o_ = qt[h % 2], kt[h % 2], vt[h % 2], ot[h % 2]
                    dma("sp", q_[:], sc_q_da[0][h * 128:(h + 1) * 128, :], W=[q_], R=[sc_q_da[0]])
                    dma("sp", k_[:], sc_k_da[h * 128:(h + 1) * 128, :], W=[k_], R=[sc_k_da])
                    dma("sp", v_[:], sc_v_da[:, h * 128:(h + 1) * 128].rearrange("(t p) d -> p t d", p=128), W=[v_], R=[sc_v_da])
                    for s in range(4):
                        for j in range(2):
                            kts = [(k_[j * 64:(j + 1) * 64, s * 256 + i * 128:s * 256 + (i + 1) * 128], [k_]) for i in range(2)]
                            vts = [(v_[:, s * 2 + i, :], [v_]) for i in range(2)]
                            attn_block(kts, vts, (q_[j * 64:(j + 1) * 64, s * 256:(s + 1) * 256], [q_]), 256, 128, 0,
                                       PS[2 + 2 * j], PS[3 + 2 * j], et, ecnt)
                        da_finish(256, PS[2], PS[3], PS[4], PS[5], work, o_[:, s * 256:(s + 1) * 256], [o_])
                    dma("pool", sc_o[0][h * 128:(h + 1) * 128, :], o_[:], W=[sc_o[0]], R=[o_])
                S.barrier()

        def load_ctx_kT(l, src, dst, s2):
            stg = [sb([128, 512], F32, s2, "cstg") for _ in range(2)]
            for t4 in range(4):
                g_ = stg[t4 % 2]
                dma("sp", g_[:], src[l, t4 * 128:(t4 + 1) * 128, :], W=[g_])
                p = PS[t4 % 2]
                for c in range(4):
                    op("pe", lambda e, p=p, c=c, g_=g_: e.transpose(p[:, c * 128:(c + 1) * 128], g_[:, c * 128:(c + 1) * 128], ident[:]), W=[p], R=[g_, ident])
                cp("dve", dst[:, :, t4 * 128:(t4 + 1) * 128], p[:].rearrange("p (c t) -> p c t", c=4), [dst], [p])

        def da_sample(l):
            with contextlib.ExitStack() as s2:
                ckt = sb([128, 4, 512], BF16, s2, "ckt")
                cv = sb([128, 4, 512], BF16, s2, "cvt")
                load_ctx_kT(l, cda_k, ckt, s2)
                S.dma("pool", lambda e: e.dma_start(out=cv[:], in_=cda_v[l].rearrange("(t p) d -> p t d", p=128)), writes=[cv.b])
                qt = [sb([128, 1024], BF16, s2, "qt") for _ in range(2)]
                kt = [sb([128, 4096], BF16, s2, "kt") for _ in range(2)]
                vt = [sb([128, 32, 128], BF16, s2, "vt") for _ in range(2)]
                et = [sb([128, 512], BF16, s2, "et") for _ in range(2)]
                ot = [sb([128, 1024], BF16, s2, "ot") for _ in range(2)]
                work = da_work(s2)
                ecnt = [0]
                kv = ag_kt_out.t.rearrange("(r f) t -> f r t", r=4)
                vv = agVo_view.rearrange("(r t) d -> r t d", r=4)
                for h in range(4):
                    q_, k_, v_, o_ = qt[h % 2], kt[h % 2], vt[h % 2], ot[h % 2]
                    dma("sp", q_[:], sc_q_da[1][h * 128:(h + 1) * 128, :], W=[q_], R=[sc_q_da[1]])
                    dma("sp", k_[:].rearrange("p (r t) -> p r t", r=4), kv[h * 128:(h + 1) * 128, :, 0:1024], W=[k_], R=[ag_kt_out])
                    for r in range(4):
                        dma("sp", v_[:, r * 8:(r + 1) * 8, :], vv[r, 0:1024, h * 128:(h + 1) * 128].rearrange("(t p) d -> p t d", p=128), W=[v_], R=[agV_o])
                    for qb in range(2):
                        for j in range(2):
                            kts = [(k_[j * 64:(j + 1) * 64, i * 128:(i + 1) * 128], [k_]) for i in range(32)]
                            kts += [(ckt[j * 64:(j + 1) * 64, h, i * 128:(i + 1) * 128], [ckt]) for i in range(4)]
                            vts = [(v_[:, i, :], [v_]) for i in range(32)] + [(cv[:, i, h * 128:(h + 1) * 128], [cv]) for i in range(4)]
                            attn_block(kts, vts, (q_[j * 64:(j + 1) * 64, qb * 512:(qb + 1) * 512], [q_]), 512, 128, 0,
                                       PS[2 + 2 * j], PS[3 + 2 * j], et, ecnt)
                        da_finish(512, PS[2], PS[3], PS[4], PS[5], work, o_[:, qb * 512:(qb + 1) * 512], [o_])
                    dma("pool", sc_o[1][h * 128:(h + 1) * 128, :], o_[:], W=[sc_o[1]], R=[o_])
                S.barrier()

        def na_finish(nq, o_ps, d_ps, rr, out_ap, W):
            op("dve", lambda e: e.reciprocal(rr[:, 0:nq], d_ps[:, 0:nq]), W=[rr], R=[d_ps])
            tt("dve", out_ap, o_ps[:, 0:nq], rr[:, 0:nq], ALU.mult, W, [o_ps, rr])

        def na_prompt(l):
            with contextlib.ExitStack() as s2:
                qt = [sb([128, 1024], BF16, s2, "qt") for _ in range(2)]
                kt = [sb([128, 1024], BF16, s2, "kt") for _ in range(2)]
                vt = sb([128, 8, 512], BF16, s2, "vt")
                et = [sb([128, 512], BF16, s2, "et") for _ in range(2)]
                ot = [sb([128, 1024], BF16, s2, "ot") for _ in range(2)]
                rr = sb([128, 512], F32, s2, "rr")
                ecnt = [0]
                dma("sp", vt[:], sc_v_na[0][:, :].rearrange("(t p) d -> p t d", p=128), W=[vt], R=[sc_v_na[0]])
                for c in range(4):
                    q_, k_, o_ = qt[c % 2], kt[c % 2], ot[c % 2]
                    dma("sp", q_[:], sc_q_na[0][c * 128:(c + 1) * 128, :], W=[q_], R=[sc_q_na[0]])
                    dma("sp", k_[:], sc_k_na[0][c * 128:(c + 1) * 128, :], W=[k_], R=[sc_k_na[0]])
                    for s in range(4):
                        for hh in range(2):
                            hd = 2 * c + hh
                            kts = [(k_[hh * 64:(hh + 1) * 64, s * 256 + i * 128:s * 256 + (i + 1) * 128], [k_]) for i in range(2)]
                            vts = [(vt[:, s * 2 + i, hd * 64:(hd + 1) * 64], [vt]) for i in range(2)]
                            attn_block(kts, vts, (q_[hh * 64:(hh + 1) * 64, s * 256:(s + 1) * 256], [q_]), 256, 64, hh * 64,
                                       PS[2], PS[3], et, ecnt)
                        na_finish(256, PS[2], PS[3], rr, o_[:, s * 256:(s + 1) * 256], [o_])
                    dma("pool", sc_o[0][1024 + c * 128:1024 + (c + 1) * 128, :], o_[:], W=[sc_o[0]], R=[o_])
                S.barrier()

        def sel_combine(dst_ap, W, src4, off, stk_tmp):
            ts("dve", dst_ap, src4[0], selv[:, off:off + 1], None, ALU.mult, None, W, [stk_tmp, selv])
            for r in range(1, 4):
                stt("dve", dst_ap, src4[r], selv[:, off + r:off + r + 1], dst_ap, ALU.mult, ALU.add, W, [stk_tmp, selv] + list(W))

        def na_sample(l):
            with contextlib.ExitStack() as s2:
                ckt = sb([128, 4, 512], BF16, s2, "ckt")
                cv = sb([128, 4, 512], BF16, s2, "cvt")
                load_ctx_kT(l, cna_k, ckt, s2)
                S.dma("pool", lambda e: e.dma_start(out=cv[:], in_=cna_v[l].rearrange("(t p) d -> p t d", p=128)), writes=[cv.b])
                tbl = sb([128, 8, 15, 64], BF16, s2, "tbl")
                wmt = sb([128, 16, 512], BF16, s2, "wmt")
                S.dma("pool", lambda e: e.dma_start(out=wmt[:], in_=wm_d.rearrange("p (a b) -> p a b", a=16)), writes=[wmt.b])
                with contextlib.ExitStack() as s3:
                    rp = sb([128, 8 * 15, 64], F32, s3, "rp")
                    cm = sb([128, 64], F32, s3, "cm")
                    dma("sp", rp[0:64], rpbT[l].rearrange("p (a b) -> p a b", b=64), W=[rp])
                    dma("sp", rp[64:128], rpbT[l].rearrange("p (a b) -> p a b", b=64), W=[rp])
                    dma("sp", cm[:], colmask_d, W=[cm])
                    for h in range(8):
                        stt("dve", tbl[:, h, :, :], rp[:, h * 15:(h + 1) * 15, :], 8.0, cm[:].rearrange("p (o q) -> p o q", o=1).to_broadcast([128, 15, 64]), ALU.mult, ALU.add, [tbl], [rp, cm])
                    S.barrier()
                vx = sb([128, 12, 512], BF16, s2, "vx")
                dma("sp", vx[:, 2:10, :], sc_v_na[1][:, :].rearrange("(t p) d -> p t d", p=128), W=[vx], R=[sc_v_na[1]])
                with contextlib.ExitStack() as s3:
                    h4 = sb([128, 4, 2, 512], BF16, s3, "h4")
                    vv = ag_h_out.t.rearrange("(r i) c -> r i c", r=4)
                    for part, (row0, off, t0) in enumerate(((256, 0, 0), (0, 4, 10))):
                        for r in range(4):
                            dma("sp", h4[:, r, :, :], vv[r, row0:row0 + 256, 512:1024].rearrange("(t p) d -> p t d", p=128), W=[h4], R=[ag_h_out])
                        sel_combine(vx[:, t0:t0 + 2, :], [vx], [h4[:, r, :, :] for r in range(4)], off, h4)
                    S.barrier()
                qt = [sb([128, 1024], BF16, s2, "qt") for _ in range(2)]
                kx = [sb([128, 1536], BF16, s2, "kx") for _ in range(2)]
                k4 = sb([128, 4, 512], BF16, s2, "k4")
                et = [sb([128, 512], BF16, s2, "et") for _ in range(2)]
                ot = [sb([128, 1024], BF16, s2, "ot") for _ in range(2)]
                bts = [sb([128, 512], BF16, s2, "bt") for _ in range(8)]
                rr = sb([128, 512], F32, s2, "rr")
                ecnt = [0]
                bcnt = [0]
                kv = ag_h_out.t.rearrange("(r f) t -> f r t", r=4)
                for c in range(4):
                    q_, k_, o_ = qt[c % 2], kx[c % 2], ot[c % 2]
                    dma("sp", q_[:], sc_q_na[1][c * 128:(c + 1) * 128, :], W=[q_], R=[sc_q_na[1]])
                    dma("sp", k_[:, 256:1280], sc_k_na[1][c * 128:(c + 1) * 128, :], W=[k_], R=[sc_k_na[1]])
                    dma("sp", k4[:], kv[c * 128:(c + 1) * 128, :, 0:512], W=[k4], R=[ag_h_out])
                    sel_combine(k_[:, 0:256], [k_], [k4[:, r, 256:512] for r in range(4)], 0, k4)
                    sel_combine(k_[:, 1280:1536], [k_], [k4[:, r, 0:256] for r in range(4)], 4, k4)
                    for qb in range(2):
                        for hh in range(2):
                            hd = 2 * c + hh
                            kts, vts, bl_ = [], [], []
                            for kt8 in range(8):
                                ktile = 4 * qb + kt8
                                bt = bts[bcnt[0] % 8]
                                bcnt[0] += 1
                                cp("pool", bt[:], wmt[:, qb * 8 + kt8, :], [bt], [wmt])
                                for a in range(2):
                                    jp = 2 * ktile + a
                                    ilo = max(jp - 11, 8 * qb); ihi = min(jp + 3, 8 * qb + 7)
                                    if ilo > ihi:
                                        continue
                                    clo, chi = ilo - 8 * qb, ihi - 8 * qb
                                    tlo = ilo - jp + 11
                                    n_ = chi - clo + 1
                                    tt("pool", bt[a * 64:(a + 1) * 64, clo * 64:(chi + 1) * 64].rearrange("p (n q) -> p n q", q=64),
                                       bt[a * 64:(a + 1) * 64, clo * 64:(chi + 1) * 64].rearrange("p (n q) -> p n q", q=64),
                                       tbl[a * 64:(a + 1) * 64, hd, tlo:tlo + n_, :], ALU.add, [bt], [bt, tbl])
                                kts.append((k_[hh * 64:(hh + 1) * 64, ktile * 128:(ktile + 1) * 128], [k_]))
                                vts.append((vx[:, ktile, hd * 64:(hd + 1) * 64], [vx]))
                                bl_.append((bt[:], [bt]))
                            for i in range(4):
                                kts.append((ckt[hh * 64:(hh + 1) * 64, c, i * 128:(i + 1) * 128], [ckt]))
                                vts.append((cv[:, i, hd * 64:(hd + 1) * 64], [cv]))
                                bl_.append(None)
                            attn_block(kts, vts, (q_[hh * 64:(hh + 1) * 64, qb * 512:(qb + 1) * 512], [q_]), 512, 64, hh * 64,
                                       PS[2], PS[3], et, ecnt, bias_tiles=bl_)
                        na_finish(512, PS[2], PS[3], rr, o_[:, qb * 512:(qb + 1) * 512], [o_])
                    dma("pool", sc_o[1][1024 + c * 128:1024 + (c + 1) * 128, :], o_[:], W=[sc_o[1]], R=[o_])
                S.barrier()

        CS = 128.0 ** -0.5

        def ml_gates(g, tok0, T, s2):
            IG = sb([64, T], F32, s2, "IG"); FP = sb([64, T], F32, s2, "FP"); B = sb([64, T], F32, s2, "B"); A = sb([64, T], F32, s2, "A")
            mset("pool", IG[:], 0.0, [IG]); mset("pool", FP[:], 0.0, [FP])
            for d in range(2):
                dma("sp", IG[d * 32:d * 32 + 4, :], sc_ml_g[g][d * 8:d * 8 + 4, tok0:tok0 + T], W=[IG], R=[sc_ml_g[g]])
                dma("sp", FP[d * 32:d * 32 + 4, :], sc_ml_g[g][d * 8 + 4:d * 8 + 8, tok0:tok0 + T], W=[FP], R=[sc_ml_g[g]])
            act(FP[:], FP[:], AF.Exp, [FP], [FP], scale=-1.0)
            ts("dve", FP[:], FP[:], 1.0, None, ALU.add, None, [FP], [FP])
            act(FP[:], FP[:], AF.Ln, [FP], [FP])
            ts("dve", FP[:], FP[:], -1.0, None, ALU.mult, None, [FP], [FP])
            op("dve", lambda e: e.tensor_tensor_scan(B[0:32, :], FP[0:32, :], FP[0:32, :], 0.0, ALU.add, ALU.bypass), W=[B], R=[FP])
            op("dve", lambda e: e.tensor_tensor_scan(B[32:64, T - 1::-1], FP[32:64, T - 1::-1], FP[32:64, T - 1::-1], 0.0, ALU.add, ALU.bypass), W=[B], R=[FP])
            tt("dve", A[:], IG[:], B[:], ALU.subtract, [A], [IG, B])
            return B, A

        def cummax(G, A, init_ap, T, R):
            op("dve", lambda e: e.tensor_tensor_scan(G[0:32, :], A[0:32, :], A[0:32, :], init_ap[0:32, :], ALU.max, ALU.bypass), W=[G], R=[A] + R)
            op("dve", lambda e: e.tensor_tensor_scan(G[32:64, T - 1::-1], A[32:64, T - 1::-1], A[32:64, T - 1::-1], init_ap[32:64, :], ALU.max, ALU.bypass), W=[G], R=[A] + R)

        def ends(dst, src, T, nch, W, R):
            v = src.rearrange("p (c t) -> p c t", t=128)
            cp("dve", dst[0:32, :], v[0:32, :, 127], W, R)
            cp("dve", dst[32:64, :], v[32:64, :, 0], W, R)

        def ml_scan(l, g, tok0, T, m_in, c_in, seq, s2o):
            nch = T // 128
            with contextlib.ExitStack() as s2:
                B, A = ml_gates(g, tok0, T, s2)
                G = sb([64, T], F32, s2, "G")
                cummax(G, A, m_in, T, [m_in])
                VEC = sb([64, 4, T], F32, s2, "VEC")
                NG = sb([64, T], F32, s2, "NG")
                Ge = sb([64, nch], F32, s2, "Ge"); Gp = sb([64, nch], F32, s2, "Gp"); DEC = sb([64, nch], F32, s2, "DEC")
                Mt = sb([64, T], F32, s2, "Mt")
                cp("dve", VEC[:, 0, :], A[:], [VEC], [A])
                ts("dve", NG[:], G[:], -1.0, None, ALU.mult, None, [NG], [G])
                tt("dve", Mt[:], B[:], G[:], ALU.add, [Mt], [B, G])
                act(VEC[:, 3, :], Mt[:], AF.Exp, [VEC], [Mt], scale=-1.0)
                ends(Ge[:], G[:], T, nch, [Ge], [G])
                cp("dve", Gp[0:32, 0:1], m_in[0:32, :], [Gp], [m_in])
                cp("dve", Gp[32:64, nch - 1:nch], m_in[32:64, :], [Gp], [m_in])
                if nch > 1:
                    cp("dve", Gp[0:32, 1:nch], Ge[0:32, 0:nch - 1], [Gp], [Ge])
                    cp("dve", Gp[32:64, 0:nch - 1], Ge[32:64, 1:nch], [Gp], [Ge])
                tt("dve", DEC[:], Gp[:], Ge[:], ALU.subtract, [DEC], [Gp, Ge])
                act(DEC[:], DEC[:], AF.Exp, [DEC], [DEC])
                Gv = G[:].rearrange("p (c t) -> p c t", t=128)
                Av = A[:].rearrange("p (c t) -> p c t", t=128)
                tt("dve", VEC[:, 2, :].rearrange("p (c t) -> p c t", t=128), Gp[:].rearrange("p (c o) -> p c o", o=1).to_broadcast([64, nch, 128]), Gv, ALU.subtract, [VEC], [Gp, G])
                tt("dve", VEC[:, 1, :].rearrange("p (c t) -> p c t", t=128), Av, Ge[:].rearrange("p (c o) -> p c o", o=1).to_broadcast([64, nch, 128]), ALU.subtract, [VEC], [A, Ge])
                act(VEC[:, 1:3, :], VEC[:, 1:3, :], AF.Exp, [VEC], [VEC])
                TMV = sb([128, nch, 4, 64], F32, s2, "TMV")
                for c in range(nch):
                    p = PS[6 + c % 2]
                    for k in range(4):
                        op("pe", lambda e, p=p, k=k, c=c: e.transpose(p[:, k * 64:(k + 1) * 64], VEC[:, k, c * 128:(c + 1) * 128], ident[0:64, 0:64]), W=[p], R=[VEC, ident])
                    cp("act", TMV[:, c, :, :], p[:, 0:256].rearrange("p (k d) -> p k d", k=4), [TMV], [p])
                DECB = sb([128, 8, nch], F32, s2, "DECB")
                for dh in range(8):
                    mm(PS[6][:, dh * nch:(dh + 1) * nch], selm[:, dh, :], DEC[:], True, True, [PS[6]], [selm, DEC])
                cp("dve", DECB[:], PS[6][:, 0:8 * nch].rearrange("p (a b) -> p a b", a=8), [DECB], [PS[6]])
                if seq is not None:
                    mf = sb([64, 1], F32, s2, "mf")
                    cp("dve", mf[0:32, :], Mt[0:32, T - 1:T], [mf], [Mt])
                    cp("dve", mf[32:64, :], Mt[32:64, 0:1], [mf], [Mt])
                    for d in range(2):
                        dma("pool", o_ml_m[seq, l, d * 4:(d + 1) * 4].rearrange("(p o) -> p o", o=1), mf[d * 32:d * 32 + 4, :], W=[o_ml_m], R=[mf])
                qts = [sb([128, T], BF16, s2, "mq") for _ in range(2)]
                kts = [sb([128, T], BF16, s2, "mk") for _ in range(2)]
                ktm = [sb([128, nch, 128], BF16, s2, "mkt") for _ in range(2)]
                vau = [sb([128, nch, 132], BF16, s2, "mv") for _ in range(2)]
                CSTs = [sb([128, 129], F32, s2, "CST") for _ in range(2)]; CSTbs = [sb([128, 129], BF16, s2, "CSTb") for _ in range(2)]
                dars = [sb([128, 128], F32, s2, "dar") for _ in range(2)]; dds = [sb([128, 128], F32, s2, "dd") for _ in range(2)]
                pTs = [sb([128, 128], BF16, s2, "pT") for _ in range(2)]
                has = [sb([128, 129], F32, s2, "ha") for _ in range(2)]; hns = [sb([128, 129], F32, s2, "hn") for _ in range(2)]
                dns = [sb([128, 2], F32, s2, "dn") for _ in range(2)]
                houts = [[sb([128, 128], F32, s2, "hout") for _ in range(2)] for _ in range(2)]
                kps = [sb([128, 128], BF16, s2, "kp") for _ in range(2)]
                PSd = [[TL(PS[k].t, Buf("psml%d_%d" % (d, k))) for k in range(5)] for d in range(2)]
                hc = [0, 0]
                for h in range(4):
                    q_, k_, km, v_ = qts[h % 2], kts[h % 2], ktm[h % 2], vau[h % 2]
                    dma("sp", q_[:], sc_ml_q[g][h * 128:(h + 1) * 128, tok0:tok0 + T], W=[q_], R=[sc_ml_q[g]])
                    dma("sp", k_[:], sc_ml_k[g][h * 128:(h + 1) * 128, tok0:tok0 + T], W=[k_], R=[sc_ml_k[g]])
                    dma("sp", km[:], sc_ml_ktm[g][tok0:tok0 + T, h * 128:(h + 1) * 128].rearrange("(c p) d -> p c d", p=128), W=[km], R=[sc_ml_ktm[g]])
                    mset("pool", v_[:, :, 128:129], 1.0, [v_])
                    dma("sp", v_[:, :, 0:128], sc_ml_v[g][tok0:tok0 + T, h * 128:(h + 1) * 128].rearrange("(c p) d -> p c d", p=128), W=[v_], R=[sc_ml_v[g]])
                    for d in range(2):
                        dh = d * 4 + h
                        if c_in is None:
                            mset("pool", CSTs[d][:], 0.0, [CSTs[d]])
                        else:
                            cp("pool", CSTs[d][:], c_in[:, dh, :], [CSTs[d]], [c_in])
                        cp("dve", CSTbs[d][:], CSTs[d][:], [CSTbs[d]], [CSTs[d]])
                    for step in range(nch):
                        for d in range(2):
                            dh = d * 4 + h
                            pid = PID(dh)
                            c = step if d == 0 else nch - 1 - step
                            csl = slice(c * 128, (c + 1) * 128)
                            o0 = d * 256
                            P0, P1, P2, P3, P4 = PSd[d]
                            CST, CSTb, dar, dd, pT, ha, hn, dn, kp = CSTs[d], CSTbs[d], dars[d], dds[d], pTs[d], has[d], hns[d], dns[d], kps[d]
                            mm(P0[:, o0:o0 + 128], k_[:, csl], q_[:, csl], True, True, [P0], [k_, q_])
                            mm(P1[:, o0:o0 + 128], selm[:, dh, :], NG[:, csl], True, True, [P1], [selm, NG])
                            stt("dve", dar[:], P1[:, o0:o0 + 128], TMV[:, c, 0, pid:pid + 1], mmask[:, d, :], ALU.add, ALU.min, [dar], [P1, TMV, mmask])
                            act(dd[:], dar[:], AF.Exp, [dd], [dar])
                            stt("dve", pT[:], P0[:, o0:o0 + 128], CS, dd[:], ALU.mult, ALU.mult, [pT], [P0, dd])
                            mm(P2[:, o0:o0 + 129], pT[:], v_[:, c, 0:129], True, True, [P2], [pT, v_])
                            mm(P3[:, o0:o0 + 129], q_[:, csl], CSTb[:], True, True, [P3], [q_, CSTb])
                            cp("act", ha[:], P2[:, o0:o0 + 129], [ha], [P2])
                            stt("dve", hn[:], P3[:, o0:o0 + 129], TMV[:, c, 2, pid:pid + 1], ha[:], ALU.mult, ALU.add, [hn], [P3, TMV, ha])
                            stt("pool", dn[:, 0:1], hn[:, 128:129], -1.0, hn[:, 128:129], ALU.mult, ALU.max, [dn], [hn])
                            ts("pool", dn[:, 0:1], dn[:, 0:1], TMV[:, c, 3, pid:pid + 1], None, ALU.max, None, [dn], [dn, TMV])
                            op("dve", lambda e, dn=dn: e.reciprocal(dn[:, 1:2], dn[:, 0:1]), W=[dn], R=[dn])
                            ho = houts[d][hc[d] % 2]
                            hc[d] += 1
                            ts("pool", ho[:], hn[:, 0:128], dn[:, 1:2], None, ALU.mult, None, [ho], [hn, dn])
                            dma("pool", sc_hml[g][d, tok0 + c * 128:tok0 + (c + 1) * 128, h * 128:(h + 1) * 128], ho[:], W=[sc_hml[g]], R=[ho])
                            ts("dve", kp[:], km[:, c, :], TMV[:, c, 1, pid:pid + 1], CS, ALU.mult, ALU.mult, [kp], [km, TMV])
                            mm(P4[:, o0:o0 + 129], kp[:], v_[:, c, 0:129], True, True, [P4], [kp, v_])
                            stt("dve", CST[:], CST[:], DECB[:, dh, c:c + 1], P4[:, o0:o0 + 129], ALU.mult, ALU.add, [CST], [CST, DECB, P4])
                            cp("act", CSTb[:], CST[:], [CSTb], [CST])
                    if seq is not None:
                        for d in range(2):
                            dh = d * 4 + h
                            dma("pool", o_ml_C[seq, l, dh, :, :], CSTs[d][:, 0:128], W=[o_ml_C], R=[CSTs[d]])
                            dma("pool", o_ml_n[seq, l, dh, :].rearrange("(p o) -> p o", o=1), CSTs[d][:, 128:129], W=[o_ml_n], R=[CSTs[d]])
                S.barrier()

        def ml_finish(g):
            with contextlib.ExitStack() as s2:
                h0 = [sb([128, 512], F32, s2, "h0") for _ in range(2)]
                h1 = [sb([128, 512], F32, s2, "h1") for _ in range(2)]
                sqj = sb([128, 128], F32, s2, "sqj")
                ss = sb([128, 4], F32, s2, "ss")
                mo = [sb([128, 4, 128], BF16, s2, "mo") for _ in range(2)]
                ob = [sb([128, 4, 128], BF16, s2, "ob") for _ in range(2)]
                for t8 in range(8):
                    a, b_, m_, o_ = h0[t8 % 2], h1[t8 % 2], mo[t8 % 2], ob[t8 % 2]
                    tsl = slice(t8 * 128, (t8 + 1) * 128)
                    dma("sp", a[:], sc_hml[g][0, tsl, :], W=[a], R=[sc_hml[g]])
                    dma("sp", b_[:], sc_hml[g][1, tsl, :], W=[b_], R=[sc_hml[g]])
                    dma("sp", m_[:], sc_ml_o[g][:, tsl].rearrange("(h p) t -> p h t", p=128), W=[m_], R=[sc_ml_o[g]])
                    tt("dve", a[:], a[:], b_[:], ALU.add, [a], [a, b_])
                    mset("dve", ss[:], 0.0, [ss])
                    for h in range(4):
                        act(sqj[:], a[:, h * 128:(h + 1) * 128], AF.Square, [sqj, ss], [a], accum=ss[:, h:h + 1])
                    rstd_from(ss[:], ss[:], 128.0, [ss], [ss, epsb])
                    for h in range(4):
                        ts("dve", a[:, h * 128:(h + 1) * 128], a[:, h * 128:(h + 1) * 128], ss[:, h:h + 1], None, ALU.mult, None, [a], [a, ss])
                    tt("dve", a[:], a[:], mlnw[:], ALU.mult, [a], [a, mlnw])
                    p = PS[t8 % 2]
                    for h in range(4):
                        op("pe", lambda e, p=p, h=h, a=a: e.transpose(p[:, h * 128:(h + 1) * 128], a[:, h * 128:(h + 1) * 128], ident[:]), W=[p], R=[a, ident])
                    tt("dve", o_[:], p[:].rearrange("p (h t) -> p h t", h=4), m_[:], ALU.mult, [o_], [p, m_])
                    dma("pool", sc_o[g][512:1024, tsl].rearrange("(h p) t -> p h t", p=128), o_[:], W=[sc_o[g]], R=[o_])
                S.barrier()

        zero_m = sb([64, 1], F32, st, "zero_m")
        mset("dve", zero_m[:], 0.0, [zero_m])

        def ml_prompt(l):
            for s in range(4):
                ml_scan(l, 0, s * 256, 256, zero_m, None, s, None)
            ml_finish(0)

        def ml_sample(l):
            T = 1024
            with contextlib.ExitStack() as s1:
                m_in = sb([64, 1], F32, s1, "m_in")
                c_in = sb([128, 8, 129], F32, s1, "c_in")
                with contextlib.ExitStack() as s2:
                    B, A = ml_gates(1, 0, T, s2)
                    G0 = sb([64, T], F32, s2, "G0")
                    neg = sb([64, 1], F32, s2, "neg")
                    mset("dve", neg[:], -1e30, [neg])
                    cummax(G0, A, neg, T, [neg])
                    GF = sb([64, 2], F32, s2, "GF")
                    cp("dve", GF[0:32, 0:1], G0[0:32, T - 1:T], [GF], [G0]); cp("dve", GF[32:64, 0:1], G0[32:64, 0:1], [GF], [G0])
                    cp("dve", GF[0:32, 1:2], B[0:32, T - 1:T], [GF], [B]); cp("dve", GF[32:64, 1:2], B[32:64, 0:1], [GF], [B])
                    WA = sb([64, T], F32, s2, "WA")
                    ts("dve", WA[:], A[:], GF[:, 0:1], None, ALU.subtract, None, [WA], [A, GF])
                    act(WA[:], WA[:], AF.Exp, [WA], [WA])
                    WT = sb([128, 8, 64], F32, s2, "WT")
                    for c in range(8):
                        p = PS[6 + c % 2]
                        op("pe", lambda e, p=p, c=c: e.transpose(p[:, 0:64], WA[:, c * 128:(c + 1) * 128], ident[0:64, 0:64]), W=[p], R=[WA, ident])
                        cp("act", WT[:, c, :], p[:, 0:64], [WT], [p])
                    ktm = [sb([128, 8, 128], BF16, s2, "mkt") for _ in range(2)]
                    vau = [sb([128, 8, 132], BF16, s2, "mv") for _ in range(2)]
                    kp = [sb([128, 128], BF16, s2, "kp") for _ in range(2)]
                    so = [sb([128, 132], F32, s2, "so") for _ in range(2)]
                    kc_ = [0]
                    for h in range(4):
                        km, v_ = ktm[h % 2], vau[h % 2]
                        dma("sp", km[:], sc_ml_ktm[1][:, h * 128:(h + 1) * 128].rearrange("(c p) d -> p c d", p=128), W=[km], R=[sc_ml_ktm[1]])
                        mset("pool", v_[:, :, 128:129], 1.0, [v_])
                        dma("sp", v_[:, :, 0:128], sc_ml_v[1][:, h * 128:(h + 1) * 128].rearrange("(c p) d -> p c d", p=128), W=[v_], R=[sc_ml_v[1]])
                        for d in range(2):
                            dh = d * 4 + h
                            pid = PID(dh)
                            p = PS[dh % 2]
                            for c in range(8):
                                k2 = kp[kc_[0] % 2]
                                kc_[0] += 1
                                ts("dve", k2[:], km[:, c, :], WT[:, c, pid:pid + 1], CS, ALU.mult, ALU.mult, [k2], [km, WT])
                                mm(p[:, 0:129], k2[:], v_[:, c, 0:129], c == 0, c == 7, [p], [k2, v_])
                            mm(PS[2 + dh % 2][:, 0:2], selm[:, dh, :], GF[:], True, True, [PS[2 + dh % 2]], [selm, GF])
                            o_ = so[dh % 2]
                            mset("pool", o_[:, 131:132], 0.0, [o_])
                            cp("act", o_[:, 0:129], p[:, 0:129], [o_], [p])
                            cp("dve", o_[:, 129:131], PS[2 + dh % 2][:, 0:2], [o_], [PS[2 + dh % 2]])
                            dma("pool", ag_ml_in[dh * 128:(dh + 1) * 128, :], o_[:], W=[ag_ml_in], R=[o_])
                    S.cc(lambda e: e.collective_compute("AllGather", ALU.bypass, replica_groups=RG, ins=[ag_ml_in.t.opt()], outs=[ag_ml_out.t.opt()]),
                         reads=[ag_ml_in.b], writes=[ag_ml_out.b])
                    S.barrier()
                with contextlib.ExitStack() as s2:
                    sm = sb([128, 4, 8, 132], F32, s2, "sm")
                    dma("sp", sm[:], ag_ml_out.t.rearrange("(r d p) c -> p r d c", r=4, d=8), W=[sm], R=[ag_ml_out])
                    mrep = sb([128, 8], F32, s2, "mrep")
                    dma("sp", mrep[:], st_m[l:l + 1, :].partition_broadcast(128), W=[mrep])
                    dma("sp", c_in[:, :, 0:128], st_C[l].rearrange("d p e -> p d e"), W=[c_in])
                    dma("sp", c_in[:, :, 128:129], st_n[l].rearrange("d (p o) -> p d o", o=1), W=[c_in], slow=True)
                    fp_ = sb([128, 4], F32, s2, "fp"); gp_ = sb([128, 4], F32, s2, "gp"); off_ = sb([128, 1], F32, s2, "off")
                    mx = sb([128, 4], F32, s2, "mx"); e0 = sb([128, 4], F32, s2, "e0"); e1 = sb([128, 4], F32, s2, "e1")
                    for d in range(2):
                        hs = slice(d * 4, d * 4 + 4)
                        for r in (range(4) if d == 0 else range(3, -1, -1)):
                            fl = selv[:, 8 + 4 * d + r:8 + 4 * d + r + 1]
                            ts("dve", off_[:], fl, 1e30, -1e30, ALU.mult, ALU.add, [off_], [selv])
                            ts("dve", fp_[:], sm[:, r, hs, 130], fl, None, ALU.mult, None, [fp_], [sm, selv])
                            ts("dve", gp_[:], sm[:, r, hs, 129], fl, off_[:, 0:1], ALU.mult, ALU.add, [gp_], [sm, selv, off_])
                            tt("dve", mx[:], mrep[:, hs], gp_[:], ALU.max, [mx], [mrep, gp_])
                            tt("dve", e0[:], mrep[:, hs], mx[:], ALU.subtract, [e0], [mrep, mx])
                            tt("dve", e1[:], gp_[:], mx[:], ALU.subtract, [e1], [gp_, mx])
                            act(e0[:], e0[:], AF.Exp, [e0], [e0])
                            act(e1[:], e1[:], AF.Exp, [e1], [e1])
                            for h in range(4):
                                dh = d * 4 + h
                                ts("dve", c_in[:, dh, :], c_in[:, dh, :], e0[:, h:h + 1], None, ALU.mult, None, [c_in], [c_in, e0])
                                stt("dve", c_in[:, dh, :], sm[:, r, dh, 0:129], e1[:, h:h + 1], c_in[:, dh, :], ALU.mult, ALU.add, [c_in], [sm, e1, c_in])
                            tt("dve", mrep[:, hs], fp_[:], mx[:], ALU.add, [mrep], [fp_, mx])
                    md = sb([64, 8], F32, s2, "md")
                    tt("dve", md[:], mrep[0:64, :], selm[:, :, 0], ALU.mult, [md], [mrep, selm])
                    op("dve", lambda e: e.reduce_sum(m_in[:], md[:], mybir.AxisListType.X), W=[m_in], R=[md])
                    S.barrier()
                ml_scan(l, 1, 0, T, m_in, c_in, None, None)
            ml_finish(1)

        def merge_out(l, g):
            with contextlib.ExitStack() as s2:
                OT = sb([128, 12, 1024], BF16, s2, "OT")
                dma("sp", OT[:], sc_o[g][:, :].rearrange("(k p) t -> p k t", p=128), W=[OT], R=[sc_o[g]])
                mT = sb([128, 8, 1024], BF16, s2, "mT")
                gts = [sb([128, 3, 1024], BF16, s2, "gt") for _ in range(2)]
                wus = [sb([128, 3, 4, 128], BF16, s2, "wu") for _ in range(2)]
                tm_ = [sb([128, 512], F32, s2, "mtmp") for _ in range(3)]
                pk = 0
                for fc in range(8):
                    gt, wu = gts[fc % 2], wus[fc % 2]
                    dma("sp", gt[:], sc_gate[g][:, :].rearrange("(i f) t -> f i t", i=3)[fc * 128:(fc + 1) * 128], W=[gt], R=[sc_gate[g]])
                    for i in range(3):
                        S.dma("pool", lambda e, i=i, wu=wu: e.dma_start(out=wu[:, i, :, :], in_=w_up[i][l].rearrange("(k p) n -> p k n", p=128)[:, :, fc * 128:(fc + 1) * 128]), writes=[wu.b])
                    for tb in range(2):
                        tsl = slice(tb * 512, (tb + 1) * 512)
                        for i in range(3):
                            p = PS[pk % 4]
                            pk += 1
                            for kc in range(4):
                                mm(p[:], wu[:, i, kc, :], OT[:, i * 4 + kc, tsl], kc == 0, kc == 3, [p], [wu, OT])
                            tt("dve", tm_[i][:], p[:], gt[:, i, tsl], ALU.mult, [tm_[i]], [p, gt])
                        tt("pool", tm_[0][:], tm_[0][:], tm_[1][:], ALU.add, [tm_[0]], [tm_[0], tm_[1]])
                        tt("pool", mT[:, fc, tsl], tm_[0][:], tm_[2][:], ALU.add, [mT], [tm_[0], tm_[2]])
                wo = [sb([128, 8, 512], BF16, s2, "wo") for _ in range(2)]
                wv = w_out[l].rearrange("(k p) n -> p k n", p=128)
                for hf in range(2):
                    S.dma("pool", lambda e, hf=hf: e.dma_start(out=wo[hf][:], in_=wv[:, :, hf * 512:(hf + 1) * 512]), writes=[wo[hf].b])
                for fc in range(8):
                    w_ = wo[fc // 4]
                    for tb in range(2):
                        tsl = slice(tb * 512, (tb + 1) * 512)
                        p = PS[pk % 4]
                        pk += 1
                        for kc in range(8):
                            mm(p[:], w_[:, kc, (fc % 4) * 128:(fc % 4 + 1) * 128], mT[:, kc, tsl], kc == 0, kc == 7, [p], [w_, mT])
                        stt("dve", xT[g][:, fc, tsl], p[:], modv[:, 2, fc, g:g + 1], xT[g][:, fc, tsl], ALU.mult, ALU.add, [xT[g]], [p, modv, xT[g]])
                S.barrier()

        def mlp(l, g, hT):
            with contextlib.ExitStack() as s2:
                uT = sb([128, 32, 1024], BF16, s2, "uT")
                w1 = [sb([128, 8, 256], BF16, s2, "w1") for _ in range(2)]
                w2 = [sb([128, 32, 128], BF16, s2, "w2") for _ in range(2)]
                rt = [sb([128, 512], F32, s2, "rt") for _ in range(2)]
                wv1 = w_ff1[l].rearrange("(k p) n -> p k n", p=128)
                wv2 = w_ff2[l].rearrange("(k p) n -> p k n", p=128)
                pk = 0
                k = 0
                for jb in range(16):
                    w_ = w1[jb % 2]
                    S.dma("pool", lambda e, w_=w_, jb=jb: e.dma_start(out=w_[:], in_=wv1[:, :, jb * 256:(jb + 1) * 256]), writes=[w_.b])
                    for mc in range(2):
                        fi = jb * 2 + mc
                        for tb in range(2):
                            tsl = slice(tb * 512, (tb + 1) * 512)
                            p = PS[pk % 4]
                            pk += 1
                            for kc in range(8):
                                mm(p[:], w_[:, kc, mc * 128:(mc + 1) * 128], hT[:, kc, tsl], kc == 0, kc == 7, [p], [w_, hT])
                            r_ = rt[k % 2]
                            k += 1
                            ts("dve", r_[:], p[:], bff1[:, fi:fi + 1], 0.0, ALU.add, ALU.max, [r_], [p, bff1])
                            tt("pool", uT[:, fi, tsl], r_[:], r_[:], ALU.mult, [uT], [r_])
                for fc in range(8):
                    w_ = w2[fc % 2]
                    S.dma("pool", lambda e, w_=w_, fc=fc: e.dma_start(out=w_[:], in_=wv2[:, :, fc * 128:(fc + 1) * 128]), writes=[w_.b])
                    for tb in range(2):
                        tsl = slice(tb * 512, (tb + 1) * 512)
                        p = PS[pk % 4]
                        pk += 1
                        for kc in range(32):
                            mm(p[:], w_[:, kc, :], uT[:, kc, tsl], kc == 0, kc == 31, [p], [w_, uT])
                        r_ = rt[k % 2]
                        k += 1
                        ts("dve", r_[:], p[:], modv[:, 5, fc, g:g + 1], modv[:, 6, fc, g:g + 1], ALU.mult, ALU.add, [r_], [p, modv])
                        tt("pool", xT[g][:, fc, tsl], xT[g][:, fc, tsl], r_[:], ALU.add, [xT[g]], [xT[g], r_])
                S.barrier()

        for l in range(depth):
            if STOP < 1:
                break
            layer_vectors(l)
            if STOP < 2:
                break
            for g in ((1, 0) if KG != 0 else ()):
                with contextlib.ExitStack() as sg:
                    hT = sb([128, 8, 1024], BF16, sg, "hT")
                    norm_to_hT(g, hT, 0)
                    if KSUB <= 8 or 80 < KSUB < 90:
                        S.barrier(); break
                    inproj(l, g, hT)
                    S.barrier()
                    if KSUB < 14 or KG == 1:
                        break
            if KG == 0:
                with contextlib.ExitStack() as sg:
                    hT = sb([128, 8, 1024], BF16, sg, "hT")
                    norm_to_hT(0, hT, 0)
                    inproj(l, 0, hT)
                    S.barrier()
                break
            if STOP >= 3:
                da_prompt(l); na_prompt(l)
            if STOP >= 4:
                ml_prompt(l)
            if STOP >= 5:
                ml_sample(l)
            if STOP >= 6:
                da_sample(l); na_sample(l)
            if STOP < 7:
                break
            for g in (0, 1):
                merge_out(l, g)
                with contextlib.ExitStack() as sg:
                    hT = sb([128, 8, 1024], BF16, sg, "hT")
                    norm_to_hT(g, hT, 1)
                    mlp(l, g, hT)
                    S.barrier()

        with contextlib.ExitStack() as s2:
            sq = sb([128, 8, 512], BF16, s2, "sq")
            rs = sb([128, 512], F32, s2, "rs")
            yt = sb([128, 8, 512], F32, s2, "yt")
            yo = [sb([128, 1024], F32, s2, "yo") for _ in range(2)]
            k = 0
            for g in range(2):
                for tb in range(2):
                    tsl = slice(tb * 512, (tb + 1) * 512)
                    act(sq[:], xT[g][:, :, tsl], AF.Square, [sq], [xT[g]])
                    for c in range(8):
                        mm(PS[7][:], onesb[:], sq[:, c, :], c == 0, c == 7, [PS[7]], [onesb, sq])
                    rstd_from(PS[7][:], rs[:], 1024.0, [rs], [PS[7], epsb])
                    for c in range(8):
                        stt("dve", yt[:, c, :], xT[g][:, c, tsl], nrm[:, 2, c:c + 1], rs[:], ALU.mult, ALU.mult, [yt], [xT[g], nrm, rs])
                    for t4 in range(4):
                        o_ = yo[k % 2]
                        k += 1
                        for half in range(2):
                            p = PS[half + 2 * (k % 2)]
                            for c in range(4):
                                cc_ = half * 4 + c
                                op("pe", lambda e, p=p, c=c, cc_=cc_, t4=t4: e.transpose(p[:, c * 128:(c + 1) * 128], yt[:, cc_, t4 * 128:(t4 + 1) * 128], ident[:]), W=[p], R=[yt, ident])
                            cp("dve" if half == 0 else "act", o_[:, half * 512:(half + 1) * 512], p[:], [o_], [p])
                        row = tb * 512 + t4 * 128
                        dma("pool", y_out[g][row:row + 128, :], o_[:], W=[y_out[g]], R=[o_])
            S.barrier()
        S.finish()
        print("program built: n_inst=%d sems=%d" % (S.n_inst, len(S.sems)))
    return nc


def _rope_tables(qtr):
    pos = np.arange(qtr * 1024, (qtr + 1) * 1024)
    rows = (pos // 64).astype(np.float32)
    cols = (pos % 64).astype(np.float32)
    freqs = (10000.0 ** (-np.arange(0, 32, 2, dtype=np.float32) / np.float32(32))).astype(np.float32)
    cos = np.zeros((64, 1024), np.float32)
    sin = np.zeros((64, 1024), np.float32)
    for half, p in enumerate((rows, cols)):
        ang = (p[None, :] * freqs[:, None]).astype(np.float32)
        c, s = np.cos(ang).astype(np.float32), np.sin(ang).astype(np.float32)
        b = half * 32
        cos[b:b + 16] = c; cos[b + 16:b + 32] = c
        sin[b:b + 16] = -s; sin[b + 16:b + 32] = s
    return np.concatenate([cos, cos], 0), np.concatenate([sin, sin], 0)


def _consts(core):
    r = core % 4
    selv = np.zeros((16,), np.float32)
    for rr in range(4):
        selv[0 + rr] = 1.0 if rr == r - 1 else 0.0
        selv[4 + rr] = 1.0 if rr == r + 1 else 0.0
        selv[8 + rr] = 1.0 if rr < r else 0.0
        selv[12 + rr] = 1.0 if rr > r else 0.0
    selv = np.tile(selv[None, :], (128, 1))
    wm = np.full((128, 16, 512), NEG * 8, np.float32)
    for qb in range(2):
        for kt8 in range(8):
            ktile = 4 * qb + kt8
            for a in range(2):
                jp = 2 * ktile + a
                j = 16 * r - 4 + jp
                for c in range(8):
                    i = 8 * qb + c
                    R = 16 * r + i
                    ws = min(max(R - 4, 0), 56)
                    if ws <= j < ws + 8:
                        wm[a * 64:(a + 1) * 64, qb * 8 + kt8, c * 64:(c + 1) * 64] = 0.0
    cos, sin = _rope_tables(r)
    return selv, wm.reshape(128, 16 * 512), cos, sin


def _static_consts():
    kc = np.arange(64)[:, None]; qc = np.arange(64)[None, :]
    cs = np.clip(qc - 8, 0, 48)
    ok = (kc >= cs) & (kc < cs + 16)
    colmask = np.where(ok, 0.0, NEG * 8).astype(np.float32)
    colmask = np.concatenate([colmask, colmask], 0)
    selm = np.zeros((64, 8, 128), np.float32)
    for dh in range(8):
        selm[PID(dh), dh, :] = 1.0
    s = np.arange(128)[:, None]; t = np.arange(128)[None, :]
    mm = np.zeros((128, 2, 128), np.float32)
    mm[:, 0, :] = np.where(s <= t, 0.0, NEG)
    mm[:, 1, :] = np.where(s >= t, 0.0, NEG)
    return colmask, selm.reshape(64, 1024), mm.reshape(128, 256), np.eye(128, dtype=np.float32)


_CACHE = {}
RUN_DEPTH = DEPTH
import os
STOP = int(os.environ.get("KSTOP", "99"))
KSUB = int(os.environ.get("KSUB", "99"))
KG = int(os.environ.get("KG", "2"))
POOL_COMPUTE = bool(int(os.environ.get("KPOOL", "0")))
TINY = set()
if STOP <= 1:
    TINY = {"w_in", "w_up_da", "w_up_ml", "w_up_na", "w_out", "w_ff1", "w_ff2", "rpbT", "wm", "cda_k", "cda_v", "cna_k", "cna_v", "st_C"}


def kernel(**inp):
    f = lambda a: np.ascontiguousarray(np.asarray(a, dtype=np.float32))
    inp = {k: f(v) for k, v in inp.items()}
    LD = RUN_DEPTH
    if "nc" not in _CACHE:
        _CACHE["nc"] = build_program(depth=LD)
    nc = _CACHE["nc"]
    colmask, selm, mmask, ident = _static_consts()
    rpb = inp["na_rpb"]
    kc = np.arange(64)[:, None]; qc = np.arange(64)[None, :]
    dc = np.clip(kc - qc + 15, 0, 30)
    g_ = rpb[:, :, ::-1, :][:, :, :, dc]
    rpbT = np.ascontiguousarray(np.transpose(g_, (0, 3, 1, 2, 4))).reshape(DEPTH, 64, 8 * 15 * 64)
    shared = {
        "w_mod": inp["w_mod"][:LD], "b_mod": inp["b_mod"][:LD], "norm1": inp["norm1"][:LD], "w_in": inp["w_in"][:LD], "b_in": inp["b_in"][:LD],
        "da_lam": inp["da_lam"].reshape(DEPTH, 256)[:LD], "da_subln": inp["da_subln"][:LD], "ml_norm": inp["ml_norm"][:LD], "rpbT": rpbT[:LD],
        "w_up_da": inp["w_up_da"][:LD], "w_up_ml": inp["w_up_ml"][:LD], "w_up_na": inp["w_up_na"][:LD], "w_out": inp["w_out"][:LD],
        "norm2": inp["norm2"][:LD], "w_ff1": inp["w_ff1"][:LD], "b_ff1": inp["b_ff1"][:LD], "w_ff2": inp["w_ff2"][:LD], "b_ff2": inp["b_ff2"][:LD],
        "norm_f": inp["norm_f"], "ident": ident, "colmask": colmask, "selm": selm, "mmask": mmask,
    }
    in_maps = []
    for c in range(8):
        b = c // 4
        q = c % 4
        selv, wm, cos, sin = _consts(c)
        m = dict(shared)
        m.update({
            "xp": inp["x_prompt"][4 * c:4 * c + 4].reshape(1024, D),
            "xs": inp["x_sample"][b, q * 1024:(q + 1) * 1024],
            "cvec": np.stack([inp["c_ctx"], inp["c"][b]], 0),
            "cda_k": inp["cache_da_k"][b].reshape(DEPTH, 512, 512)[:LD], "cda_v": inp["cache_da_v"][b].reshape(DEPTH, 512, 512)[:LD],
            "cna_k": inp["cache_na_k"][b].reshape(DEPTH, 512, 512)[:LD], "cna_v": inp["cache_na_v"][b].reshape(DEPTH, 512, 512)[:LD],
            "st_C": inp["state_ml_C"][b].reshape(DEPTH, 8, 128, 128)[:LD], "st_n": inp["state_ml_n"][b].reshape(DEPTH, 8, 128)[:LD],
            "st_m": inp["state_ml_m"][b].reshape(DEPTH, 8)[:LD],
            "rope_cos": cos, "rope_sin": sin, "selv": selv, "wm": wm,
        })
        for k in TINY:
            m[k] = np.zeros((1, 1), np.float32)
        in_maps.append({k: np.ascontiguousarray(v) for k, v in m.items()})
    res = run_bass_kernel_spmd(nc, in_maps, core_ids=list(range(8)))
    R = res.results
    y_p = np.concatenate([R[c]["y_p"].reshape(4, 256, D) for c in range(8)], 0)
    y_s = np.stack([np.concatenate([R[b * 4 + q]["y_s"] for q in range(4)], 0) for b in range(2)], 0)

    def cat(name, shape):
        a = np.concatenate([R[c][name].reshape((4, LD) + shape) for c in range(8)], 0)
        if LD < DEPTH:
            a = np.concatenate([a, np.zeros((a.shape[0], DEPTH - LD) + shape, np.float32)], 1)
        return a
    da_k = cat("o_da_k", (256, 4, 128)); da_v = cat("o_da_v", (256, 4, 128))
    na_k = cat("o_na_k", (256, 8, 64)); na_v = cat("o_na_v", (256, 8, 64))
    ml_C = cat("o_ml_C", (2, 4, 128, 128)); ml_n = cat("o_ml_n", (2, 4, 128)); ml_m = cat("o_ml_m", (2, 4))
    return tuple(np.ascontiguousarray(a, dtype=np.float32) for a in (y_p, y_s, da_k, da_v, na_k, na_v, ml_C, ml_n, ml_m))
```

```python
import contextlib
import math
import numpy as np
import concourse.bass as bass
import concourse.mybir as mybir
from concourse.bass_utils import run_bass_kernel_spmd

F32 = mybir.dt.float32
BF16 = mybir.dt.bfloat16
AF = mybir.ActivationFunctionType
ALU = mybir.AluOpType

DEPTH = 4
D = 1024
NPROJ = 8208
EPS = 1e-6
NEG = -30000.0
EPOCH = 20000
N_DMA_SEMS = 24


class Buf:
    __slots__ = ("name", "w", "r")

    def __init__(self, name):
        self.name = name
        self.w = None
        self.r = []


class Sched:
    ENG = ("pe", "dve", "act", "pool", "sp")

    def __init__(self, nc, stack):
        self.nc = nc
        self.stack = stack
        self.eobj = {"pe": nc.tensor, "dve": nc.vector, "act": nc.scalar, "pool": nc.gpsimd, "sp": nc.sync}
        self.sems = []
        self.cur = {}
        for e in self.ENG:
            self.cur[e] = [self._new_sem("s_" + e), 0]
        self.waited = {e: {} for e in self.ENG}
        self.dma_sems = [self._new_sem("d%d" % i) for i in range(N_DMA_SEMS)]
        self.dma_val = [0] * N_DMA_SEMS
        self.dma_rr = 0
        self.n_inst = 0
        self.cc_sem = self._new_sem("cc")
        self.cc_val = 0

    def _new_sem(self, name):
        h = self.stack.enter_context(self.nc.semaphore(name + "_%d" % len(self.sems)))
        self.sems.append(h)
        return len(self.sems) - 1

    def _need(self, eng, ev, out):
        if ev is None:
            return
        si, val, src = ev
        if src == eng and eng == "pe":
            return
        if self.waited[eng].get(si, 0) >= val:
            return
        if out.get(si, 0) < val:
            out[si] = val

    def _emit_waits(self, eng, reads, writes, same_engine_war=False):
        need = {}
        for b in reads:
            self._need(eng, b.w, need)
        for b in writes:
            self._need(eng, b.w, need)
            for ev in b.r:
                if ev[2] == eng and not same_engine_war:
                    continue
                self._need(eng, ev, need)
        for si, val in need.items():
            self.waited[eng][si] = val
            self.eobj[eng].wait_ge(self.sems[si], val)

    def _mark(self, ev, reads, writes):
        for b in writes:
            b.w = ev
            b.r = []
        for b in reads:
            if b.w is not ev:
                b.r.append(ev)
            if len(b.r) > 48:
                last = {}
                for x in b.r:
                    if last.get(x[0], (0, 0, 0))[1] <= x[1]:
                        last[x[0]] = x
                b.r = list(last.values())

    def op(self, eng, fn, reads=(), writes=()):
        reads = [b for b in reads if b is not None]
        writes = [b for b in writes if b is not None]
        self._emit_waits(eng, reads, writes)
        c = self.cur[eng]
        if c[1] >= EPOCH:
            c[0] = self._new_sem("s_" + eng)
            c[1] = 0
        c[1] += 1
        si, val = c[0], c[1]
        fn(self.eobj[eng]).then_inc(self.sems[si], 1)
        self._mark((si, val, eng), reads, writes)
        self.n_inst += 1

    def dma(self, q, fn, reads=(), writes=()):
        reads = [b for b in reads if b is not None]
        writes = [b for b in writes if b is not None]
        self._emit_waits(q, reads, writes, same_engine_war=True)
        k = self.dma_rr
        self.dma_rr = (self.dma_rr + 1) % N_DMA_SEMS
        si = self.dma_sems[k]
        h = self.sems[si]
        prev = self.dma_val[k]
        if prev > 0 and self.waited[q].get(si, 0) < prev:
            self.waited[q][si] = prev
            self.eobj[q].wait_ge(h, prev)
        self.dma_val[k] = prev + 16
        fn(self.eobj[q]).then_inc(h, 16)
        ev = (si, prev + 16, "dma")
        self._mark(ev, reads, writes)
        self.n_inst += 1
        return ev

    def cc(self, fn, reads=(), writes=()):
        q = "pool"
        reads = [b for b in reads if b is not None]
        writes = [b for b in writes if b is not None]
        self._emit_waits(q, reads, writes, same_engine_war=True)
        si = self.cc_sem
        if self.cc_val > 0:
            self.wait_event(q, (si, self.cc_val, "cc"))
        self.cc_val += 1
        fn(self.eobj[q]).then_inc(self.sems[si], 1)
        ev = (si, self.cc_val, "cc")
        self._mark(ev, reads, writes)
        self.n_inst += 1
        return ev

    def wait_event(self, eng, ev):
        si, val, _ = ev
        if self.waited[eng].get(si, 0) >= val:
            return
        self.waited[eng][si] = val
        self.eobj[eng].wait_ge(self.sems[si], val)

    def all_events(self):
        evs = [(self.cur[p][0], self.cur[p][1], p) for p in self.ENG if self.cur[p][1] > 0]
        for k in range(N_DMA_SEMS):
            if self.dma_val[k] > 0:
                evs.append((self.dma_sems[k], self.dma_val[k], "dma"))
        if self.cc_val > 0:
            evs.append((self.cc_sem, self.cc_val, "cc"))
        return evs

    def barrier(self):
        evs = self.all_events()
        for e in self.ENG:
            for ev in evs:
                if ev[2] == e:
                    continue
                self.wait_event(e, ev)

    def finish(self):
        for ev in self.all_events():
            self.wait_event("sp", ev)


class TL:
    __slots__ = ("t", "b")

    def __init__(self, t, b):
        self.t = t
        self.b = b

    def __getitem__(self, k):
        return self.t[k]


def PID(dh):
    return (dh // 4) * 32 + (dh % 4)


def build_program(depth=DEPTH, dbg=False):
    nc = bass.Bass("TRN2", target_bir_lowering=False)
    LD = depth

    def din(name, shape, dt=F32):
        if name in TINY:
            shape = [1, 1]
        return nc.dram_tensor(name, list(shape), dt, kind="ExternalInput").ap()

    def dout(name, shape):
        return TL(nc.dram_tensor(name, list(shape), F32, kind="ExternalOutput").ap(), Buf(name))

    def dscr(name, shape, dt):
        return TL(nc.dram_tensor(name, list(shape), dt).ap(), Buf(name))

    xin = [din("xp", [1024, D]), din("xs", [1024, D])]
    cvec = din("cvec", [2, D])
    w_mod = din("w_mod", [LD, D, 6 * D]); b_mod = din("b_mod", [LD, 6 * D])
    norm1 = din("norm1", [LD, D]); w_in = din("w_in", [LD, D, NPROJ]); b_in = din("b_in", [LD, NPROJ])
    da_lam = din("da_lam", [LD, 256]); da_subln = din("da_subln", [LD, 128]); ml_norm = din("ml_norm", [LD, 512])
    rpbT = din("rpbT", [LD, 64, 8 * 15 * 64])
    w_up = [din("w_up_da", [LD, 512, D]), din("w_up_ml", [LD, 512, D]), din("w_up_na", [LD, 512, D])]
    w_out = din("w_out", [LD, D, D]); norm2 = din("norm2", [LD, D])
    w_ff1 = din("w_ff1", [LD, D, 4 * D]); b_ff1 = din("b_ff1", [LD, 4 * D])
    w_ff2 = din("w_ff2", [LD, 4 * D, D]); b_ff2 = din("b_ff2", [LD, D]); norm_f = din("norm_f", [D])
    cda_k = din("cda_k", [LD, 512, 512]); cda_v = din("cda_v", [LD, 512, 512])
    cna_k = din("cna_k", [LD, 512, 512]); cna_v = din("cna_v", [LD, 512, 512])
    st_C = din("st_C", [LD, 8, 128, 128]); st_n = din("st_n", [LD, 8, 128]); st_m = din("st_m", [LD, 8])
    ident_d = din("ident", [128, 128]); rope_cos = din("rope_cos", [128, 1024]); rope_sin = din("rope_sin", [128, 1024])
    selv_d = din("selv", [128, 16]); wm_d = din("wm", [128, 16 * 512]); colmask_d = din("colmask", [128, 64])
    selm_d = din("selm", [64, 8 * 128]); mmask_d = din("mmask", [128, 256])

    y_out = [dout("y_p", [1024, D]), dout("y_s", [1024, D])]
    o_da_k = dout("o_da_k", [4, LD, 256, 512]); o_da_v = dout("o_da_v", [4, LD, 256, 512])
    o_na_k = dout("o_na_k", [4, LD, 256, 512]); o_na_v = dout("o_na_v", [4, LD, 256, 512])
    o_ml_C = dout("o_ml_C", [4, LD, 8, 128, 128]); o_ml_n = dout("o_ml_n", [4, LD, 8, 128]); o_ml_m = dout("o_ml_m", [4, LD, 8])

    sc_q_da = [dscr("sc_q_da%d" % g, [512, 1024], BF16) for g in range(2)]
    sc_k_da = dscr("sc_k_da0", [512, 1024], BF16)
    sc_v_da = dscr("sc_v_da0", [1024, 512], BF16)
    sc_q_na = [dscr("sc_q_na%d" % g, [512, 1024], BF16) for g in range(2)]
    sc_k_na = [dscr("sc_k_na%d" % g, [512, 1024], BF16) for g in range(2)]
    sc_v_na = [dscr("sc_v_na%d" % g, [1024, 512], BF16) for g in range(2)]
    sc_ml_q = [dscr("sc_ml_q%d" % g, [512, 1024], BF16) for g in range(2)]
    sc_ml_k = [dscr("sc_ml_k%d" % g, [512, 1024], BF16) for g in range(2)]
    sc_ml_ktm = [dscr("sc_ml_ktm%d" % g, [1024, 512], BF16) for g in range(2)]
    sc_ml_v = [dscr("sc_ml_v%d" % g, [1024, 512], BF16) for g in range(2)]
    sc_ml_o = [dscr("sc_ml_o%d" % g, [512, 1024], BF16) for g in range(2)]
    sc_ml_g = [dscr("sc_ml_g%d" % g, [16, 1024], F32) for g in range(2)]
    sc_gate = [dscr("sc_gate%d" % g, [3072, 1024], BF16) for g in range(2)]
    sc_o = [dscr("sc_o%d" % g, [1536, 1024], BF16) for g in range(2)]
    sc_hml = [dscr("sc_hml%d" % g, [2, 1024, 512], F32) for g in range(2)]
    ag_kt_in = dscr("ag_kt_in", [512, 1024], BF16); ag_kt_out = dscr("ag_kt_out", [2048, 1024], BF16)
    agV_i = dscr("agV_i", [512, 1024], BF16); agV_o = dscr("agV_o", [2048, 1024], BF16)
    ag_h_in = dscr("ag_h_in", [512, 1024], BF16); ag_h_out = dscr("ag_h_out", [2048, 1024], BF16)
    ag_ml_in = dscr("ag_ml_in", [1024, 132], F32); ag_ml_out = dscr("ag_ml_out", [4096, 132], F32)
    RG = [[0, 1, 2, 3], [4, 5, 6, 7]]
    agVi_view = agV_i.t.rearrange("r (h c) -> (r h) c", h=2)
    agVo_view = agV_o.t.rearrange("r (h c) -> (r h) c", h=2)

    with contextlib.ExitStack() as st:
        S = Sched(nc, st)
        cnt = [0]

        def sb(shape, dt, stk, name=None):
            cnt[0] += 1
            nm = (name or "t") + "_%d" % cnt[0]
            return TL(stk.enter_context(nc.sbuf_tensor(nm, list(shape), dt)), Buf(nm))

        PS = [TL(st.enter_context(nc.psum_tensor("ps%d" % i, [128, 512], F32)), Buf("ps%d" % i)) for i in range(8)]

        def bl(x):
            return [t.b if isinstance(t, TL) else t for t in x]

        def op(eng, fn, W=(), R=()):
            if eng == "pool" and not POOL_COMPUTE:
                eng = "dve"
            S.op(eng, fn, reads=bl(R), writes=bl(W))

        def dma(q, out_ap, in_ap, W=(), R=(), slow=False):
            if slow:
                S.dma(q, lambda e: e.dma_start(out=out_ap, in_=in_ap, allow_slow_non_contiguous=True), reads=bl(R), writes=bl(W))
            else:
                S.dma(q, lambda e: e.dma_start(out=out_ap, in_=in_ap), reads=bl(R), writes=bl(W))

        def mm(out_ap, lhsT, rhs, start, stop, W, R):
            op("pe", lambda e: e.matmul(out_ap, lhsT, rhs, start=start, stop=stop), W=W, R=R)

        def act(out_ap, in_ap, func, W, R, bias=None, scale=None, accum=None):
            kw = {}
            if bias is not None:
                kw["bias"] = bias
            if scale is not None:
                kw["scale"] = scale
            if accum is not None:
                kw["accum_out"] = accum
            op("act", lambda e: e.activation(out_ap, in_ap, func, **kw), W=W, R=R)

        def tt(eng, out_ap, a, b, alu, W, R):
            op(eng, lambda e: e.tensor_tensor(out_ap, a, b, alu), W=W, R=R)

        def ts(eng, out_ap, a, s1, s2, op0, op1, W, R):
            if s2 is None:
                op(eng, lambda e: e.tensor_scalar(out_ap, a, s1, None, op0), W=W, R=R)
            else:
                op(eng, lambda e: e.tensor_scalar(out_ap, a, s1, s2, op0, op1), W=W, R=R)

        def stt(eng, out_ap, a, s, b, op0, op1, W, R):
            eng = "dve"
            op(eng, lambda e: e.scalar_tensor_tensor(out_ap, a, s, b, op0, op1), W=W, R=R)

        def cp(eng, out_ap, in_ap, W, R):
            if eng == "act":
                act(out_ap, in_ap, AF.Identity, W, R)
            else:
                op(eng, lambda e: e.tensor_copy(out_ap, in_ap), W=W, R=R)

        def mset(eng, ap, val, W):
            op(eng, lambda e: e.memset(ap, val), W=W)

        def rstd_from(ps_ap, out_ap, n, W, R):
            act(out_ap, ps_ap, AF.Ln, W, R, bias=epsb[:, 0:1], scale=1.0 / n)
            act(out_ap, out_ap, AF.Exp, W, W, scale=-0.5)

        xT = [sb([128, 8, 1024], F32, st, "xT%d" % g) for g in range(2)]
        ident = sb([128, 128], F32, st, "ident")
        identb = sb([128, 128], BF16, st, "identb")
        onesb = sb([128, 128], BF16, st, "onesb")
        epsb = sb([128, 1], F32, st, "epsb")
        selv = sb([128, 16], F32, st, "selv")
        selm = sb([64, 8, 128], F32, st, "selm")
        mmask = sb([128, 2, 128], F32, st, "mmask")
        modv = sb([128, 7, 8, 2], F32, st, "modv")
        bin_fm = sb([128, 65], F32, st, "bin_fm")
        bin_sw = sb([128, 8], F32, st, "bin_sw")
        bg16 = sb([16, 1], F32, st, "bg16")
        bff1 = sb([128, 32], F32, st, "bff1")
        bff2 = sb([128, 8], F32, st, "bff2")
        nrm = sb([128, 3, 8], F32, st, "nrm")
        subw = sb([128, 1], F32, st, "subw")
        lamt = sb([128, 4], F32, st, "lamt")
        mlnw = sb([128, 512], F32, st, "mlnw")

        dma("sp", ident[:], ident_d, W=[ident])
        dma("sp", selv[:], selv_d, W=[selv])
        dma("sp", selm[:], selm_d.rearrange("p (a b) -> p a b", a=8), W=[selm])
        dma("sp", mmask[:], mmask_d.rearrange("p (a b) -> p a b", a=2), W=[mmask])
        cp("dve", identb[:], ident[:], [identb], [ident])
        mset("dve", onesb[:], 1.0, [onesb])
        mset("dve", epsb[:], EPS, [epsb])

        def fmvec(dst_ap, W, src2d, n, stk):
            stg = sb([64, 128], F32, stk, "fmstg")
            dma("sp", stg[0:n, :], src2d, W=[stg])
            op("pe", lambda e: e.transpose(PS[7][:, 0:n], stg[0:n, :], ident[0:n, 0:n]), W=[PS[7]], R=[stg, ident])
            cp("dve", dst_ap, PS[7][:, 0:n], W, [PS[7]])

        with contextlib.ExitStack() as s2:
            xl = [sb([128, 1024], F32, s2, "xl") for _ in range(2)]
            for g in range(2):
                for t8 in range(8):
                    x_ = xl[t8 % 2]
                    dma("sp", x_[:], xin[g][t8 * 128:(t8 + 1) * 128, :], W=[x_])
                    for half in range(2):
                        p = PS[half + 2 * (t8 % 2)]
                        for c in range(4):
                            cc_ = half * 4 + c
                            op("pe", lambda e, p=p, c=c, cc_=cc_, x_=x_: e.transpose(p[:, c * 128:(c + 1) * 128], x_[:, cc_ * 128:(cc_ + 1) * 128], ident[:]),
                               W=[p], R=[x_, ident])
                        cp("dve" if half == 0 else "act", xT[g][:, half * 4:half * 4 + 4, t8 * 128:(t8 + 1) * 128],
                           p[:].rearrange("p (c t) -> p c t", c=4), [xT[g]], [p])
            fmvec(nrm[:, 2, :], [nrm], norm_f.rearrange("(c p) -> c p", p=128), 8, s2)
            S.barrier()

        def layer_vectors(l):
            with contextlib.ExitStack() as s2:
                cv = sb([2, 1024], F32, s2, "cv")
                cT = sb([128, 8, 2], F32, s2, "cT")
                dma("sp", cv[:], cvec, W=[cv])
                act(cv[:], cv[:], AF.Silu, [cv], [cv])
                for kc in range(8):
                    op("pe", lambda e, kc=kc: e.transpose(PS[6][:, kc * 2:kc * 2 + 2], cv[0:2, kc * 128:(kc + 1) * 128], ident[0:2, 0:2]),
                       W=[PS[6]], R=[cv, ident])
                cp("dve", cT[:], PS[6][:, 0:16].rearrange("p (k v) -> p k v", v=2), [cT], [PS[6]])
                if KSUB <= 1:
                    S.barrier(); return
                bm = sb([128, 48], F32, s2, "bm")
                fmvec(bm[:], [bm], b_mod[l].rearrange("(c p) -> c p", p=128), 48, s2)
                if KSUB <= 2:
                    S.barrier(); return
                mraw = sb([128, 48, 2], F32, s2, "mraw")
                wts = [sb([128, 8, 512], F32, s2, "wmod") for _ in range(2)]
                wv = w_mod[l].rearrange("(kc p) n -> p kc n", p=128)
                for blk in range(12):
                    wt = wts[blk % 2]
                    dma("sp", wt[:], wv[:, :, blk * 512:(blk + 1) * 512], W=[wt])
                    p = PS[4 + blk % 2]
                    for j in range(4):
                        for kc in range(8):
                            mm(p[:, j * 2:j * 2 + 2], wt[:, kc, j * 128:(j + 1) * 128], cT[:, kc, :], kc == 0, kc == 7, [p], [wt, cT])
                    for j in range(4):
                        jj = blk * 4 + j
                        ts("dve", mraw[:, jj, :], p[:, j * 2:j * 2 + 2], bm[:, jj:jj + 1], None, ALU.add, None, [mraw], [p, bm])
                if KSUB <= 3:
                    S.barrier(); return
                fmvec(nrm[:, 0, :], [nrm], norm1[l].rearrange("(c p) -> c p", p=128), 8, s2)
                fmvec(nrm[:, 1, :], [nrm], norm2[l].rearrange("(c p) -> c p", p=128), 8, s2)
                fmvec(bff2[:], [bff2], b_ff2[l].rearrange("(c p) -> c p", p=128), 8, s2)
                fmvec(bff1[:], [bff1], b_ff1[l].rearrange("(c p) -> c p", p=128), 32, s2)
                fmvec(bin_fm[:, 0:28], [bin_fm], b_in[l, 0:3584].rearrange("(c p) -> c p", p=128), 28, s2)
                fmvec(bin_fm[:, 28:64], [bin_fm], b_in[l, 3600:8208].rearrange("(c p) -> c p", p=128), 36, s2)
                dma("sp", bg16[:], b_in[l, 3584:3600].rearrange("(p o) -> p o", o=1), W=[bg16])
                if KSUB <= 4:
                    S.barrier(); return
                stg = sb([8, 4, 2, 16], F32, s2, "bsw")
                src = b_in[l, 0:1024].rearrange("(c b t s) -> c b t s", c=8, b=4, t=2)
                dma("sp", stg[:, :, 0, :], src[:, :, 1, :], W=[stg])
                dma("sp", stg[:, :, 1, :], src[:, :, 0, :], W=[stg])
                op("pe", lambda e: e.transpose(PS[7][:, 0:8], stg[:].rearrange("c b t s -> c (b t s)"), ident[0:8, 0:8]), W=[PS[7]], R=[stg, ident])
                cp("dve", bin_sw[:], PS[7][:, 0:8], [bin_sw], [PS[7]])
                if KSUB <= 5:
                    S.barrier(); return
                for k, (sh, sc, gg, nidx) in enumerate(((0, 8, 16, 0), (24, 32, 40, 1))):
                    for v in range(2):
                        stt("dve", modv[:, 3 * k + 0, :, v], mraw[:, sc:sc + 8, v], 1.0, nrm[:, nidx, :], ALU.add, ALU.mult, [modv], [mraw, nrm])
                        cp("dve", modv[:, 3 * k + 1, :, v], mraw[:, sh:sh + 8, v], [modv], [mraw])
                        cp("dve", modv[:, 3 * k + 2, :, v], mraw[:, gg:gg + 8, v], [modv], [mraw])
                for v in range(2):
                    tt("dve", modv[:, 6, :, v], modv[:, 5, :, v], bff2[:], ALU.mult, [modv], [modv, bff2])
                if KSUB <= 6:
                    S.barrier(); return
                lv = sb([128, 4, 64], F32, s2, "lv")
                dma("sp", lv[:].rearrange("p a b -> p (a b)"), da_lam[l:l + 1, :].partition_broadcast(128), W=[lv])
                pr = sb([128, 2, 64], F32, s2, "pr")
                sm = sb([128, 2], F32, s2, "sm")
                tt("dve", pr[:, 0, :], lv[:, 0, :], lv[:, 1, :], ALU.mult, [pr], [lv])
                tt("dve", pr[:, 1, :], lv[:, 2, :], lv[:, 3, :], ALU.mult, [pr], [lv])
                op("dve", lambda e: e.reduce_sum(sm[:], pr[:], mybir.AxisListType.X), W=[sm], R=[pr])
                act(sm[:], sm[:], AF.Exp, [sm], [sm])
                lam_init = 0.8 - 0.6 * math.exp(-0.3 * l)
                tt("dve", lamt[:, 0:1], sm[:, 0:1], sm[:, 1:2], ALU.subtract, [lamt], [sm])
                ts("dve", lamt[:, 0:1], lamt[:, 0:1], lam_init, None, ALU.add, None, [lamt], [lamt])
                ts("dve", lamt[:, 1:2], lamt[:, 0:1], -1.0, None, ALU.mult, None, [lamt], [lamt])
                if KSUB <= 7:
                    S.barrier(); return
                dma("sp", subw[:], da_subln[l].rearrange("(p o) -> p o", o=1), W=[subw])
                ts("dve", subw[:], subw[:], 1.0 - lam_init, None, ALU.mult, None, [subw], [subw])
                dma("sp", mlnw[:], ml_norm[l:l + 1, :].partition_broadcast(128), W=[mlnw])
                S.barrier()

        def norm_to_hT(g, hT, kset):
            with contextlib.ExitStack() as s2:
                sq = sb([128, 8, 512], BF16, s2, "sq")
                rs = sb([128, 512], F32, s2, "rs")
                tmp = [sb([128, 512], F32, s2, "ntmp") for _ in range(2)]
                for tb in range(2):
                    tsl = slice(tb * 512, (tb + 1) * 512)
                    act(sq[:], xT[g][:, :, tsl], AF.Square, [sq], [xT[g]])
                    if KSUB == 81:
                        continue
                    for c in range(8):
                        mm(PS[7][:], onesb[:], sq[:, c, :], c == 0, c == 7, [PS[7]], [onesb, sq])
                    if KSUB == 82:
                        continue
                    rstd_from(PS[7][:], rs[:], 1024.0, [rs], [PS[7], epsb])
                    if KSUB == 83:
                        continue
                    for c in range(8):
                        t_ = tmp[c % 2]
                        stt("dve", t_[:], xT[g][:, c, tsl], modv[:, 3 * kset, c, g:g + 1], rs[:], ALU.mult, ALU.mult, [t_], [xT[g], modv, rs])
                        if KSUB == 84:
                            continue
                        ts("pool" if c % 2 else "dve", hT[:, c, tsl], t_[:], modv[:, 3 * kset + 1, c, g:g + 1], None, ALU.add, None, [hT], [t_, modv])
                S.barrier()

        def bin_chunk(col):
            return col // 128 if col < 3584 else 28 + (col - 3600) // 128

        def proj_fm(l, hT, wts, col0, ncols, handler):
            wv = w_in[l].rearrange("(kc p) n -> p kc n", p=128)
            wt = wts[proj_fm.k % len(wts)]
            proj_fm.k += 1
            stg_ = proj_fm.stg[proj_fm.k % 2]
            dma("sp", stg_[:, :, 0:ncols], wv[:, :, col0:col0 + ncols], W=[stg_])
            cp("act", wt[:, :, 0:ncols], stg_[:, :, 0:ncols], [wt], [stg_])
            nmc = (ncols + 127) // 128
            for mc in range(nmc):
                m = min(128, ncols - mc * 128)
                for tb in range(2):
                    p = PS[proj_fm.pk % 4]
                    proj_fm.pk += 1
                    for kc in range(8):
                        mm(p[0:m, :], wt[:, kc, mc * 128:mc * 128 + m], hT[:, kc, tb * 512:(tb + 1) * 512], kc == 0, kc == 7, [p], [wt, hT])
                    handler(mc, tb, p)
        proj_fm.k = 0
        proj_fm.pk = 0

        def proj_tm(l, hT, wts, col0, handler):
            wv = w_in[l].rearrange("(kc p) n -> p kc n", p=128)
            wt = wts[proj_fm.k % len(wts)]
            proj_fm.k += 1
            stg_ = proj_fm.stg[proj_fm.k % 2]
            dma("sp", stg_[:, :, 0:512], wv[:, :, col0:col0 + 512], W=[stg_])
            cp("act", wt[:, :, 0:512], stg_[:, :, 0:512], [wt], [stg_])
            S.dma("pool", lambda e: e.dma_start(out=brow[:], in_=b_in[l:l + 1, col0:col0 + 512]), writes=[brow.b])
            for t8 in range(8):
                p = PS[proj_fm.pk % 4]
                proj_fm.pk += 1
                for kc in range(8):
                    mm(p[:], hT[:, kc, t8 * 128:(t8 + 1) * 128], wt[:, kc, 0:512], kc == 0, False, [p], [wt, hT])
                mm(p[:], onesb[0:1, :], brow[0:1, :], False, True, [p], [onesb, brow])
                handler(t8, p)

        brow = sb([1, 512], BF16, st, "brow")

        def inproj(l, g, hT):
            with contextlib.ExitStack() as s2:
                wts = [sb([128, 8, 512], BF16, s2, "wt") for _ in range(2)]
                proj_fm.stg = [sb([128, 8, 512], F32, s2, "wstg") for _ in range(2)]
                wsw = sb([128, 8, 512], BF16, s2, "wsw")
                ofm = [sb([128, 1024], BF16, s2, "ofm") for _ in range(2)]
                otm = [sb([128, 512], BF16, s2, "otm") for _ in range(2)]
                otf = [sb([128, 512], F32, s2, "otf") for _ in range(2)]
                og = sb([16, 1024], F32, s2, "og")
                kk = [0, 0]
                if g == 1:
                    rc = sb([128, 1024], F32, s2, "rc"); rsn = sb([128, 1024], F32, s2, "rsn")
                    dma("sp", rc[:], rope_cos, W=[rc]); dma("sp", rsn[:], rope_sin, W=[rsn])
                    t1 = sb([128, 512], F32, s2, "rt1"); t2 = sb([128, 512], F32, s2, "rt2")

                def fm_plain(col0, func, dsts):
                    def h(mc, tb, p):
                        o = ofm[(kk[0] // 2) % 2]
                        kk[0] += 1
                        bc_ = bin_chunk(col0) + mc
                        ts("dve", o[:, tb * 512:(tb + 1) * 512], p[:], bin_fm[:, bc_:bc_ + 1], None, ALU.add, None, [o], [p, bin_fm])
                        if func != AF.Identity:
                            act(o[:, tb * 512:(tb + 1) * 512], o[:, tb * 512:(tb + 1) * 512], func, [o], [o])
                        if tb == 1:
                            for d in dsts:
                                d(mc, o)
                    proj_fm(l, hT, wts, col0, 512, h)

                def to_rows(scr, row0):
                    return lambda mc, o: dma("pool", scr[row0 + mc * 128:row0 + (mc + 1) * 128, :], o[:], W=[scr], R=[o])

                def fm_rope(col0, dsts):
                    wv = w_in[l].rearrange("(kc p) n -> p kc n", p=128)
                    wt = wts[proj_fm.k % 2]
                    proj_fm.k += 1
                    stg_ = proj_fm.stg[proj_fm.k % 2]
                    dma("sp", stg_[:], wv[:, :, col0:col0 + 512], W=[stg_])
                    cp("act", wt[:], stg_[:], [wt], [stg_])
                    wtv = wt[:].rearrange("p k (b t s) -> p k b t s", t=2, s=16)
                    wsv = wsw[:].rearrange("p k (b t s) -> p k b t s", t=2, s=16)
                    for kc in range(8):
                        cp("pool", wsv[:, kc, :, 0, :], wtv[:, kc, :, 1, :], [wsw], [wt])
                        cp("pool", wsv[:, kc, :, 1, :], wtv[:, kc, :, 0, :], [wsw], [wt])
                    for mc in range(4):
                        ch = col0 // 128 + mc
                        for tb in range(2):
                            tsl = slice(tb * 512, (tb + 1) * 512)
                            pa = PS[proj_fm.pk % 4]; pb = PS[(proj_fm.pk + 1) % 4]
                            proj_fm.pk += 2
                            for kc in range(8):
                                mm(pa[:], wt[:, kc, mc * 128:(mc + 1) * 128], hT[:, kc, tsl], kc == 0, kc == 7, [pa], [wt, hT])
                            for kc in range(8):
                                mm(pb[:], wsw[:, kc, mc * 128:(mc + 1) * 128], hT[:, kc, tsl], kc == 0, kc == 7, [pb], [wsw, hT])
                            o = ofm[(kk[0] // 2) % 2]
                            kk[0] += 1
                            stt("dve", t1[:], pa[:], bin_fm[:, ch:ch + 1], rc[:, tsl], ALU.add, ALU.mult, [t1], [pa, bin_fm, rc])
                            stt("dve", t2[:], pb[:], bin_sw[:, ch:ch + 1], rsn[:, tsl], ALU.add, ALU.mult, [t2], [pb, bin_sw, rsn])
                            tt("pool", o[:, tsl], t1[:], t2[:], ALU.add, [o], [t1, t2])
                            if tb == 1:
                                for d in dsts:
                                    d(mc, o)

                def tm_seg(col0, f32_dst, bf_dsts):
                    def h(t8, p):
                        if f32_dst is not None:
                            o = otf[kk[1] % 2]
                            cp("act", o[:], p[:], [o], [p])
                            f32_dst(t8, o)
                        if bf_dsts:
                            ob = otm[kk[1] % 2]
                            if f32_dst is not None:
                                cp("dve", ob[:], o[:], [ob], [o])
                            else:
                                cp("dve", ob[:], p[:], [ob], [p])
                            for d in bf_dsts:
                                d(t8, ob)
                        kk[1] += 1
                    proj_tm(l, hT, wts, col0, h)

                def tm_rows(scr, row0=0):
                    return lambda t8, o: dma("pool", scr[row0 + t8 * 128:row0 + (t8 + 1) * 128, :], o[:], W=[scr], R=[o])

                def out_rows(dst):
                    return lambda t8, o: dma("pool", dst[t8 // 2, l, (t8 % 2) * 128:(t8 % 2) * 128 + 128, :], o[:], W=[dst], R=[o])

                def halo_rows(scr):
                    def f(t8, o):
                        if t8 < 2:
                            dma("pool", scr[t8 * 128:(t8 + 1) * 128, 512:1024], o[:], W=[scr], R=[o])
                        if t8 >= 6:
                            dma("pool", scr[256 + (t8 - 6) * 128:256 + (t8 - 5) * 128, 512:1024], o[:], W=[scr], R=[o])
                    return f

                def kt_cols(scr, c0, c1, d0):
                    return lambda mc, o: dma("pool", scr[mc * 128:(mc + 1) * 128, d0:d0 + (c1 - c0)], o[:, c0:c1], W=[scr], R=[o])

                if g == 0:
                    fm_plain(0, AF.Identity, [to_rows(sc_q_da[0], 0)])
                    fm_plain(512, AF.Identity, [to_rows(sc_k_da, 0)])
                    if KSUB == 19:
                        S.barrier(); return
                    tm_seg(512, out_rows(o_da_k), [])
                    if KSUB == 20:
                        S.barrier(); return
                    tm_seg(1024, out_rows(o_da_v), [tm_rows(sc_v_da)])
                    if KSUB == 21:
                        S.barrier(); return
                else:
                    fm_rope(0, [to_rows(sc_q_da[1], 0)])
                    if KSUB <= 9:
                        S.barrier(); return
                    fm_rope(512, [kt_cols(ag_kt_in, 0, 1024, 0)])
                    tm_seg(1024, None, [lambda t8, o: dma("pool", agVi_view[t8 * 128:(t8 + 1) * 128, :], o[:], W=[agV_i], R=[o])])
                if KSUB <= 10:
                    S.barrier(); return
                fm_plain(3600, AF.Identity, [to_rows(sc_q_na[g], 0)])
                if g == 0:
                    fm_plain(4112, AF.Identity, [to_rows(sc_k_na[0], 0)])
                    tm_seg(4112, out_rows(o_na_k), [])
                    tm_seg(4624, out_rows(o_na_v), [tm_rows(sc_v_na[0])])
                else:
                    fm_plain(4112, AF.Identity, [to_rows(sc_k_na[1], 0), kt_cols(ag_h_in, 0, 256, 0), kt_cols(ag_h_in, 768, 1024, 256)])
                    tm_seg(4624, None, [tm_rows(sc_v_na[1]), halo_rows(ag_h_in)])
                    if KSUB <= 11:
                        S.barrier(); return
                    S.cc(lambda e: e.collective_compute("AllGather", ALU.bypass, replica_groups=RG, ins=[ag_h_in.t.opt()], outs=[ag_h_out.t.opt()]),
                         reads=[ag_h_in.b], writes=[ag_h_out.b])
                    S.cc(lambda e: e.collective_compute("AllGather", ALU.bypass, replica_groups=RG, ins=[ag_kt_in.t.opt()], outs=[ag_kt_out.t.opt()]),
                         reads=[ag_kt_in.b], writes=[ag_kt_out.b])
                    S.cc(lambda e: e.collective_compute("AllGather", ALU.bypass, replica_groups=RG, ins=[agV_i.t.opt()], outs=[agV_o.t.opt()]),
                         reads=[agV_i.b], writes=[agV_o.b])
                if KSUB <= 12:
                    S.barrier(); return
                fm_plain(1536, AF.Identity, [to_rows(sc_ml_q[g], 0)])
                fm_plain(2048, AF.Identity, [to_rows(sc_ml_k[g], 0)])
                fm_plain(3072, AF.Sigmoid, [to_rows(sc_ml_o[g], 0)])
                tm_seg(2048, None, [tm_rows(sc_ml_ktm[g])])
                tm_seg(2560, None, [tm_rows(sc_ml_v[g])])
                if KSUB <= 13:
                    S.barrier(); return

                def hg(mc, tb, p):
                    ts("dve", og[:, tb * 512:(tb + 1) * 512], p[0:16, :], bg16[:, 0:1], None, ALU.add, None, [og], [p, bg16])
                    if tb == 1:
                        dma("pool", sc_ml_g[g][:, :], og[:], W=[sc_ml_g[g]], R=[og])
                proj_fm(l, hT, wts, 3584, 16, hg)
                for i in range(6):
                    fm_plain(5136 + 512 * i, AF.Sigmoid, [to_rows(sc_gate[g], 512 * i)])
                S.barrier()

        def attn_block(kts, vts, qT, nq, dv, p_lo, o_ps, den_ps, e_tiles, ecnt, bias_tiles=None):
            n = len(kts)
            for i in range(n):
                sp_ = PS[ecnt[0] % 2]
                e_ = e_tiles[ecnt[0] % 2]
                ecnt[0] += 1
                bt = bias_tiles[i] if bias_tiles is not None else None
                mm(sp_[:, 0:nq], kts[i][0], qT[0], True, bt is None, [sp_], kts[i][1] + qT[1])
                if bt is not None:
                    mm(sp_[:, 0:nq], identb[:], bt[0], False, True, [sp_], [identb] + bt[1])
                act(e_[:, 0:nq], sp_[:, 0:nq], AF.Exp, [e_], [sp_], scale=0.125)
                mm(o_ps[p_lo:p_lo + dv, 0:nq], vts[i][0], e_[:, 0:nq], i == 0, i == n - 1, [o_ps], vts[i][1] + [e_])
                mm(den_ps[p_lo:p_lo + dv, 0:nq], onesb[:, 0:dv], e_[:, 0:nq], i == 0, i == n - 1, [den_ps], [onesb, e_])

        def da_finish(nq, o0, d0, o1, d1, work, out_ap, W):
            r0, r1, t0, t1, sq, rs = work
            op("dve", lambda e: e.reciprocal(r0[:, 0:nq], d0[:, 0:nq]), W=[r0], R=[d0])
            tt("dve", t0[:, 0:nq], o0[:, 0:nq], r0[:, 0:nq], ALU.mult, [t0], [o0, r0])
            op("dve", lambda e: e.reciprocal(r1[:, 0:nq], d1[:, 0:nq]), W=[r1], R=[d1])
            tt("dve", t1[:, 0:nq], o1[:, 0:nq], r1[:, 0:nq], ALU.mult, [t1], [o1, r1])
            stt("dve", t0[:, 0:nq], t1[:, 0:nq], lamt[:, 1:2], t0[:, 0:nq], ALU.mult, ALU.add, [t0], [t1, lamt, t0])
            act(sq[:, 0:nq], t0[:, 0:nq], AF.Square, [sq], [t0])
            mm(PS[6][:, 0:nq], onesb[:], sq[:, 0:nq], True, True, [PS[6]], [onesb, sq])
            rstd_from(PS[6][:, 0:nq], rs[:, 0:nq], 128.0, [rs], [PS[6], epsb])
            stt("dve", out_ap, t0[:, 0:nq], subw[:, 0:1], rs[:, 0:nq], ALU.mult, ALU.mult, W, [t0, subw, rs])

        def da_work(s2):
            return (sb([128, 512], F32, s2, "r0"), sb([128, 512], F32, s2, "r1"), sb([128, 512], F32, s2, "t0"),
                    sb([128, 512], F32, s2, "t1"), sb([128, 512], BF16, s2, "sq"), sb([128, 512], F32, s2, "rs"))

        def da_prompt(l):
            with contextlib.ExitStack() as s2:
                qt = [sb([128, 1024], BF16, s2, "qt") for _ in range(2)]
                kt = [sb([128, 1024], BF16, s2, "kt") for _ in range(2)]
                vt = [sb([128, 8, 128], BF16, s2, "vt") for _ in range(2)]
                et = [sb([128, 512], BF16, s2, "et") for _ in range(2)]
                ot = [sb([128, 1024], BF16, s2, "ot") for _ in range(2)]
                work = da_work(s2)
                ecnt = [0]
                for h in range(4):
                    q_, k_, v_, o_ = qt[h % 2], kt[h % 2], vt[h % 2], ot[h % 2]
                    dma("sp", q_[:], sc_q_da[0][h * 128:(h + 1) * 128, :], W=[q_], R=[sc_q_da[0]])
                    dma("sp", k_[:], sc_k_da[h * 128:(h + 1) * 128, :], W=[k_], R=[sc_k_da])
                    dma("sp", v_[:], sc_v_da[:, h * 128:(h + 1) * 128].rearrange("(t p) d -> p t d", p=128), W=[v_], R=[sc_v_da])
                    for s in range(4):
                        for j in range(2):
                            kts = [(k_[j * 64:(j + 1) * 64, s * 256 + i * 128:s * 256 + (i + 1) * 128], [k_]) for i in range(2)]
                            vts = [(v_[:, s * 2 + i, :], [v_]) for i in range(2)]
                            attn_block(kts, vts, (q_[j * 64:(j + 1) * 64, s * 256:(s + 1) * 256], [q_]), 256, 128, 0,
                                       PS[2 + 2 * j], PS[3 + 2 * j], et, ecnt)
                        da_finish(256, PS[2], PS[3], PS[4], PS[5], work, o_[:, s * 256:(s + 1) * 256], [o_])
                    dma("pool", sc_o[0][h * 128:(h + 1) * 128, :], o_[:], W=[sc_o[0]], R=[o_])
                S.barrier()

        def load_ctx_kT(l, src, dst, s2):
            stg = [sb([128, 512], F32, s2, "cstg") for _ in range(2)]
            for t4 in range(4):
                g_ = stg[t4 % 2]
                dma("sp", g_[:], src[l, t4 * 128:(t4 + 1) * 128, :], W=[g_])
                p = PS[t4 % 2]
                for c in range(4):
                    op("pe", lambda e, p=p, c=c, g_=g_: e.transpose(p[:, c * 128:(c + 1) * 128], g_[:, c * 128:(c + 1) * 128], ident[:]), W=[p], R=[g_, ident])
                cp("dve", dst[:, :, t4 * 128:(t4 + 1) * 128], p[:].rearrange("p (c t) -> p c t", c=4), [dst], [p])

        def da_sample(l):
            with contextlib.ExitStack() as s2:
                ckt = sb([128, 4, 512], BF16, s2, "ckt")
                cv = sb([128, 4, 512], BF16, s2, "cvt")
                load_ctx_kT(l, cda_k, ckt, s2)
                S.dma("pool", lambda e: e.dma_start(out=cv[:], in_=cda_v[l].rearrange("(t p) d -> p t d", p=128)), writes=[cv.b])
                qt = [sb([128, 1024], BF16, s2, "qt") for _ in range(2)]
                kt = [sb([128, 4096], BF16, s2, "kt") for _ in range(2)]
                vt = [sb([128, 32, 128], BF16, s2, "vt") for _ in range(2)]
                et = [sb([128, 512], BF16, s2, "et") for _ in range(2)]
                ot = [sb([128, 1024], BF16, s2, "ot") for _ in range(2)]
                work = da_work(s2)
                ecnt = [0]
                kv = ag_kt_out.t.rearrange("(r f) t -> f r t", r=4)
                vv = agVo_view.rearrange("(r t) d -> r t d", r=4)
                for h in range(4):
                    q_, k_, v_, o_ = qt[h % 2], kt[h % 2], vt[h % 2], ot[h % 2]
                    dma("sp", q_[:], sc_q_da[1][h * 128:(h + 1) * 128, :], W=[q_], R=[sc_q_da[1]])
                    dma("sp", k_[:].rearrange("p (r t) -> p r t", r=4), kv[h * 128:(h + 1) * 128, :, 0:1024], W=[k_], R=[ag_kt_out])
                    for r in range(4):
                        dma("sp", v_[:, r * 8:(r + 1) * 8, :], vv[r, 0:1024, h * 128:(h + 1) * 128].rearrange("(t p) d -> p t d", p=128), W=[v_], R=[agV_o])
                    for qb in range(2):
                        for j in range(2):
                            kts = [(k_[j * 64:(j + 1) * 64, i * 128:(i + 1) * 128], [k_]) for i in range(32)]
                            kts += [(ckt[j * 64:(j + 1) * 64, h, i * 128:(i + 1) * 128], [ckt]) for i in range(4)]
                            vts = [(v_[:, i, :], [v_]) for i in range(32)] + [(cv[:, i, h * 128:(h + 1) * 128], [cv]) for i in range(4)]
                            attn_block(kts, vts, (q_[j * 64:(j + 1) * 64, qb * 512:(qb + 1) * 512], [q_]), 512, 128, 0,
                                       PS[2 + 2 * j], PS[3 + 2 * j], et, ecnt)
                        da_finish(512, PS[2], PS[3], PS[4], PS[5], work, o_[:, qb * 512:(qb + 1) * 512], [o_])
                    dma("pool", sc_o[1][h * 128:(h + 1) * 128, :], o_[:], W=[sc_o[1]], R=[o_])
                S.barrier()

        def na_finish(nq, o_ps, d_ps, rr, out_ap, W):
            op("dve", lambda e: e.reciprocal(rr[:, 0:nq], d_ps[:, 0:nq]), W=[rr], R=[d_ps])
            tt("dve", out_ap, o_ps[:, 0:nq], rr[:, 0:nq], ALU.mult, W, [o_ps, rr])

        def na_prompt(l):
            with contextlib.ExitStack() as s2:
                qt = [sb([128, 1024], BF16, s2, "qt") for _ in range(2)]
                kt = [sb([128, 1024], BF16, s2, "kt") for _ in range(2)]
                vt = sb([128, 8, 512], BF16, s2, "vt")
                et = [sb([128, 512], BF16, s2, "et") for _ in range(2)]
                ot = [sb([128, 1024], BF16, s2, "ot") for _ in range(2)]
                rr = sb([128, 512], F32, s2, "rr")
                ecnt = [0]
                dma("sp", vt[:], sc_v_na[0][:, :].rearrange("(t p) d -> p t d", p=128), W=[vt], R=[sc_v_na[0]])
                for c in range(4):
                    q_, k_, o_ = qt[c % 2], kt[c % 2], ot[c % 2]
                    dma("sp", q_[:], sc_q_na[0][c * 128:(c + 1) * 128, :], W=[q_], R=[sc_q_na[0]])
                    dma("sp", k_[:], sc_k_na[0][c * 128:(c + 1) * 128, :], W=[k_], R=[sc_k_na[0]])
                    for s in range(4):
                        for hh in range(2):
                            hd = 2 * c + hh
                            kts = [(k_[hh * 64:(hh + 1) * 64, s * 256 + i * 128:s * 256 + (i + 1) * 128], [k_]) for i in range(2)]
                            vts = [(vt[:, s * 2 + i, hd * 64:(hd + 1) * 64], [vt]) for i in range(2)]
                            attn_block(kts, vts, (q_[hh * 64:(hh + 1) * 64, s * 256:(s + 1) * 256], [q_]), 256, 64, hh * 64,
                                       PS[2], PS[3], et, ecnt)
                        na_finish(256, PS[2], PS[3], rr, o_[:, s * 256:(s + 1) * 256], [o_])
                    dma("pool", sc_o[0][1024 + c * 128:1024 + (c + 1) * 128, :], o_[:], W=[sc_o[0]], R=[o_])
                S.barrier()

        def sel_combine(dst_ap, W, src4, off, stk_tmp):
            ts("dve", dst_ap, src4[0], selv[:, off:off + 1], None, ALU.mult, None, W, [stk_tmp, selv])
            for r in range(1, 4):
                stt("dve", dst_ap, src4[r], selv[:, off + r:off + r + 1], dst_ap, ALU.mult, ALU.add, W, [stk_tmp, selv] + list(W))

        def na_sample(l):
            with contextlib.ExitStack() as s2:
                ckt = sb([128, 4, 512], BF16, s2, "ckt")
                cv = sb([128, 4, 512], BF16, s2, "cvt")
                load_ctx_kT(l, cna_k, ckt, s2)
                S.dma("pool", lambda e: e.dma_start(out=cv[:], in_=cna_v[l].rearrange("(t p) d -> p t d", p=128)), writes=[cv.b])
                tbl = sb([128, 8, 15, 64], BF16, s2, "tbl")
                wmt = sb([128, 16, 512], BF16, s2, "wmt")
                S.dma("pool", lambda e: e.dma_start(out=wmt[:], in_=wm_d.rearrange("p (a b) -> p a b", a=16)), writes=[wmt.b])
                with contextlib.ExitStack() as s3:
                    rp = sb([128, 8 * 15, 64], F32, s3, "rp")
                    cm = sb([128, 64], F32, s3, "cm")
                    dma("sp", rp[0:64], rpbT[l].rearrange("p (a b) -> p a b", b=64), W=[rp])
                    dma("sp", rp[64:128], rpbT[l].rearrange("p (a b) -> p a b", b=64), W=[rp])
                    dma("sp", cm[:], colmask_d, W=[cm])
                    for h in range(8):
                        stt("dve", tbl[:, h, :, :], rp[:, h * 15:(h + 1) * 15, :], 8.0, cm[:].rearrange("p (o q) -> p o q", o=1).to_broadcast([128, 15, 64]), ALU.mult, ALU.add, [tbl], [rp, cm])
                    S.barrier()
                vx = sb([128, 12, 512], BF16, s2, "vx")
                dma("sp", vx[:, 2:10, :], sc_v_na[1][:, :].rearrange("(t p) d -> p t d", p=128), W=[vx], R=[sc_v_na[1]])
                with contextlib.ExitStack() as s3:
                    h4 = sb([128, 4, 2, 512], BF16, s3, "h4")
                    vv = ag_h_out.t.rearrange("(r i) c -> r i c", r=4)
                    for part, (row0, off, t0) in enumerate(((256, 0, 0), (0, 4, 10))):
                        for r in range(4):
                            dma("sp", h4[:, r, :, :], vv[r, row0:row0 + 256, 512:1024].rearrange("(t p) d -> p t d", p=128), W=[h4], R=[ag_h_out])
                        sel_combine(vx[:, t0:t0 + 2, :], [vx], [h4[:, r, :, :] for r in range(4)], off, h4)
                    S.barrier()
                qt = [sb([128, 1024], BF16, s2, "qt") for _ in range(2)]
                kx = [sb([128, 1536], BF16, s2, "kx") for _ in range(2)]
                k4 = sb([128, 4, 512], BF16, s2, "k4")
                et = [sb([128, 512], BF16, s2, "et") for _ in range(2)]
                ot = [sb([128, 1024], BF16, s2, "ot") for _ in range(2)]
                bts = [sb([128, 512], BF16, s2, "bt") for _ in range(8)]
                rr = sb([128, 512], F32, s2, "rr")
                ecnt = [0]
                bcnt = [0]
                kv = ag_h_out.t.rearrange("(r f) t -> f r t", r=4)
                for c in range(4):
                    q_, k_, o_ = qt[c % 2], kx[c % 2], ot[c % 2]
                    dma("sp", q_[:], sc_q_na[1][c * 128:(c + 1) * 128, :], W=[q_], R=[sc_q_na[1]])
                    dma("sp", k_[:, 256:1280], sc_k_na[1][c * 128:(c + 1) * 128, :], W=[k_], R=[sc_k_na[1]])
                    dma("sp", k4[:], kv[c * 128:(c + 1) * 128, :, 0:512], W=[k4], R=[ag_h_out])
                    sel_combine(k_[:, 0:256], [k_], [k4[:, r, 256:512] for r in range(4)], 0, k4)
                    sel_combine(k_[:, 1280:1536], [k_], [k4[:, r, 0:256] for r in range(4)], 4, k4)
                    for qb in range(2):
                        for hh in range(2):
                            hd = 2 * c + hh
                            kts, vts, bl_ = [], [], []
                            for kt8 in range(8):
                                ktile = 4 * qb + kt8
                                bt = bts[bcnt[0] % 8]
                                bcnt[0] += 1
                                cp("pool", bt[:], wmt[:, qb * 8 + kt8, :], [bt], [wmt])
                                for a in range(2):
                                    jp = 2 * ktile + a
                                    ilo = max(jp - 11, 8 * qb); ihi = min(jp + 3, 8 * qb + 7)
                                    if ilo > ihi:
                                        continue
                                    clo, chi = ilo - 8 * qb, ihi - 8 * qb
                                    tlo = ilo - jp + 11
                                    n_ = chi - clo + 1
                                    tt("pool", bt[a * 64:(a + 1) * 64, clo * 64:(chi + 1) * 64].rearrange("p (n q) -> p n q", q=64),
                                       bt[a * 64:(a + 1) * 64, clo * 64:(chi + 1) * 64].rearrange("p (n q) -> p n q", q=64),
                                       tbl[a * 64:(a + 1) * 64, hd, tlo:tlo + n_, :], ALU.add, [bt], [bt, tbl])
                                kts.append((k_[hh * 64:(hh + 1) * 64, ktile * 128:(ktile + 1) * 128], [k_]))
                                vts.append((vx[:, ktile, hd * 64:(hd + 1) * 64], [vx]))
                                bl_.append((bt[:], [bt]))
                            for i in range(4):
                                kts.append((ckt[hh * 64:(hh + 1) * 64, c, i * 128:(i + 1) * 128], [ckt]))
                                vts.append((cv[:, i, hd * 64:(hd + 1) * 64], [cv]))
                                bl_.append(None)
                            attn_block(kts, vts, (q_[hh * 64:(hh + 1) * 64, qb * 512:(qb + 1) * 512], [q_]), 512, 64, hh * 64,
                                       PS[2], PS[3], et, ecnt, bias_tiles=bl_)
                        na_finish(512, PS[2], PS[3], rr, o_[:, qb * 512:(qb + 1) * 512], [o_])
                    dma("pool", sc_o[1][1024 + c * 128:1024 + (c + 1) * 128, :], o_[:], W=[sc_o[1]], R=[o_])
                S.barrier()

        CS = 128.0 ** -0.5

        def ml_gates(g, tok0, T, s2):
            IG = sb([64, T], F32, s2, "IG"); FP = sb([64, T], F32, s2, "FP"); B = sb([64, T], F32, s2, "B"); A = sb([64, T], F32, s2, "A")
            mset("pool", IG[:], 0.0, [IG]); mset("pool", FP[:], 0.0, [FP])
            for d in range(2):
                dma("sp", IG[d * 32:d * 32 + 4, :], sc_ml_g[g][d * 8:d * 8 + 4, tok0:tok0 + T], W=[IG], R=[sc_ml_g[g]])
                dma("sp", FP[d * 32:d * 32 + 4, :], sc_ml_g[g][d * 8 + 4:d * 8 + 8, tok0:tok0 + T], W=[FP], R=[sc_ml_g[g]])
            act(FP[:], FP[:], AF.Exp, [FP], [FP], scale=-1.0)
            ts("dve", FP[:], FP[:], 1.0, None, ALU.add, None, [FP], [FP])
            act(FP[:], FP[:], AF.Ln, [FP], [FP])
            ts("dve", FP[:], FP[:], -1.0, None, ALU.mult, None, [FP], [FP])
            op("dve", lambda e: e.tensor_tensor_scan(B[0:32, :], FP[0:32, :], FP[0:32, :], 0.0, ALU.add, ALU.bypass), W=[B], R=[FP])
            op("dve", lambda e: e.tensor_tensor_scan(B[32:64, T - 1::-1], FP[32:64, T - 1::-1], FP[32:64, T - 1::-1], 0.0, ALU.add, ALU.bypass), W=[B], R=[FP])
            tt("dve", A[:], IG[:], B[:], ALU.subtract, [A], [IG, B])
            return B, A

        def cummax(G, A, init_ap, T, R):
            op("dve", lambda e: e.tensor_tensor_scan(G[0:32, :], A[0:32, :], A[0:32, :], init_ap[0:32, :], ALU.max, ALU.bypass), W=[G], R=[A] + R)
            op("dve", lambda e: e.tensor_tensor_scan(G[32:64, T - 1::-1], A[32:64, T - 1::-1], A[32:64, T - 1::-1], init_ap[32:64, :], ALU.max, ALU.bypass), W=[G], R=[A] + R)

        def ends(dst, src, T, nch, W, R):
            v = src.rearrange("p (c t) -> p c t", t=128)
            cp("dve", dst[0:32, :], v[0:32, :, 127], W, R)
            cp("dve", dst[32:64, :], v[32:64, :, 0], W, R)

        def ml_scan(l, g, tok0, T, m_in, c_in, seq, s2o):
            nch = T // 128
            with contextlib.ExitStack() as s2:
                B, A = ml_gates(g, tok0, T, s2)
                G = sb([64, T], F32, s2, "G")
                cummax(G, A, m_in, T, [m_in])
                VEC = sb([64, 4, T], F32, s2, "VEC")
                NG = sb([64, T], F32, s2, "NG")
                Ge = sb([64, nch], F32, s2, "Ge"); Gp = sb([64, nch], F32, s2, "Gp"); DEC = sb([64, nch], F32, s2, "DEC")
                Mt = sb([64, T], F32, s2, "Mt")
                cp("dve", VEC[:, 0, :], A[:], [VEC], [A])
                ts("dve", NG[:], G[:], -1.0, None, ALU.mult, None, [NG], [G])
                tt("dve", Mt[:], B[:], G[:], ALU.add, [Mt], [B, G])
                act(VEC[:, 3, :], Mt[:], AF.Exp, [VEC], [Mt], scale=-1.0)
                ends(Ge[:], G[:], T, nch, [Ge], [G])
                cp("dve", Gp[0:32, 0:1], m_in[0:32, :], [Gp], [m_in])
                cp("dve", Gp[32:64, nch - 1:nch], m_in[32:64, :], [Gp], [m_in])
                if nch > 1:
                    cp("dve", Gp[0:32, 1:nch], Ge[0:32, 0:nch - 1], [Gp], [Ge])
                    cp("dve", Gp[32:64, 0:nch - 1], Ge[32:64, 1:nch], [Gp], [Ge])
                tt("dve", DEC[:], Gp[:], Ge[:], ALU.subtract, [DEC], [Gp, Ge])
                act(DEC[:], DEC[:], AF.Exp, [DEC], [DEC])
                Gv = G[:].rearrange("p (c t) -> p c t", t=128)
                Av = A[:].rearrange("p (c t) -> p c t", t=128)
                tt("dve", VEC[:, 2, :].rearrange("p (c t) -> p c t", t=128), Gp[:].rearrange("p (c o) -> p c o", o=1).to_broadcast([64, nch, 128]), Gv, ALU.subtract, [VEC], [Gp, G])
                tt("dve", VEC[:, 1, :].rearrange("p (c t) -> p c t", t=128), Av, Ge[:].rearrange("p (c o) -> p c o", o=1).to_broadcast([64, nch, 128]), ALU.subtract, [VEC], [A, Ge])
                act(VEC[:, 1:3, :], VEC[:, 1:3, :], AF.Exp, [VEC], [VEC])
                TMV = sb([128, nch, 4, 64], F32, s2, "TMV")
                for c in range(nch):
                    p = PS[6 + c % 2]
                    for k in range(4):
                        op("pe", lambda e, p=p, k=k, c=c: e.transpose(p[:, k * 64:(k + 1) * 64], VEC[:, k, c * 128:(c + 1) * 128], ident[0:64, 0:64]), W=[p], R=[VEC, ident])
                    cp("act", TMV[:, c, :, :], p[:, 0:256].rearrange("p (k d) -> p k d", k=4), [TMV], [p])
                DECB = sb([128, 8, nch], F32, s2, "DECB")
                for dh in range(8):
                    mm(PS[6][:, dh * nch:(dh + 1) * nch], selm[:, dh, :], DEC[:], True, True, [PS[6]], [selm, DEC])
                cp("dve", DECB[:], PS[6][:, 0:8 * nch].rearrange("p (a b) -> p a b", a=8), [DECB], [PS[6]])
                if seq is not None:
                    mf = sb([64, 1], F32, s2, "mf")
                    cp("dve", mf[0:32, :], Mt[0:32, T - 1:T], [mf], [Mt])
                    cp("dve", mf[32:64, :], Mt[32:64, 0:1], [mf], [Mt])
                    for d in range(2):
                        dma("pool", o_ml_m[seq, l, d * 4:(d + 1) * 4].rearrange("(p o) -> p o", o=1), mf[d * 32:d * 32 + 4, :], W=[o_ml_m], R=[mf])
                qts = [sb([128, T], BF16, s2, "mq") for _ in range(2)]
                kts = [sb([128, T], BF16, s2, "mk") for _ in range(2)]
                ktm = [sb([128, nch, 128], BF16, s2, "mkt") for _ in range(2)]
                vau = [sb([128, nch, 132], BF16, s2, "mv") for _ in range(2)]
                CSTs = [sb([128, 129], F32, s2, "CST") for _ in range(2)]; CSTbs = [sb([128, 129], BF16, s2, "CSTb") for _ in range(2)]
                dars = [sb([128, 128], F32, s2, "dar") for _ in range(2)]; dds = [sb([128, 128], F32, s2, "dd") for _ in range(2)]
                pTs = [sb([128, 128], BF16, s2, "pT") for _ in range(2)]
                has = [sb([128, 129], F32, s2, "ha") for _ in range(2)]; hns = [sb([128, 129], F32, s2, "hn") for _ in range(2)]
                dns = [sb([128, 2], F32, s2, "dn") for _ in range(2)]
                houts = [[sb([128, 128], F32, s2, "hout") for _ in range(2)] for _ in range(2)]
                kps = [sb([128, 128], BF16, s2, "kp") for _ in range(2)]
                PSd = [[TL(PS[k].t, Buf("psml%d_%d" % (d, k))) for k in range(5)] for d in range(2)]
                hc = [0, 0]
                for h in range(4):
                    q_, k_, km, v_ = qts[h % 2], kts[h % 2], ktm[h % 2], vau[h % 2]
                    dma("sp", q_[:], sc_ml_q[g][h * 128:(h + 1) * 128, tok0:tok0 + T], W=[q_], R=[sc_ml_q[g]])
                    dma("sp", k_[:], sc_ml_k[g][h * 128:(h + 1) * 128, tok0:tok0 + T], W=[k_], R=[sc_ml_k[g]])
                    dma("sp", km[:], sc_ml_ktm[g][tok0:tok0 + T, h * 128:(h + 1) * 128].rearrange("(c p) d -> p c d", p=128), W=[km], R=[sc_ml_ktm[g]])
                    mset("pool", v_[:, :, 128:129], 1.0, [v_])
                    dma("sp", v_[:, :, 0:128], sc_ml_v[g][tok0:tok0 + T, h * 128:(h + 1) * 128].rearrange("(c p) d -> p c d", p=128), W=[v_], R=[sc_ml_v[g]])
                    for d in range(2):
                        dh = d * 4 + h
                        if c_in is None:
                            mset("pool", CSTs[d][:], 0.0, [CSTs[d]])
                        else:
                            cp("pool", CSTs[d][:], c_in[:, dh, :], [CSTs[d]], [c_in])
                        cp("dve", CSTbs[d][:], CSTs[d][:], [CSTbs[d]], [CSTs[d]])
                    for step in range(nch):
                        for d in range(2):
                            dh = d * 4 + h
                            pid = PID(dh)
                            c = step if d == 0 else nch - 1 - step
                            csl = slice(c * 128, (c + 1) * 128)
                            o0 = d * 256
                            P0, P1, P2, P3, P4 = PSd[d]
                            CST, CSTb, dar, dd, pT, ha, hn, dn, kp = CSTs[d], CSTbs[d], dars[d], dds[d], pTs[d], has[d], hns[d], dns[d], kps[d]
                            mm(P0[:, o0:o0 + 128], k_[:, csl], q_[:, csl], True, True, [P0], [k_, q_])
                            mm(P1[:, o0:o0 + 128], selm[:, dh, :], NG[:, csl], True, True, [P1], [selm, NG])
                            stt("dve", dar[:], P1[:, o0:o0 + 128], TMV[:, c, 0, pid:pid + 1], mmask[:, d, :], ALU.add, ALU.min, [dar], [P1, TMV, mmask])
                            act(dd[:], dar[:], AF.Exp, [dd], [dar])
                            stt("dve", pT[:], P0[:, o0:o0 + 128], CS, dd[:], ALU.mult, ALU.mult, [pT], [P0, dd])
                            mm(P2[:, o0:o0 + 129], pT[:], v_[:, c, 0:129], True, True, [P2], [pT, v_])
                            mm(P3[:, o0:o0 + 129], q_[:, csl], CSTb[:], True, True, [P3], [q_, CSTb])
                            cp("act", ha[:], P2[:, o0:o0 + 129], [ha], [P2])
                            stt("dve", hn[:], P3[:, o0:o0 + 129], TMV[:, c, 2, pid:pid + 1], ha[:], ALU.mult, ALU.add, [hn], [P3, TMV, ha])
                            stt("pool", dn[:, 0:1], hn[:, 128:129], -1.0, hn[:, 128:129], ALU.mult, ALU.max, [dn], [hn])
                            ts("pool", dn[:, 0:1], dn[:, 0:1], TMV[:, c, 3, pid:pid + 1], None, ALU.max, None, [dn], [dn, TMV])
                            op("dve", lambda e, dn=dn: e.reciprocal(dn[:, 1:2], dn[:, 0:1]), W=[dn], R=[dn])
                            ho = houts[d][hc[d] % 2]
                            hc[d] += 1
                            ts("pool", ho[:], hn[:, 0:128], dn[:, 1:2], None, ALU.mult, None, [ho], [hn, dn])
                            dma("pool", sc_hml[g][d, tok0 + c * 128:tok0 + (c + 1) * 128, h * 128:(h + 1) * 128], ho[:], W=[sc_hml[g]], R=[ho])
                            ts("dve", kp[:], km[:, c, :], TMV[:, c, 1, pid:pid + 1], CS, ALU.mult, ALU.mult, [kp], [km, TMV])
                            mm(P4[:, o0:o0 + 129], kp[:], v_[:, c, 0:129], True, True, [P4], [kp, v_])
                            stt("dve", CST[:], CST[:], DECB[:, dh, c:c + 1], P4[:, o0:o0 + 129], ALU.mult, ALU.add, [CST], [CST, DECB, P4])
                            cp("act", CSTb[:], CST[:], [CSTb], [CST])
                    if seq is not None:
                        for d in range(2):
                            dh = d * 4 + h
                            dma("pool", o_ml_C[seq, l, dh, :, :], CSTs[d][:, 0:128], W=[o_ml_C], R=[CSTs[d]])
                            dma("pool", o_ml_n[seq, l, dh, :].rearrange("(p o) -> p o", o=1), CSTs[d][:, 128:129], W=[o_ml_n], R=[CSTs[d]])
                S.barrier()

        def ml_finish(g):
            with contextlib.ExitStack() as s2:
                h0 = [sb([128, 512], F32, s2, "h0") for _ in range(2)]
                h1 = [sb([128, 512], F32, s2, "h1") for _ in range(2)]
                sqj = sb([128, 128], F32, s2, "sqj")
                ss = sb([128, 4], F32, s2, "ss")
                mo = [sb([128, 4, 128], BF16, s2, "mo") for _ in range(2)]
                ob = [sb([128, 4, 128], BF16, s2, "ob") for _ in range(2)]
                for t8 in range(8):
                    a, b_, m_, o_ = h0[t8 % 2], h1[t8 % 2], mo[t8 % 2], ob[t8 % 2]
                    tsl = slice(t8 * 128, (t8 + 1) * 128)
                    dma("sp", a[:], sc_hml[g][0, tsl, :], W=[a], R=[sc_hml[g]])
                    dma("sp", b_[:], sc_hml[g][1, tsl, :], W=[b_], R=[sc_hml[g]])
                    dma("sp", m_[:], sc_ml_o[g][:, tsl].rearrange("(h p) t -> p h t", p=128), W=[m_], R=[sc_ml_o[g]])
                    tt("dve", a[:], a[:], b_[:], ALU.add, [a], [a, b_])
                    mset("dve", ss[:], 0.0, [ss])
                    for h in range(4):
                        act(sqj[:], a[:, h * 128:(h + 1) * 128], AF.Square, [sqj, ss], [a], accum=ss[:, h:h + 1])
                    rstd_from(ss[:], ss[:], 128.0, [ss], [ss, epsb])
                    for h in range(4):
                        ts("dve", a[:, h * 128:(h + 1) * 128], a[:, h * 128:(h + 1) * 128], ss[:, h:h + 1], None, ALU.mult, None, [a], [a, ss])
                    tt("dve", a[:], a[:], mlnw[:], ALU.mult, [a], [a, mlnw])
                    p = PS[t8 % 2]
                    for h in range(4):
                        op("pe", lambda e, p=p, h=h, a=a: e.transpose(p[:, h * 128:(h + 1) * 128], a[:, h * 128:(h + 1) * 128], ident[:]), W=[p], R=[a, ident])
                    tt("dve", o_[:], p[:].rearrange("p (h t) -> p h t", h=4), m_[:], ALU.mult, [o_], [p, m_])
                    dma("pool", sc_o[g][512:1024, tsl].rearrange("(h p) t -> p h t", p=128), o_[:], W=[sc_o[g]], R=[o_])
                S.barrier()

        zero_m = sb([64, 1], F32, st, "zero_m")
        mset("dve", zero_m[:], 0.0, [zero_m])

        def ml_prompt(l):
            for s in range(4):
                ml_scan(l, 0, s * 256, 256, zero_m, None, s, None)
            ml_finish(0)

        def ml_sample(l):
            T = 1024
            with contextlib.ExitStack() as s1:
                m_in = sb([64, 1], F32, s1, "m_in")
                c_in = sb([128, 8, 129], F32, s1, "c_in")
                with contextlib.ExitStack() as s2:
                    B, A = ml_gates(1, 0, T, s2)
                    G0 = sb([64, T], F32, s2, "G0")
                    neg = sb([64, 1], F32, s2, "neg")
                    mset("dve", neg[:], -1e30, [neg])
                    cummax(G0, A, neg, T, [neg])
                    GF = sb([64, 2], F32, s2, "GF")
                    cp("dve", GF[0:32, 0:1], G0[0:32, T - 1:T], [GF], [G0]); cp("dve", GF[32:64, 0:1], G0[32:64, 0:1], [GF], [G0])
                    cp("dve", GF[0:32, 1:2], B[0:32, T - 1:T], [GF], [B]); cp("dve", GF[32:64, 1:2], B[32:64, 0:1], [GF], [B])
                    WA = sb([64, T], F32, s2, "WA")
                    ts("dve", WA[:], A[:], GF[:, 0:1], None, ALU.subtract, None, [WA], [A, GF])
                    act(WA[:], WA[:], AF.Exp, [WA], [WA])
                    WT = sb([128, 8, 64], F32, s2, "WT")
                    for c in range(8):
                        p = PS[6 + c % 2]
                        op("pe", lambda e, p=p, c=c: e.transpose(p[:, 0:64], WA[:, c * 128:(c + 1) * 128], ident[0:64, 0:64]), W=[p], R=[WA, ident])
                        cp("act", WT[:, c, :], p[:, 0:64], [WT], [p])
                    ktm = [sb([128, 8, 128], BF16, s2, "mkt") for _ in range(2)]
                    vau = [sb([128, 8, 132], BF16, s2, "mv") for _ in range(2)]
                    kp = [sb([128, 128], BF16, s2, "kp") for _ in range(2)]
                    so = [sb([128, 132], F32, s2, "so") for _ in range(2)]
                    kc_ = [0]
                    for h in range(4):
                        km, v_ = ktm[h % 2], vau[h % 2]
                        dma("sp", km[:], sc_ml_ktm[1][:, h * 128:(h + 1) * 128].rearrange("(c p) d -> p c d", p=128), W=[km], R=[sc_ml_ktm[1]])
                        mset("pool", v_[:, :, 128:129], 1.0, [v_])
                        dma("sp", v_[:, :, 0:128], sc_ml_v[1][:, h * 128:(h + 1) * 128].rearrange("(c p) d -> p c d", p=128), W=[v_], R=[sc_ml_v[1]])
                        for d in range(2):
                            dh = d * 4 + h
                            pid = PID(dh)
                            p = PS[dh % 2]
                            for c in range(8):
                                k2 = kp[kc_[0] % 2]
                                kc_[0] += 1
                                ts("dve", k2[:], km[:, c, :], WT[:, c, pid:pid + 1], CS, ALU.mult, ALU.mult, [k2], [km, WT])
                                mm(p[:, 0:129], k2[:], v_[:, c, 0:129], c == 0, c == 7, [p], [k2, v_])
                            mm(PS[2 + dh % 2][:, 0:2], selm[:, dh, :], GF[:], True, True, [PS[2 + dh % 2]], [selm, GF])
                            o_ = so[dh % 2]
                            mset("pool", o_[:, 131:132], 0.0, [o_])
                            cp("act", o_[:, 0:129], p[:, 0:129], [o_], [p])
                            cp("dve", o_[:, 129:131], PS[2 + dh % 2][:, 0:2], [o_], [PS[2 + dh % 2]])
                            dma("pool", ag_ml_in[dh * 128:(dh + 1) * 128, :], o_[:], W=[ag_ml_in], R=[o_])
                    S.cc(lambda e: e.collective_compute("AllGather", ALU.bypass, replica_groups=RG, ins=[ag_ml_in.t.opt()], outs=[ag_ml_out.t.opt()]),
                         reads=[ag_ml_in.b], writes=[ag_ml_out.b])
                    S.barrier()
                with contextlib.ExitStack() as s2:
                    sm = sb([128, 4, 8, 132], F32, s2, "sm")
                    dma("sp", sm[:], ag_ml_out.t.rearrange("(r d p) c -> p r d c", r=4, d=8), W=[sm], R=[ag_ml_out])
                    mrep = sb([128, 8], F32, s2, "mrep")
                    dma("sp", mrep[:], st_m[l:l + 1, :].partition_broadcast(128), W=[mrep])
                    dma("sp", c_in[:, :, 0:128], st_C[l].rearrange("d p e -> p d e"), W=[c_in])
                    dma("sp", c_in[:, :, 128:129], st_n[l].rearrange("d (p o) -> p d o", o=1), W=[c_in], slow=True)
                    fp_ = sb([128, 4], F32, s2, "fp"); gp_ = sb([128, 4], F32, s2, "gp"); off_ = sb([128, 1], F32, s2, "off")
                    mx = sb([128, 4], F32, s2, "mx"); e0 = sb([128, 4], F32, s2, "e0"); e1 = sb([128, 4], F32, s2, "e1")
                    for d in range(2):
                        hs = slice(d * 4, d * 4 + 4)
                        for r in (range(4) if d == 0 else range(3, -1, -1)):
                            fl = selv[:, 8 + 4 * d + r:8 + 4 * d + r + 1]
                            ts("dve", off_[:], fl, 1e30, -1e30, ALU.mult, ALU.add, [off_], [selv])
                            ts("dve", fp_[:], sm[:, r, hs, 130], fl, None, ALU.mult, None, [fp_], [sm, selv])
                            ts("dve", gp_[:], sm[:, r, hs, 129], fl, off_[:, 0:1], ALU.mult, ALU.add, [gp_], [sm, selv, off_])
                            tt("dve", mx[:], mrep[:, hs], gp_[:], ALU.max, [mx], [mrep, gp_])
                            tt("dve", e0[:], mrep[:, hs], mx[:], ALU.subtract, [e0], [mrep, mx])
                            tt("dve", e1[:], gp_[:], mx[:], ALU.subtract, [e1], [gp_, mx])
                            act(e0[:], e0[:], AF.Exp, [e0], [e0])
                            act(e1[:], e1[:], AF.Exp, [e1], [e1])
                            for h in range(4):
                                dh = d * 4 + h
                                ts("dve", c_in[:, dh, :], c_in[:, dh, :], e0[:, h:h + 1], None, ALU.mult, None, [c_in], [c_in, e0])
                                stt("dve", c_in[:, dh, :], sm[:, r, dh, 0:129], e1[:, h:h + 1], c_in[:, dh, :], ALU.mult, ALU.add, [c_in], [sm, e1, c_in])
                            tt("dve", mrep[:, hs], fp_[:], mx[:], ALU.add, [mrep], [fp_, mx])
                    md = sb([64, 8], F32, s2, "md")
                    tt("dve", md[:], mrep[0:64, :], selm[:, :, 0], ALU.mult, [md], [mrep, selm])
                    op("dve", lambda e: e.reduce_sum(m_in[:], md[:], mybir.AxisListType.X), W=[m_in], R=[md])
                    S.barrier()
                ml_scan(l, 1, 0, T, m_in, c_in, None, None)
            ml_finish(1)

        def merge_out(l, g):
            with contextlib.ExitStack() as s2:
                OT = sb([128, 12, 1024], BF16, s2, "OT")
                dma("sp", OT[:], sc_o[g][:, :].rearrange("(k p) t -> p k t", p=128), W=[OT], R=[sc_o[g]])
                mT = sb([128, 8, 1024], BF16, s2, "mT")
                gts = [sb([128, 3, 1024], BF16, s2, "gt") for _ in range(2)]
                wus = [sb([128, 3, 4, 128], BF16, s2, "wu") for _ in range(2)]
                tm_ = [sb([128, 512], F32, s2, "mtmp") for _ in range(3)]
                pk = 0
                for fc in range(8):
                    gt, wu = gts[fc % 2], wus[fc % 2]
                    dma("sp", gt[:], sc_gate[g][:, :].rearrange("(i f) t -> f i t", i=3)[fc * 128:(fc + 1) * 128], W=[gt], R=[sc_gate[g]])
                    for i in range(3):
                        S.dma("pool", lambda e, i=i, wu=wu: e.dma_start(out=wu[:, i, :, :], in_=w_up[i][l].rearrange("(k p) n -> p k n", p=128)[:, :, fc * 128:(fc + 1) * 128]), writes=[wu.b])
                    for tb in range(2):
                        tsl = slice(tb * 512, (tb + 1) * 512)
                        for i in range(3):
                            p = PS[pk % 4]
                            pk += 1
                            for kc in range(4):
                                mm(p[:], wu[:, i, kc, :], OT[:, i * 4 + kc, tsl], kc == 0, kc == 3, [p], [wu, OT])
                            tt("dve", tm_[i][:], p[:], gt[:, i, tsl], ALU.mult, [tm_[i]], [p, gt])
                        tt("pool", tm_[0][:], tm_[0][:], tm_[1][:], ALU.add, [tm_[0]], [tm_[0], tm_[1]])
                        tt("pool", mT[:, fc, tsl], tm_[0][:], tm_[2][:], ALU.add, [mT], [tm_[0], tm_[2]])
                wo = [sb([128, 8, 512], BF16, s2, "wo") for _ in range(2)]
                wv = w_out[l].rearrange("(k p) n -> p k n", p=128)
                for hf in range(2):
                    S.dma("pool", lambda e, hf=hf: e.dma_start(out=wo[hf][:], in_=wv[:, :, hf * 512:(hf + 1) * 512]), writes=[wo[hf].b])
                for fc in range(8):
                    w_ = wo[fc // 4]
                    for tb in range(2):
                        tsl = slice(tb * 512, (tb + 1) * 512)
                        p = PS[pk % 4]
                        pk += 1
                        for kc in range(8):
                            mm(p[:], w_[:, kc, (fc % 4) * 128:(fc % 4 + 1) * 128], mT[:, kc, tsl], kc == 0, kc == 7, [p], [w_, mT])
                        stt("dve", xT[g][:, fc, tsl], p[:], modv[:, 2, fc, g:g + 1], xT[g][:, fc, tsl], ALU.mult, ALU.add, [xT[g]], [p, modv, xT[g]])
                S.barrier()

        def mlp(l, g, hT):
            with contextlib.ExitStack() as s2:
                uT = sb([128, 32, 1024], BF16, s2, "uT")
                w1 = [sb([128, 8, 256], BF16, s2, "w1") for _ in range(2)]
                w2 = [sb([128, 32, 128], BF16, s2, "w2") for _ in range(2)]
                rt = [sb([128, 512], F32, s2, "rt") for _ in range(2)]
                wv1 = w_ff1[l].rearrange("(k p) n -> p k n", p=128)
                wv2 = w_ff2[l].rearrange("(k p) n -> p k n", p=128)
                pk = 0
                k = 0
                for jb in range(16):
                    w_ = w1[jb % 2]
                    S.dma("pool", lambda e, w_=w_, jb=jb: e.dma_start(out=w_[:], in_=wv1[:, :, jb * 256:(jb + 1) * 256]), writes=[w_.b])
                    for mc in range(2):
                        fi = jb * 2 + mc
                        for tb in range(2):
                            tsl = slice(tb * 512, (tb + 1) * 512)
                            p = PS[pk % 4]
                            pk += 1
                            for kc in range(8):
                                mm(p[:], w_[:, kc, mc * 128:(mc + 1) * 128], hT[:, kc, tsl], kc == 0, kc == 7, [p], [w_, hT])
                            r_ = rt[k % 2]
                            k += 1
                            ts("dve", r_[:], p[:], bff1[:, fi:fi + 1], 0.0, ALU.add, ALU.max, [r_], [p, bff1])
                            tt("pool", uT[:, fi, tsl], r_[:], r_[:], ALU.mult, [uT], [r_])
                for fc in range(8):
                    w_ = w2[fc % 2]
                    S.dma("pool", lambda e, w_=w_, fc=fc: e.dma_start(out=w_[:], in_=wv2[:, :, fc * 128:(fc + 1) * 128]), writes=[w_.b])
                    for tb in range(2):
                        tsl = slice(tb * 512, (tb + 1) * 512)
                        p = PS[pk % 4]
                        pk += 1
                        for kc in range(32):
                            mm(p[:], w_[:, kc, :], uT[:, kc, tsl], kc == 0, kc == 31, [p], [w_, uT])
                        r_ = rt[k % 2]
                        k += 1
                        ts("dve", r_[:], p[:], modv[:, 5, fc, g:g + 1], modv[:, 6, fc, g:g + 1], ALU.mult, ALU.add, [r_], [p, modv])
                        tt("pool", xT[g][:, fc, tsl], xT[g][:, fc, tsl], r_[:], ALU.add, [xT[g]], [xT[g], r_])
                S.barrier()

        for l in range(depth):
            if STOP < 1:
                break
            layer_vectors(l)
            if STOP < 2:
                break
            for g in ((1, 0) if KG != 0 else ()):
                with contextlib.ExitStack() as sg:
                    hT = sb([128, 8, 1024], BF16, sg, "hT")
                    norm_to_hT(g, hT, 0)
                    if KSUB <= 8 or 80 < KSUB < 90:
                        S.barrier(); break
                    inproj(l, g, hT)
                    S.barrier()
                    if KSUB < 14 or KG == 1:
                        break
            if KG == 0:
                with contextlib.ExitStack() as sg:
                    hT = sb([128, 8, 1024], BF16, sg, "hT")
                    norm_to_hT(0, hT, 0)
                    inproj(l, 0, hT)
                    S.barrier()
                break
            if STOP >= 3:
                da_prompt(l); na_prompt(l)
            if STOP >= 4:
                ml_prompt(l)
            if STOP >= 5:
                ml_sample(l)
            if STOP >= 6:
                da_sample(l); na_sample(l)
            if STOP < 7:
                break
            for g in (0, 1):
                merge_out(l, g)
                with contextlib.ExitStack() as sg:
                    hT = sb([128, 8, 1024], BF16, sg, "hT")
                    norm_to_hT(g, hT, 1)
                    mlp(l, g, hT)
                    S.barrier()

        with contextlib.ExitStack() as s2:
            sq = sb([128, 8, 512], BF16, s2, "sq")
            rs = sb([128, 512], F32, s2, "rs")
            yt = sb([128, 8, 512], F32, s2, "yt")
            yo = [sb([128, 1024], F32, s2, "yo") for _ in range(2)]
            k = 0
            for g in range(2):
                for tb in range(2):
                    tsl = slice(tb * 512, (tb + 1) * 512)
                    act(sq[:], xT[g][:, :, tsl], AF.Square, [sq], [xT[g]])
                    for c in range(8):
                        mm(PS[7][:], onesb[:], sq[:, c, :], c == 0, c == 7, [PS[7]], [onesb, sq])
                    rstd_from(PS[7][:], rs[:], 1024.0, [rs], [PS[7], epsb])
                    for c in range(8):
                        stt("dve", yt[:, c, :], xT[g][:, c, tsl], nrm[:, 2, c:c + 1], rs[:], ALU.mult, ALU.mult, [yt], [xT[g], nrm, rs])
                    for t4 in range(4):
                        o_ = yo[k % 2]
                        k += 1
                        for half in range(2):
                            p = PS[half + 2 * (k % 2)]
                            for c in range(4):
                                cc_ = half * 4 + c
                                op("pe", lambda e, p=p, c=c, cc_=cc_, t4=t4: e.transpose(p[:, c * 128:(c + 1) * 128], yt[:, cc_, t4 * 128:(t4 + 1) * 128], ident[:]), W=[p], R=[yt, ident])
                            cp("dve" if half == 0 else "act", o_[:, half * 512:(half + 1) * 512], p[:], [o_], [p])
                        row = tb * 512 + t4 * 128
                        dma("pool", y_out[g][row:row + 128, :], o_[:], W=[y_out[g]], R=[o_])
            S.barrier()
        S.finish()
        print("program built: n_inst=%d sems=%d" % (S.n_inst, len(S.sems)))
    return nc


def _rope_tables(qtr):
    pos = np.arange(qtr * 1024, (qtr + 1) * 1024)
    rows = (pos // 64).astype(np.float32)
    cols = (pos % 64).astype(np.float32)
    freqs = (10000.0 ** (-np.arange(0, 32, 2, dtype=np.float32) / np.float32(32))).astype(np.float32)
    cos = np.zeros((64, 1024), np.float32)
    sin = np.zeros((64, 1024), np.float32)
    for half, p in enumerate((rows, cols)):
        ang = (p[None, :] * freqs[:, None]).astype(np.float32)
        c, s = np.cos(ang).astype(np.float32), np.sin(ang).astype(np.float32)
        b = half * 32
        cos[b:b + 16] = c; cos[b + 16:b + 32] = c
        sin[b:b + 16] = -s; sin[b + 16:b + 32] = s
    return np.concatenate([cos, cos], 0), np.concatenate([sin, sin], 0)


def _consts(core):
    r = core % 4
    selv = np.zeros((16,), np.float32)
    for rr in range(4):
        selv[0 + rr] = 1.0 if rr == r - 1 else 0.0
        selv[4 + rr] = 1.0 if rr == r + 1 else 0.0
        selv[8 + rr] = 1.0 if rr < r else 0.0
        selv[12 + rr] = 1.0 if rr > r else 0.0
    selv = np.tile(selv[None, :], (128, 1))
    wm = np.full((128, 16, 512), NEG * 8, np.float32)
    for qb in range(2):
        for kt8 in range(8):
            ktile = 4 * qb + kt8
            for a in range(2):
                jp = 2 * ktile + a
                j = 16 * r - 4 + jp
                for c in range(8):
                    i = 8 * qb + c
                    R = 16 * r + i
                    ws = min(max(R - 4, 0), 56)
                    if ws <= j < ws + 8:
                        wm[a * 64:(a + 1) * 64, qb * 8 + kt8, c * 64:(c + 1) * 64] = 0.0
    cos, sin = _rope_tables(r)
    return selv, wm.reshape(128, 16 * 512), cos, sin


def _static_consts():
    kc = np.arange(64)[:, None]; qc = np.arange(64)[None, :]
    cs = np.clip(qc - 8, 0, 48)
    ok = (kc >= cs) & (kc < cs + 16)
    colmask = np.where(ok, 0.0, NEG * 8).astype(np.float32)
    colmask = np.concatenate([colmask, colmask], 0)
    selm = np.zeros((64, 8, 128), np.float32)
    for dh in range(8):
        selm[PID(dh), dh, :] = 1.0
    s = np.arange(128)[:, None]; t = np.arange(128)[None, :]
    mm = np.zeros((128, 2, 128), np.float32)
    mm[:, 0, :] = np.where(s <= t, 0.0, NEG)
    mm[:, 1, :] = np.where(s >= t, 0.0, NEG)
    return colmask, selm.reshape(64, 1024), mm.reshape(128, 256), np.eye(128, dtype=np.float32)


_CACHE = {}
RUN_DEPTH = DEPTH
import os
STOP = int(os.environ.get("KSTOP", "99"))
KSUB = int(os.environ.get("KSUB", "99"))
KG = int(os.environ.get("KG", "2"))
POOL_COMPUTE = bool(int(os.environ.get("KPOOL", "0")))
TINY = set()
if STOP <= 1:
    TINY = {"w_in", "w_up_da", "w_up_ml", "w_up_na", "w_out", "w_ff1", "w_ff2", "rpbT", "wm", "cda_k", "cda_v", "cna_k", "cna_v", "st_C"}


def kernel(**inp):
    f = lambda a: np.ascontiguousarray(np.asarray(a, dtype=np.float32))
    inp = {k: f(v) for k, v in inp.items()}
    LD = RUN_DEPTH
    if "nc" not in _CACHE:
        _CACHE["nc"] = build_program(depth=LD)
    nc = _CACHE["nc"]
    colmask, selm, mmask, ident = _static_consts()
    rpb = inp["na_rpb"]
    kc = np.arange(64)[:, None]; qc = np.arange(64)[None, :]
    dc = np.clip(kc - qc + 15, 0, 30)
    g_ = rpb[:, :, ::-1, :][:, :, :, dc]
    rpbT = np.ascontiguousarray(np.transpose(g_, (0, 3, 1, 2, 4))).reshape(DEPTH, 64, 8 * 15 * 64)
    shared = {
        "w_mod": inp["w_mod"][:LD], "b_mod": inp["b_mod"][:LD], "norm1": inp["norm1"][:LD], "w_in": inp["w_in"][:LD], "b_in": inp["b_in"][:LD],
        "da_lam": inp["da_lam"].reshape(DEPTH, 256)[:LD], "da_subln": inp["da_subln"][:LD], "ml_norm": inp["ml_norm"][:LD], "rpbT": rpbT[:LD],
        "w_up_da": inp["w_up_da"][:LD], "w_up_ml": inp["w_up_ml"][:LD], "w_up_na": inp["w_up_na"][:LD], "w_out": inp["w_out"][:LD],
        "norm2": inp["norm2"][:LD], "w_ff1": inp["w_ff1"][:LD], "b_ff1": inp["b_ff1"][:LD], "w_ff2": inp["w_ff2"][:LD], "b_ff2": inp["b_ff2"][:LD],
        "norm_f": inp["norm_f"], "ident": ident, "colmask": colmask, "selm": selm, "mmask": mmask,
    }
    in_maps = []
    for c in range(8):
        b = c // 4
        q = c % 4
        selv, wm, cos, sin = _consts(c)
        m = dict(shared)
        m.update({
            "xp": inp["x_prompt"][4 * c:4 * c + 4].reshape(1024, D),
            "xs": inp["x_sample"][b, q * 1024:(q + 1) * 1024],
            "cvec": np.stack([inp["c_ctx"], inp["c"][b]], 0),
            "cda_k": inp["cache_da_k"][b].reshape(DEPTH, 512, 512)[:LD], "cda_v": inp["cache_da_v"][b].reshape(DEPTH, 512, 512)[:LD],
            "cna_k": inp["cache_na_k"][b].reshape(DEPTH, 512, 512)[:LD], "cna_v": inp["cache_na_v"][b].reshape(DEPTH, 512, 512)[:LD],
            "st_C": inp["state_ml_C"][b].reshape(DEPTH, 8, 128, 128)[:LD], "st_n": inp["state_ml_n"][b].reshape(DEPTH, 8, 128)[:LD],
            "st_m": inp["state_ml_m"][b].reshape(DEPTH, 8)[:LD],
            "rope_cos": cos, "rope_sin": sin, "selv": selv, "wm": wm,
        })
        for k in TINY:
            m[k] = np.zeros((1, 1), np.float32)
        in_maps.append({k: np.ascontiguousarray(v) for k, v in m.items()})
    res = run_bass_kernel_spmd(nc, in_maps, core_ids=list(range(8)))
    R = res.results
    y_p = np.concatenate([R[c]["y_p"].reshape(4, 256, D) for c in range(8)], 0)
    y_s = np.stack([np.concatenate([R[b * 4 + q]["y_s"] for q in range(4)], 0) for b in range(2)], 0)

    def cat(name, shape):
        a = np.concatenate([R[c][name].reshape((4, LD) + shape) for c in range(8)], 0)
        if LD < DEPTH:
            a = np.concatenate([a, np.zeros((a.shape[0], DEPTH - LD) + shape, np.float32)], 1)
        return a
    da_k = cat("o_da_k", (256, 4, 128)); da_v = cat("o_da_v", (256, 4, 128))
    na_k = cat("o_na_k", (256, 8, 64)); na_v = cat("o_na_v", (256, 8, 64))
    ml_C = cat("o_ml_C", (2, 4, 128, 128)); ml_n = cat("o_ml_n", (2, 4, 128)); ml_m = cat("o_ml_m", (2, 4))
    return tuple(np.ascontiguousarray(a, dtype=np.float32) for a in (y_p, y_s, da_k, da_v, na_k, na_v, ml_C, ml_n, ml_m))
```
